# Optimizing a Trainium2 kernel written in Bass

```python
import math
import jax, jax.numpy as jnp
from jax import lax
import numpy as np

D_MODEL = 4096
BATCH = 2
SEQ = 4096
DEPTH = 2

FOX_HEAD_DIM = 128
FOX_WIDTH = 3 * D_MODEL // 8
FOX_HEADS = FOX_WIDTH // FOX_HEAD_DIM
FOX_BLOCK = 128
MLSTM_HEADS = 4
MLSTM_WIDTH = D_MODEL // 4
MLSTM_V_DIM = MLSTM_WIDTH // MLSTM_HEADS
MLSTM_QK_DIM = MLSTM_V_DIM // 2
MLSTM_QK_WIDTH = MLSTM_HEADS * MLSTM_QK_DIM
MLSTM_CHUNK = 64
MLSTM_CONV = 4
GATE_SOFTCAP = 15.0
RWKV_HEAD_DIM = 64
RWKV_WIDTH = D_MODEL - FOX_WIDTH - MLSTM_WIDTH
RWKV_HEADS = RWKV_WIDTH // RWKV_HEAD_DIM
DECAY_LORA = max(32, int(round(1.8 * D_MODEL ** 0.5 / 32)) * 32)
AAA_LORA = max(32, int(round(1.8 * D_MODEL ** 0.5 / 32)) * 32)
GATE_LORA = max(32, int(round(0.6 * D_MODEL ** 0.8 / 32)) * 32)
RWKV_LN_EPS = 64e-5
D_FF = ((8 * D_MODEL // 3 + 255) // 256) * 256
NORM_EPS = 1e-6

FOX_SIZES = (FOX_WIDTH, FOX_WIDTH, FOX_WIDTH, FOX_HEADS)
MLSTM_SIZES = (2 * MLSTM_QK_WIDTH, MLSTM_WIDTH, MLSTM_HEADS, MLSTM_HEADS, MLSTM_WIDTH)
RWKV_SIZES = (RWKV_WIDTH, RWKV_WIDTH, RWKV_WIDTH, DECAY_LORA, AAA_LORA, GATE_LORA)
FOX_IN = sum(FOX_SIZES)
MLSTM_IN = sum(MLSTM_SIZES)
RWKV_IN = sum(RWKV_SIZES)
GROUP_SIZES = (FOX_IN, MLSTM_IN, RWKV_IN)
N_IN = FOX_IN + MLSTM_IN + RWKV_IN

kernel_name = 'hymba_style_fox_mlstm_rwkv7_hybrid'


def _split(t, sizes):
    idx = np.cumsum(sizes)[:-1].tolist()
    return jnp.split(t, idx, axis=-1)


def _rms(t, g):
    tf = t.astype(jnp.float32)
    y = tf * lax.rsqrt(jnp.mean(tf * tf, axis=-1, keepdims=True) + NORM_EPS)
    return (y * g.astype(jnp.float32)).astype(t.dtype)


def _softcap(t):
    return GATE_SOFTCAP * jnp.tanh(t / GATE_SOFTCAP)


def _causal_conv(t, w):
    K, C = w.shape
    return lax.conv_general_dilated(t, w[:, None, :], window_strides=(1,), padding=[(K - 1, 0)],
                                    dimension_numbers=('NWC', 'WIO', 'NWC'), feature_group_count=C)


def _forgetting_attention(p, f_bias, out_g):
    B, T, _ = p.shape
    q, k, v, f = _split(p, FOX_SIZES)
    q = q.reshape(B, T, FOX_HEADS, FOX_HEAD_DIM)
    k = k.reshape(B, T, FOX_HEADS, FOX_HEAD_DIM)
    v = v.reshape(B, T, FOX_HEADS, FOX_HEAD_DIM)
    logf = jax.nn.log_sigmoid(f.astype(jnp.float32) + f_bias.astype(jnp.float32))
    cumf = jnp.cumsum(logf, axis=1).transpose(0, 2, 1)
    nb = T // FOX_BLOCK
    q_blocks = q.reshape(B, nb, FOX_BLOCK, FOX_HEADS, FOX_HEAD_DIM).transpose(1, 0, 3, 2, 4)
    f_blocks = cumf.reshape(B, FOX_HEADS, nb, FOX_BLOCK).transpose(2, 0, 1, 3)
    key_pos = jnp.arange(T)
    scale = FOX_HEAD_DIM ** -0.5

    def one_block(args):
        qb, fb, bi = args
        s = jnp.einsum('bhqd,bshd->bhqs', qb, k).astype(jnp.float32) * scale
        s = s + fb[..., :, None] - cumf[:, :, None, :]
        q_pos = bi * FOX_BLOCK + jnp.arange(FOX_BLOCK)
        s = jnp.where(key_pos[None, :] <= q_pos[:, None], s, -jnp.inf)
        w = jax.nn.softmax(s, axis=-1).astype(v.dtype)
        return jnp.einsum('bhqs,bshd->bqhd', w, v)

    o = lax.map(one_block, (q_blocks, f_blocks, jnp.arange(nb)))
    o = o.transpose(1, 0, 2, 3, 4).reshape(B, T, FOX_HEADS, FOX_HEAD_DIM)
    return _rms(o, out_g.reshape(FOX_HEADS, FOX_HEAD_DIM)).reshape(B, T, FOX_WIDTH)


def _mlstm(p, conv_w, conv_b, i_bias, f_bias, out_g):
    B, T, _ = p.shape
    f32 = jnp.float32
    qk, v, ig, fg, og = _split(p, MLSTM_SIZES)
    qk = jax.nn.silu(_causal_conv(qk, conv_w) + conv_b)
    q, k = jnp.split(qk, 2, axis=-1)
    nc = T // MLSTM_CHUNK
    L = MLSTM_CHUNK

    def chunks(t, d):
        return t.astype(f32).reshape(B, nc, L, MLSTM_HEADS, d).transpose(1, 0, 3, 2, 4)

    def gchunks(t):
        return t.reshape(B, nc, L, MLSTM_HEADS).transpose(1, 0, 3, 2)

    qc = chunks(q, MLSTM_QK_DIM) * (MLSTM_QK_DIM ** -0.5)
    kc = chunks(k, MLSTM_QK_DIM)
    vc = chunks(v, MLSTM_V_DIM)
    li = gchunks(_softcap(ig.astype(f32) + i_bias.astype(f32)))
    lf = gchunks(jax.nn.log_sigmoid(_softcap(fg.astype(f32) + f_bias.astype(f32))))
    causal = jnp.tril(jnp.ones((L, L), dtype=bool))

    def step(carry, xs):
        Cm, nm, m = carry
        qt, kt, vt, lit, lft = xs
        b = jnp.cumsum(lft, axis=-1)
        g = b[..., -1]
        a_inter = b + m[..., None]
        Dm = jnp.where(causal, b[..., :, None] - b[..., None, :] + lit[..., None, :], -jnp.inf)
        m_t = jnp.maximum(a_inter, jnp.max(Dm, axis=-1))
        w_inter = jnp.exp(a_inter - m_t)
        s = jnp.einsum('bhtd,bhsd->bhts', qt, kt) * jnp.exp(Dm - m_t[..., None])
        num = w_inter[..., None] * jnp.einsum('bhtd,bhde->bhte', qt, Cm) + jnp.einsum('bhts,bhse->bhte', s, vt)
        den = w_inter * jnp.einsum('bhtd,bhd->bht', qt, nm) + jnp.sum(s, axis=-1)
        h = num / jnp.maximum(jnp.abs(den), jnp.exp(-m_t))[..., None]
        upd = g[..., None] - b + lit
        m_new = jnp.maximum(g + m, jnp.max(upd, axis=-1))
        decay = jnp.exp(g + m - m_new)
        wk = jnp.exp(upd - m_new[..., None])
        C_new = decay[..., None, None] * Cm + jnp.einsum('bhs,bhsd,bhse->bhde', wk, kt, vt)
        n_new = decay[..., None] * nm + jnp.einsum('bhs,bhsd->bhd', wk, kt)
        return (C_new, n_new, m_new), h

    init = (jnp.zeros((B, MLSTM_HEADS, MLSTM_QK_DIM, MLSTM_V_DIM), f32),
            jnp.zeros((B, MLSTM_HEADS, MLSTM_QK_DIM), f32),
            jnp.zeros((B, MLSTM_HEADS), f32))
    _, h = lax.scan(step, init, (qc, kc, vc, li, lf))
    h = h.transpose(1, 0, 3, 2, 4).reshape(B, T, MLSTM_HEADS, MLSTM_V_DIM)
    h = _rms(h, out_g.reshape(MLSTM_HEADS, MLSTM_V_DIM)).reshape(B, T, MLSTM_WIDTH).astype(p.dtype)
    return h * jax.nn.sigmoid(og)


def _rwkv7(p, mu, w0, w_up, a0, a_up, g_up, k_k, k_a, r_k, ln_w, ln_b):
    B, T, _ = p.shape
    f32 = jnp.float32
    H, N = RWKV_HEADS, RWKV_HEAD_DIM
    p_prev = jnp.pad(p, ((0, 0), (1, 0), (0, 0)))[:, :-1]
    p = p + (p_prev - p) * mu
    r, k, v, wl, al, gl = _split(p, RWKV_SIZES)
    w = -jax.nn.softplus(-(w0 + jnp.tanh(wl) @ w_up).astype(f32)) - 0.5
    decay = jnp.exp(-jnp.exp(w))
    a = jax.nn.sigmoid((a0 + al @ a_up).astype(f32))
    g = jax.nn.sigmoid(gl) @ g_up

    def heads(t):
        return t.astype(f32).reshape(B, T, H, N)

    r, k, v, a, decay = heads(r), heads(k), heads(v), heads(a), heads(decay)
    kk = k * k_k.astype(f32).reshape(H, N)
    kk = kk / jnp.maximum(jnp.sqrt(jnp.sum(kk * kk, axis=-1, keepdims=True)), 1e-12)
    k = k * (1.0 + (a - 1.0) * k_a.astype(f32).reshape(H, N))

    def step(S, xs):
        r_t, w_t, k_t, v_t, kk_t, a_t = xs
        sk = jnp.einsum('bhij,bhj->bhi', S, kk_t)
        S = S * w_t[:, :, None, :] - sk[..., None] * (kk_t * a_t)[:, :, None, :] + v_t[..., None] * k_t[:, :, None, :]
        return S, jnp.einsum('bhij,bhj->bhi', S, r_t)

    def seq_first(t):
        return t.transpose(1, 0, 2, 3)

    S0 = jnp.zeros((B, H, N, N), f32)
    _, y = lax.scan(step, S0, (seq_first(r), seq_first(decay), seq_first(k), seq_first(v), seq_first(kk), seq_first(a)))
    y = y.transpose(1, 0, 2, 3)
    mean = jnp.mean(y, axis=-1, keepdims=True)
    var = jnp.mean(jnp.square(y - mean), axis=-1, keepdims=True)
    y = (y - mean) * lax.rsqrt(var + RWKV_LN_EPS)
    y = y * ln_w.astype(f32).reshape(H, N) + ln_b.astype(f32).reshape(H, N)
    y = y + jnp.sum(r * k * r_k.astype(f32), axis=-1, keepdims=True) * v
    return y.reshape(B, T, RWKV_WIDTH).astype(p.dtype) * g


def setup_inputs(seed: int = 0) -> dict:
    key = jax.random.key(seed)
    ks = iter(jax.random.split(key, 40))

    def nrm(shape, scale=1.0):
        return jax.random.normal(next(ks), shape, jnp.float32) * scale

    def gain(shape):
        return 1.0 + 0.05 * nrm(shape)

    L, D = DEPTH, D_MODEL
    return {
        'x': nrm((BATCH, SEQ, D)),
        'c': nrm((BATCH, D)),
        'ada_w': nrm((L, D, 6 * D), D ** -0.5),
        'ada_b': nrm((L, 6 * D), 0.02),
        'norm1': gain((L, D)),
        'w_in': nrm((L, D, N_IN), D ** -0.5),
        'fox_f_bias': jnp.linspace(1.0, 6.0, FOX_HEADS)[None, :] + 0.1 * nrm((L, FOX_HEADS)),
        'fox_norm': gain((L, FOX_WIDTH)),
        'ml_conv_w': nrm((L, MLSTM_CONV, 2 * MLSTM_QK_WIDTH), MLSTM_CONV ** -0.5),
        'ml_conv_b': nrm((L, 2 * MLSTM_QK_WIDTH), 0.02),
        'ml_i_bias': -2.0 + 0.1 * nrm((L, MLSTM_HEADS)),
        'ml_f_bias': jnp.linspace(3.0, 6.0, MLSTM_HEADS)[None, :] + 0.1 * nrm((L, MLSTM_HEADS)),
        'ml_norm': gain((L, MLSTM_WIDTH)),
        'rw_mu': jax.random.uniform(next(ks), (L, RWKV_IN), jnp.float32),
        'rw_w0': -3.0 + 0.5 * nrm((L, RWKV_WIDTH)),
        'rw_w_up': nrm((L, DECAY_LORA, RWKV_WIDTH), DECAY_LORA ** -0.5),
        'rw_a0': 0.1 * nrm((L, RWKV_WIDTH)),
        'rw_a_up': nrm((L, AAA_LORA, RWKV_WIDTH), AAA_LORA ** -0.5),
        'rw_g_up': nrm((L, GATE_LORA, RWKV_WIDTH), GATE_LORA ** -0.5),
        'rw_k_k': 1.0 + 0.1 * nrm((L, RWKV_WIDTH)),
        'rw_k_a': 1.0 + 0.1 * nrm((L, RWKV_WIDTH)),
        'rw_r_k': 0.1 * nrm((L, RWKV_HEADS, RWKV_HEAD_DIM)),
        'rw_ln_w': gain((L, RWKV_WIDTH)),
        'rw_ln_b': nrm((L, RWKV_WIDTH), 0.02),
        'w_out': nrm((L, D, D), D ** -0.5),
        'norm2': gain((L, D)),
        'ffn_gate': nrm((L, D, D_FF), D ** -0.5),
        'ffn_up': nrm((L, D, D_FF), D ** -0.5),
        'ffn_down': nrm((L, D_FF, D), D_FF ** -0.5),
        'final_norm': gain((D,)),
    }


def reference(x, c, ada_w, ada_b, norm1, w_in, fox_f_bias, fox_norm, ml_conv_w, ml_conv_b, ml_i_bias,
              ml_f_bias, ml_norm, rw_mu, rw_w0, rw_w_up, rw_a0, rw_a_up, rw_g_up, rw_k_k, rw_k_a, rw_r_k,
              rw_ln_w, rw_ln_b, w_out, norm2, ffn_gate, ffn_up, ffn_down, final_norm):
    for l in range(DEPTH):
        mod = jax.nn.silu(c) @ ada_w[l] + ada_b[l]
        sh1, sc1, g1, sh2, sc2, g2 = jnp.split(mod[:, None, :], 6, axis=-1)
        h = _rms(x, norm1[l]) * (1.0 + sc1) + sh1
        p = h @ w_in[l]
        p_fox, p_ml, p_rw = _split(p, GROUP_SIZES)
        y_fox = _forgetting_attention(p_fox, fox_f_bias[l], fox_norm[l])
        y_ml = _mlstm(p_ml, ml_conv_w[l], ml_conv_b[l], ml_i_bias[l], ml_f_bias[l], ml_norm[l])
        y_rw = _rwkv7(p_rw, rw_mu[l], rw_w0[l], rw_w_up[l], rw_a0[l], rw_a_up[l], rw_g_up[l],
                      rw_k_k[l], rw_k_a[l], rw_r_k[l], rw_ln_w[l], rw_ln_b[l])
        mix = jnp.concatenate([y_fox, y_ml, y_rw], axis=-1) @ w_out[l]
        x = x + g1 * mix
        h = _rms(x, norm2[l]) * (1.0 + sc2) + sh2
        x = x + g2 * ((jax.nn.silu(h @ ffn_gate[l]) * (h @ ffn_up[l])) @ ffn_down[l])
    return _rms(x, final_norm)
```

```python
import math
from contextlib import ExitStack
import numpy as np
import ml_dtypes
import concourse.bass as bass
import concourse.mybir as mybir
from concourse.bass_utils import run_bass_kernel_spmd

F32 = mybir.dt.float32
BF16 = mybir.dt.bfloat16
AF = mybir.ActivationFunctionType
ALU = mybir.AluOpType
AX = mybir.AxisListType
NPBF16 = ml_dtypes.bfloat16

D_MODEL = 4096
SEQ = 4096
BATCH = 2
DEPTH = 2
D_FF = 11008
NORM_EPS = 1e-6
RWKV_LN_EPS = 64e-5


class Buf:
    __slots__ = ("name", "writer", "readers", "dsem", "dcount")

    def __init__(self, name):
        self.name = name
        self.writer = None
        self.readers = {}
        self.dsem = None
        self.dcount = 0


class Sched:
    def __init__(self, nc, es):
        self.nc = nc
        self.es = es
        self.engs = {"pe": nc.tensor, "dve": nc.vector, "act": nc.scalar, "pool": nc.gpsimd, "sp": nc.sync}
        self.sem = {}
        self.cnt = {}
        for k in ("pe", "dve", "act", "pool"):
            self.sem[k] = es.enter_context(nc.semaphore("prog_" + k))
            self.cnt[k] = 0
        self.waited = {}
        self.nsem = 4
        self.out_events = []
        self.n_ins = 0
        self.dma_events = {}

    def _wait(self, eng, evs, skip_sem=None):
        need = {}
        for ev in evs:
            if ev is None:
                continue
            sem, val = ev
            if skip_sem is not None and sem is skip_sem:
                continue
            if need.get(sem, (None, 0))[1] < val:
                need[sem] = (sem, val)
        E = self.engs[eng]
        for sem, val in need.values():
            key = (eng, sem)
            if self.waited.get(key, 0) >= val:
                continue
            E.wait_ge(sem, val)
            self.waited[key] = val
            self.n_ins += 1

    def _deps(self, reads, writes):
        evs = []
        for b in reads:
            evs.append(b.writer)
        for b in writes:
            evs.append(b.writer)
            for s, v in b.readers.items():
                evs.append((s, v))
        return evs

    def _record(self, ev, reads, writes):
        for b in writes:
            b.writer = ev
            b.readers = {}
        for b in reads:
            if b.readers.get(ev[0], 0) < ev[1]:
                b.readers[ev[0]] = ev[1]

    def op(self, eng, fn, reads=(), writes=(), pe_chain=False):
        evs = self._deps(reads, writes)
        self._wait(eng, evs, skip_sem=self.sem[eng] if (pe_chain and eng == "pe") else None)
        ins = fn(self.engs[eng])
        self.cnt[eng] += 1
        ins.then_inc(self.sem[eng], 1)
        ev = (self.sem[eng], self.cnt[eng])
        self._record(ev, reads, writes)
        self.n_ins += 1
        return ev

    def dma(self, q, out_ap, in_ap, reads=(), writes=(), owner=None, is_output=False, **kw):
        if owner is None:
            owner = writes[0]
        evs = self._deps(reads, writes)
        if owner.dsem is not None and owner.dcount > 0:
            evs.append((owner.dsem, owner.dcount))
        self._wait(q, evs)
        if owner.dsem is None:
            owner.dsem = self.es.enter_context(self.nc.semaphore("d_%s_%d" % (owner.name, self.nsem)))
            self.nsem += 1
        ins = self.engs[q].dma_start(out=out_ap, in_=in_ap, **kw)
        owner.dcount += 16
        ins.then_inc(owner.dsem, 16)
        ev = (owner.dsem, owner.dcount)
        self.dma_events[owner.dsem] = ev
        self._record(ev, reads, writes)
        if is_output:
            self.out_events.append(ev)
        self.n_ins += 1
        return ev

    def barrier(self, bufs=()):
        evs = [(self.sem[k], self.cnt[k]) for k in self.sem if self.cnt[k] > 0]
        evs += list(self.dma_events.values())
        for e in ("pe", "dve", "act", "pool", "sp"):
            self._wait(e, evs)

    def finish(self):
        evs = list(self.out_events) + [(self.sem[k], self.cnt[k]) for k in self.sem if self.cnt[k] > 0]
        self._wait("sp", evs)


class _Stage:
    def __init__(self, C):
        self.C = C

    def __enter__(self):
        self.prev = self.C.stack
        self.C.stack = ExitStack()
        return self

    def __exit__(self, *a):
        self.C.S.barrier()
        self.C.stack.close()
        self.C.stack = self.prev
        return False


def ktiles(K):
    return [(k0, min(128, K - k0)) for k0 in range(0, K, 128)]


class Ctx:
    def __init__(self):
        self.nc = bass.Bass("TRN2", target_bir_lowering=False)
        self.es = ExitStack()
        self.S = Sched(self.nc, self.es)
        self.uid = 0
        self.stack = self.es
        eps_tile(self)

    def sb(self, name, shape, dt):
        self.uid += 1
        return self.stack.enter_context(self.nc.sbuf_tensor("%s_%d" % (name, self.uid), list(shape), dt))

    def ps(self, name, shape, dt=F32):
        self.uid += 1
        return self.stack.enter_context(self.nc.psum_tensor("%s_%d" % (name, self.uid), list(shape), dt))

    def stage(self):
        return _Stage(self)

    def dram(self, name, shape, dt, kind="Internal"):
        return self.nc.dram_tensor(name, list(shape), dt, kind=kind).ap()

    def close(self):
        self.S.finish()
        self.es.close()


def gemm_fm(C, xT, xT_buf, w_list, K, N, T, epi, TB=1024, NCH=512, tag="g"):
    S = C.S
    kts = ktiles(K)
    KT = len(kts)
    TB = min(TB, T)
    nW = len(w_list)
    xt = C.sb(tag + "_x", [128, KT, TB], BF16)
    xt_b = Buf(tag + "_x")
    wts = [[C.sb(tag + "_w%d_%d" % (wi, i), [128, KT, NCH], BF16) for i in range(2)] for wi in range(nW)]
    wt_b = [[Buf(tag + "_w%d_%d" % (wi, i)) for i in range(2)] for wi in range(nW)]
    NPS = 2 if nW > 1 else 4
    pss = [[C.ps(tag + "_ps%d_%d" % (wi, i), [128, 512]) for i in range(NPS)] for wi in range(nW)]
    ps_b = [[Buf(tag + "_ps%d_%d" % (wi, i)) for i in range(NPS)] for wi in range(nW)]
    full_k = (K % 128 == 0)
    it = 0
    pi = 0
    for t0 in range(0, T, TB):
        tsz_b = min(TB, T - t0)
        if full_k:
            S.dma("sp", xt[:, :, 0:tsz_b], xT[:, t0:t0 + tsz_b].rearrange("(kt p) t -> p kt t", p=128),
                  reads=[xT_buf], writes=[xt_b])
        else:
            for ki, (k0, ksz) in enumerate(kts):
                S.dma("sp", xt[0:ksz, ki, 0:tsz_b], xT[k0:k0 + ksz, t0:t0 + tsz_b], reads=[xT_buf], writes=[xt_b])
        for n0 in range(0, N, NCH):
            nsz_c = min(NCH, N - n0)
            slot = it % 2
            it += 1
            for wi, (w, w_buf) in enumerate(w_list):
                if full_k:
                    S.dma("pool", wts[wi][slot][:, :, 0:nsz_c],
                          w[:, n0:n0 + nsz_c].rearrange("(kt p) n -> p kt n", p=128),
                          reads=[w_buf], writes=[wt_b[wi][slot]])
                else:
                    for ki, (k0, ksz) in enumerate(kts):
                        S.dma("pool", wts[wi][slot][0:ksz, ki, 0:nsz_c], w[k0:k0 + ksz, n0:n0 + nsz_c],
                              reads=[w_buf], writes=[wt_b[wi][slot]])
            for m0 in range(0, nsz_c, 128):
                msz = min(128, nsz_c - m0)
                for tt in range(0, tsz_b, 512):
                    tsz = min(512, tsz_b - tt)
                    p = pi % NPS
                    pi += 1
                    for wi in range(nW):
                        for ki, (k0, ksz) in enumerate(kts):
                            S.op("pe", lambda E, wi=wi, ki=ki, ksz=ksz: E.matmul(
                                pss[wi][p][0:msz, 0:tsz], lhsT=wts[wi][slot][0:ksz, ki, m0:m0 + msz],
                                rhs=xt[0:ksz, ki, tt:tt + tsz], start=(ki == 0), stop=(ki == KT - 1)),
                                reads=[wt_b[wi][slot], xt_b], writes=[ps_b[wi][p]], pe_chain=(ki > 0))
                    epi(n0 + m0, msz, t0 + tt, tsz, [pss[wi][p][0:msz, 0:tsz] for wi in range(nW)],
                        [ps_b[wi][p] for wi in range(nW)])


def norm_stage(C, xT, xT_buf, g_ap, sc_ap, sh_ap, out_ap, out_buf, D, T, out_dt, is_output, tag="n"):
    S = C.S
    KT = D // 128
    gt = C.sb(tag + "_g", [128, KT], F32)
    A = C.sb(tag + "_A", [128, KT], F32)
    gb = Buf(tag + "_g")
    Ab = Buf(tag + "_A")
    S.dma("sp", gt[:], g_ap, writes=[gb])
    if sc_ap is not None:
        sct = C.sb(tag + "_sc", [128, KT], F32)
        sht = C.sb(tag + "_sh", [128, KT], F32)
        scb = Buf(tag + "_sc")
        shb = Buf(tag + "_sh")
        S.dma("sp", sct[:], sc_ap, writes=[scb])
        S.dma("sp", sht[:], sh_ap, writes=[shb])
        S.op("dve", lambda E: E.scalar_tensor_tensor(out=A[:], in0=sct[:], scalar=1.0, in1=gt[:], op0=ALU.add, op1=ALU.mult),
             reads=[scb, gb], writes=[Ab])
    else:
        S.op("dve", lambda E: E.tensor_copy(out=A[:], in_=gt[:]), reads=[gb], writes=[Ab])
    ones = C.sb(tag + "_ones", [128, 128], F32)
    onb = Buf(tag + "_ones")
    S.op("pool", lambda E: E.memset(ones[:], 1.0), writes=[onb])
    TBN = 256
    xs = [C.sb(tag + "_xs%d" % i, [128, KT, TBN], F32) for i in range(2)]
    xb = [Buf(tag + "_xs%d" % i) for i in range(2)]
    sq = [C.sb(tag + "_sq%d" % i, [128, TBN], F32) for i in range(2)]
    sqb = [Buf(tag + "_sq%d" % i) for i in range(2)]
    ho = [C.sb(tag + "_ho%d" % i, [128, KT, TBN], out_dt) for i in range(2)]
    hob = [Buf(tag + "_ho%d" % i) for i in range(2)]
    pss = C.ps(tag + "_ps", [128, TBN])
    psb = Buf(tag + "_ps")
    rs = C.sb(tag + "_rs", [128, TBN], F32)
    rsb = Buf(tag + "_rs")
    tmp = C.sb(tag + "_tmp", [128, TBN], F32)
    tmpb = Buf(tag + "_tmp")
    for bi, t0 in enumerate(range(0, T, TBN)):
        tsz = min(TBN, T - t0)
        s = bi % 2
        S.dma("sp", xs[s][:, :, 0:tsz], xT[:, t0:t0 + tsz].rearrange("(kt p) t -> p kt t", p=128),
              reads=[xT_buf], writes=[xb[s]])
        for kt in range(KT):
            q = kt % 2
            S.op("act", lambda E, kt=kt, q=q: E.activation(out=sq[q][:, 0:tsz], in_=xs[s][:, kt, 0:tsz], func=AF.Square),
                 reads=[xb[s]], writes=[sqb[q]])
            S.op("pe", lambda E, kt=kt, q=q: E.matmul(pss[:, 0:tsz], lhsT=ones[:], rhs=sq[q][:, 0:tsz],
                                                      start=(kt == 0), stop=(kt == KT - 1)),
                 reads=[onb, sqb[q]], writes=[psb], pe_chain=(kt > 0))
        S.op("act", lambda E: E.activation(out=rs[:, 0:tsz], in_=pss[:, 0:tsz], func=AF.Sqrt, scale=1.0 / D, bias=eps_tile(C)[:, 0:1]),
             reads=[psb, eps_buf(C)], writes=[rsb])
        S.op("dve", lambda E: E.reciprocal(out=rs[:, 0:tsz], in_=rs[:, 0:tsz]), reads=[rsb], writes=[rsb])
        for kt in range(KT):
            S.op("dve", lambda E, kt=kt: E.tensor_tensor(out=tmp[:, 0:tsz], in0=xs[s][:, kt, 0:tsz], in1=rs[:, 0:tsz], op=ALU.mult),
                 reads=[xb[s], rsb], writes=[tmpb])
            if sc_ap is not None:
                S.op("act", lambda E, kt=kt: E.activation(out=ho[s][:, kt, 0:tsz], in_=tmp[:, 0:tsz], func=AF.Identity,
                                                          scale=A[:, kt:kt + 1], bias=sht[:, kt:kt + 1]),
                     reads=[tmpb, Ab, shb], writes=[hob[s]])
            else:
                S.op("act", lambda E, kt=kt: E.activation(out=ho[s][:, kt, 0:tsz], in_=tmp[:, 0:tsz], func=AF.Identity,
                                                          scale=A[:, kt:kt + 1]),
                     reads=[tmpb, Ab], writes=[hob[s]])
        S.dma("sp", out_ap[:, t0:t0 + tsz].rearrange("(kt p) t -> p kt t", p=128), ho[s][:, :, 0:tsz],
              reads=[hob[s]], writes=[out_buf], owner=hob[s], is_output=is_output)


def eps_tile(C):
    if not hasattr(C, "_eps"):
        C._eps = C.es.enter_context(C.nc.sbuf_tensor("eps_const", [128, 1], F32))
        C._epsb = Buf("eps")
        C.S.op("pool", lambda E: E.memset(C._eps[:], NORM_EPS), writes=[C._epsb])
    return C._eps


def eps_buf(C):
    eps_tile(C)
    return C._epsb


def make_ident(C, tag):
    S = C.S
    ident = C.sb(tag + "_id", [128, 128], F32)
    idb = Buf(tag + "_id")
    S.op("pool", lambda E: E.memset(ident[:], 1.0), writes=[idb])
    S.op("pool", lambda E: E.affine_select(out=ident[:], in_=ident[:], pattern=[[1, 128]], compare_op=ALU.is_equal,
                                           fill=0.0, base=0, channel_multiplier=-1), reads=[idb], writes=[idb])
    return ident, idb


def fox_stage(C, pT, pTb, r_q, r_k, r_v, r_f, nh, fbias, gain_rep, y, yb, ycol0, T):
    S = C.S
    scale = 128 ** -0.5
    NQ = (T + 511) // 512
    NK = T // 128
    ident, idb = make_ident(C, "fx")
    ones_row = C.sb("fx_ones", [1, max(T, 128)], F32)
    onb = Buf("fx_ones")
    S.op("pool", lambda E: E.memset(ones_row[:], 1.0), writes=[onb])
    negone = C.sb("fx_neg1", [1, 1], F32)
    n1b = Buf("fx_neg1")
    S.op("pool", lambda E: E.memset(negone[:], -1.0), writes=[n1b])
    fb = C.sb("fx_fb", [1, nh], F32)
    fbb = Buf("fx_fb")
    S.dma("sp", fb[:], fbias, writes=[fbb])
    S.op("dve", lambda E: E.tensor_scalar(out=fb[:], in0=fb[:], scalar1=-1.0, scalar2=None, op0=ALU.mult), reads=[fbb], writes=[fbb])
    gain = C.sb("fx_gain", [128, nh * 128], F32)
    gnb = Buf("fx_gain")
    S.dma("sp", gain[:], gain_rep, writes=[gnb])
    epsb = eps_buf(C)
    eps = eps_tile(C)
    QT = C.sb("fx_q", [128, T], BF16)
    KTt = C.sb("fx_k", [128, T], BF16)
    VT = C.sb("fx_v", [128, T], F32)
    Vext = C.sb("fx_ve", [128, NK, 130], BF16)
    frow = C.sb("fx_f", [1, T], F32)
    Frow = C.sb("fx_F", [1, T], F32)
    Fbc = C.sb("fx_Fbc", [128, T], F32)
    negF = C.sb("fx_nF", [128, NK], F32)
    QTb, KTb, VTb, Vxb, frb, Frb, Fbb, nFb = [Buf("fx_b%d" % i) for i in range(8)]
    E1 = [C.sb("fx_e%d" % i, [128, 512], F32) for i in range(2)]
    E1b = [Buf("fx_e%d" % i) for i in range(2)]
    PT = [C.sb("fx_p%d" % i, [128, 512], BF16) for i in range(2)]
    PTb = [Buf("fx_p%d" % i) for i in range(2)]
    ps_s = [C.ps("fx_ps%d" % i, [128, 512]) for i in range(2)]
    ps_sb = [Buf("fx_ps%d" % i) for i in range(2)]
    ps_o = [C.ps("fx_po%d" % i, [128, 512]) for i in range(4)]
    ps_ob = [Buf("fx_po%d" % i) for i in range(4)]
    ps_m = C.ps("fx_pm", [128, 512])
    psmb = Buf("fx_pm")
    yst = [C.sb("fx_ys%d" % i, [128, 128], F32) for i in range(2)]
    ystb = [Buf("fx_ys%d" % i) for i in range(2)]
    junk = C.sb("fx_junk", [128, 128], F32)
    junkb = Buf("fx_junk")
    sml = [C.sb("fx_sm%d" % i, [128, 4], F32) for i in range(2)]
    smlb = [Buf("fx_sm%d" % i) for i in range(2)]
    yo = [C.sb("fx_yo%d" % i, [128, 128], BF16) for i in range(2)]
    yob = [Buf("fx_yo%d" % i) for i in range(2)]
    blk = 0
    fin = 0
    neg_fill = C.nc.gpsimd.to_reg(-30000.0)
    for h in range(nh):
        S.dma("pool", QT[:], pT[r_q + h * 128:r_q + (h + 1) * 128, :], reads=[pTb], writes=[QTb])
        S.dma("pool", KTt[:], pT[r_k + h * 128:r_k + (h + 1) * 128, :], reads=[pTb], writes=[KTb])
        S.dma("sp", VT[:], pT[r_v + h * 128:r_v + (h + 1) * 128, :], reads=[pTb], writes=[VTb])
        S.dma("sp", frow[:], pT[r_f + h:r_f + h + 1, :], reads=[pTb], writes=[frb])
        for kt in range(NK):
            S.op("pe", lambda E, kt=kt: E.transpose(ps_m[:, 0:128], VT[:, kt * 128:(kt + 1) * 128], ident[:]),
                 reads=[VTb, idb], writes=[psmb])
            S.op("act", lambda E, kt=kt: E.copy(out=Vext[:, kt, 0:128], in_=ps_m[:, 0:128]), reads=[psmb], writes=[Vxb])
        S.op("pool", lambda E: E.memset(Vext[:, :, 128:129], 1.0), writes=[Vxb])
        S.op("act", lambda E, h=h: E.activation(out=frow[:], in_=frow[:], func=AF.Exp, scale=-1.0, bias=fb[0:1, h:h + 1]),
             reads=[frb, fbb], writes=[frb])
        S.op("act", lambda E: E.activation(out=frow[:], in_=frow[:], func=AF.Ln, scale=1.0, bias=ones_row[0:1, 0:1]),
             reads=[frb, onb], writes=[frb])
        S.op("dve", lambda E: E.tensor_tensor_scan(out=Frow[:], data0=ones_row[0:1, 0:T], data1=frow[:], initial=0.0,
                                                  op0=ALU.mult, op1=ALU.subtract), reads=[frb, onb], writes=[Frb])
        for c in range(NQ):
            csz = min(512, T - c * 512)
            S.op("pe", lambda E, c=c, csz=csz: E.matmul(ps_m[:, 0:csz], lhsT=ones_row[0:1, 0:128], rhs=Frow[0:1, c * 512:c * 512 + csz],
                                                        start=True, stop=True), reads=[onb, Frb], writes=[psmb])
            S.op("act", lambda E, c=c, csz=csz: E.copy(out=Fbc[:, c * 512:c * 512 + csz], in_=ps_m[:, 0:csz]), reads=[psmb], writes=[Fbb])
        for kt in range(NK):
            S.op("pe", lambda E, kt=kt: E.matmul(ps_m[:, kt:kt + 1], lhsT=Frow[0:1, kt * 128:(kt + 1) * 128], rhs=negone[0:1, 0:1],
                                                 start=True, stop=True), reads=[Frb, n1b], writes=[psmb], pe_chain=(kt > 0))
        S.op("dve", lambda E: E.tensor_copy(out=negF[:, 0:NK], in_=ps_m[:, 0:NK]), reads=[psmb], writes=[nFb])
        for qt in range(NQ):
            q0 = qt * 512
            qsz = min(512, T - q0)
            nsub = qsz // 128
            nkt = (q0 + qsz) // 128
            for kt in range(nkt):
                c = kt - 4 * qt
                qs = 128 * c if c >= 0 else 0
                w = qsz - qs
                s = blk % 2
                blk += 1
                S.op("pe", lambda E, kt=kt, qs=qs, w=w, s=s: E.matmul(ps_s[s][:, 0:w], lhsT=KTt[:, kt * 128:(kt + 1) * 128],
                                                                     rhs=QT[:, q0 + qs:q0 + qs + w], start=True, stop=True),
                     reads=[KTb, QTb], writes=[ps_sb[s]])
                S.op("dve", lambda E, qs=qs, w=w, s=s: E.scalar_tensor_tensor(out=E1[s][:, 0:w], in0=ps_s[s][:, 0:w], scalar=scale,
                                                                            in1=Fbc[:, q0 + qs:q0 + qs + w], op0=ALU.mult, op1=ALU.add),
                     reads=[ps_sb[s], Fbb], writes=[E1b[s]])
                if c >= 0:
                    S.op("pool", lambda E, s=s: E.affine_select(out=E1[s][:, 0:128], in_=E1[s][:, 0:128], pattern=[[1, 128]],
                                                               compare_op=ALU.is_ge, fill=neg_fill, base=0, channel_multiplier=-1),
                         reads=[E1b[s]], writes=[E1b[s]])
                S.op("act", lambda E, kt=kt, w=w, s=s: E.activation(out=PT[s][:, 0:w], in_=E1[s][:, 0:w], func=AF.Exp,
                                                                  bias=negF[:, kt:kt + 1], scale=1.0),
                     reads=[E1b[s], nFb], writes=[PTb[s]])
                for qi in range(qs // 128, nsub):
                    last = (kt == 4 * qt + qi)
                    off = qi * 128 - qs
                    S.op("pe", lambda E, kt=kt, qi=qi, off=off, s=s, last=last: E.matmul(
                        ps_o[qi][:, 0:129], lhsT=PT[s][:, off:off + 128], rhs=Vext[:, kt, 0:129], start=(kt == 0), stop=last),
                        reads=[PTb[s], Vxb], writes=[ps_ob[qi]], pe_chain=(kt > 0))
                    if last:
                        f = fin % 2
                        fin += 1
                        sm = sml[f]
                        S.op("dve", lambda E, qi=qi, sm=sm: E.reciprocal(out=sm[:, 0:1], in_=ps_o[qi][:, 128:129]),
                             reads=[ps_ob[qi]], writes=[smlb[f]])
                        S.op("dve", lambda E, qi=qi, sm=sm, f=f: E.tensor_scalar(out=yst[f][:], in0=ps_o[qi][:, 0:128], scalar1=sm[:, 0:1],
                                                                                 scalar2=None, op0=ALU.mult),
                             reads=[ps_ob[qi], smlb[f]], writes=[ystb[f]])
                        S.op("act", lambda E, f=f: E.activation(out=junk[:], in_=yst[f][:], func=AF.Square), reads=[ystb[f]], writes=[junkb])
                        S.op("dve", lambda E, sm=sm: E.reduce_sum(out=sm[:, 1:2], in_=junk[:], axis=AX.X), reads=[junkb], writes=[smlb[f]])
                        S.op("act", lambda E, sm=sm: E.activation(out=sm[:, 2:3], in_=sm[:, 1:2], func=AF.Sqrt, scale=1.0 / 128, bias=eps[:, 0:1]),
                             reads=[smlb[f], epsb], writes=[smlb[f]])
                        S.op("dve", lambda E, sm=sm: E.reciprocal(out=sm[:, 3:4], in_=sm[:, 2:3]), reads=[smlb[f]], writes=[smlb[f]])
                        S.op("dve", lambda E, sm=sm, f=f, h=h: E.scalar_tensor_tensor(out=yo[f][:], in0=yst[f][:], scalar=sm[:, 3:4],
                                                                                     in1=gain[:, h * 128:(h + 1) * 128], op0=ALU.mult, op1=ALU.mult),
                             reads=[ystb[f], smlb[f], gnb], writes=[yob[f]])
                        tok0 = q0 + qi * 128
                        S.dma("sp", y[tok0:tok0 + 128, ycol0 + h * 128:ycol0 + (h + 1) * 128], yo[f][:], reads=[yob[f]], writes=[yb],
                              owner=yob[f], is_output=True)


def mlstm_stage(C, pT, pTb, r_q, r_k, r_v, r_i, r_f, r_o, cw, cb, gbias, gain_rep, y, yb, ycol0, T):
    TSEG = min(1024, T)
    state = C.sb("ml_state", [128, 257], F32)
    stb = Buf("ml_state")
    C.S.op("pool", lambda E: E.memset(state[:], 0.0), writes=[stb])
    for t0 in range(0, T, TSEG):
        with C.stage():
            _mlstm_seg(C, pT, pTb, r_q, r_k, r_v, r_i, r_f, r_o, cw, cb, gbias, gain_rep, y, yb, ycol0, min(TSEG, T - t0), t0, state, stb)


def _mlstm_seg(C, pT, pTb, r_q, r_k, r_v, r_i, r_f, r_o, cw, cb, gbias, gain_rep, y, yb, ycol0, T, t0, state, stb):
    S = C.S
    L = 64
    NC = T // L
    DK, DV = 128, 256
    ident, idb = make_ident(C, "ml")
    ones_row = C.sb("ml_ones", [1, max(T, 128)], F32)
    onb = Buf("ml_ones")
    S.op("pool", lambda E: E.memset(ones_row[:], 1.0), writes=[onb])
    rmask = C.sb("ml_rmask", [1, T], F32)
    rmb = Buf("ml_rmask")
    S.op("pool", lambda E: E.memset(rmask[:], 1.0), writes=[rmb])
    S.op("pool", lambda E: E.memset(rmask[:].rearrange("p (c l) -> p c l", l=L)[:, :, 0:1], 0.0), reads=[rmb], writes=[rmb])
    cmask = C.sb("ml_cmask", [L, L], F32)
    cmb = Buf("ml_cmask")
    S.op("pool", lambda E: E.memset(cmask[:], 1.0), writes=[cmb])
    S.op("pool", lambda E: E.affine_select(out=cmask[:], in_=cmask[:], pattern=[[1, L]], compare_op=ALU.is_ge, fill=0.0,
                                           base=0, channel_multiplier=-1), reads=[cmb], writes=[cmb])
    cwt = C.sb("ml_cw", [128, 2, 4], F32)
    cbt = C.sb("ml_cb", [128, 2], F32)
    gbt = C.sb("ml_gb", [1, 2], F32)
    gain = C.sb("ml_gain", [L, DV], F32)
    prb = Buf("ml_params")
    S.dma("sp", cwt[:], cw, writes=[prb])
    S.dma("sp", cbt[:], cb, writes=[prb])
    S.dma("sp", gbt[:], gbias, writes=[prb])
    S.dma("sp", gain[:], gain_rep, writes=[prb])
    S.op("dve", lambda E: E.tensor_scalar(out=gbt[:], in0=gbt[:], scalar1=1.0 / 15.0, scalar2=None, op0=ALU.mult), reads=[prb], writes=[prb])
    eps = eps_tile(C)
    epsb = eps_buf(C)
    ps_m = [C.ps("ml_pm%d" % i, [128, 512]) for i in range(2)]
    psmb = [Buf("ml_pm%d" % i) for i in range(2)]
    irow = C.sb("ml_i", [1, T], F32)
    frow = C.sb("ml_f", [1, T], F32)
    brow = C.sb("ml_b", [1, T], F32)
    irb, frb, brb = Buf("ml_i"), Buf("ml_f"), Buf("ml_b")
    S.dma("sp", irow[:], pT[r_i:r_i + 1, t0:t0 + T], reads=[pTb], writes=[irb])
    S.dma("sp", frow[:], pT[r_f:r_f + 1, t0:t0 + T], reads=[pTb], writes=[frb])
    S.op("act", lambda E: E.activation(out=irow[:], in_=irow[:], func=AF.Tanh, scale=1.0 / 15.0, bias=gbt[0:1, 0:1]), reads=[irb, prb], writes=[irb])
    S.op("act", lambda E: E.activation(out=frow[:], in_=frow[:], func=AF.Tanh, scale=1.0 / 15.0, bias=gbt[0:1, 1:2]), reads=[frb, prb], writes=[frb])
    S.op("act", lambda E: E.activation(out=frow[:], in_=frow[:], func=AF.Exp, scale=-15.0), reads=[frb], writes=[frb])
    S.op("act", lambda E: E.activation(out=frow[:], in_=frow[:], func=AF.Ln, scale=1.0, bias=ones_row[0:1, 0:1]), reads=[frb, onb], writes=[frb])
    S.op("dve", lambda E: E.tensor_tensor_scan(out=brow[:], data0=rmask[:], data1=frow[:], initial=0.0, op0=ALU.mult, op1=ALU.subtract),
         reads=[rmb, frb], writes=[brb])
    S.op("dve", lambda E: E.scalar_tensor_tensor(out=irow[:], in0=irow[:], scalar=15.0, in1=brow[:], op0=ALU.mult, op1=ALU.subtract),
         reads=[irb, brb], writes=[irb])
    S.op("act", lambda E: E.activation(out=irow[:], in_=irow[:], func=AF.Exp), reads=[irb], writes=[irb])
    S.op("act", lambda E: E.activation(out=frow[:], in_=brow[:], func=AF.Exp), reads=[brb], writes=[frb])
    egb_t = C.sb("ml_eg", [128, NC], F32)
    egb = Buf("ml_eg")
    S.op("pe", lambda E: E.matmul(ps_m[0][:, 0:NC], lhsT=ones_row[0:1, 0:128],
                                  rhs=frow[:].rearrange("p (c l) -> p c l", l=L)[:, :, L - 1], start=True, stop=True),
         reads=[onb, frb], writes=[psmb[0]])
    S.op("dve", lambda E: E.tensor_copy(out=egb_t[:], in_=ps_m[0][:, 0:NC]), reads=[psmb[0]], writes=[egb])
    xp = C.sb("ml_xp", [128, T + 3], F32)
    xpb = Buf("ml_xp")
    acc = C.sb("ml_acc", [128, T], F32)
    accb = Buf("ml_acc")
    qT = C.sb("ml_qT", [128, T], F32)
    kT = C.sb("ml_kT", [128, T], F32)
    qTb, kTb = Buf("ml_qT"), Buf("ml_kT")
    for which, (r0, dst, dstb, srow, srb) in enumerate(((r_q, qT, qTb, frow, frb), (r_k, kT, kTb, irow, irb))):
        if t0 == 0:
            S.op("pool", lambda E: E.memset(xp[:, 0:3], 0.0), reads=[], writes=[xpb])
            S.dma("sp", xp[:, 3:T + 3], pT[r0:r0 + 128, 0:T], reads=[pTb], writes=[xpb])
        else:
            S.dma("sp", xp[:, 0:T + 3], pT[r0:r0 + 128, t0 - 3:t0 + T], reads=[pTb], writes=[xpb])
        S.op("dve", lambda E, which=which: E.tensor_scalar(out=acc[:], in0=xp[:, 3:T + 3], scalar1=cwt[:, which, 3:4], scalar2=cbt[:, which:which + 1],
                                                          op0=ALU.mult, op1=ALU.add), reads=[xpb, prb], writes=[accb])
        for i in range(3):
            S.op("dve", lambda E, which=which, i=i: E.scalar_tensor_tensor(out=acc[:], in0=xp[:, i:T + i], scalar=cwt[:, which, i:i + 1], in1=acc[:],
                                                                          op0=ALU.mult, op1=ALU.add), reads=[xpb, prb, accb], writes=[accb])
        S.op("act", lambda E: E.activation(out=acc[:], in_=acc[:], func=AF.Silu), reads=[accb], writes=[accb])
        sc = (DK ** -0.5) if which == 0 else 1.0
        for c0 in range(0, T, 512):
            csz = min(512, T - c0)
            pm = (c0 // 512) % 2
            S.op("pe", lambda E, c0=c0, csz=csz, pm=pm, srow=srow: E.matmul(ps_m[pm][:, 0:csz], lhsT=ones_row[0:1, 0:128], rhs=srow[0:1, c0:c0 + csz],
                                                                           start=True, stop=True), reads=[onb, srb], writes=[psmb[pm]])
            S.op("dve", lambda E, c0=c0, csz=csz, pm=pm, dst=dst, sc=sc: E.scalar_tensor_tensor(out=dst[:, c0:c0 + csz], in0=acc[:, c0:c0 + csz], scalar=sc,
                                                                                                in1=ps_m[pm][:, 0:csz], op0=ALU.mult, op1=ALU.mult),
                 reads=[accb, psmb[pm]], writes=[dstb])
    ktok = C.sb("ml_ktok", [L, NC, DK], F32)
    vext = C.sb("ml_vext", [L, NC, DV + 1], F32)
    ogt = C.sb("ml_og", [L, NC, DV], F32)
    ktokb, vextb, ogb = Buf("ml_ktok"), Buf("ml_vext"), Buf("ml_og")
    S.op("pool", lambda E: E.memset(vext[:, :, DV:DV + 1], 1.0), writes=[vextb])
    tsrc = C.sb("ml_tsrc", [128, T], F32)
    tsb = Buf("ml_tsrc")
    tcnt = 0
    for (r0, kind) in ((r_v, "v0"), (r_v + 128, "v1"), (r_o, "o0"), (r_o + 128, "o1"), (None, "k")):
        if kind == "k":
            src, srcb = kT, kTb
        else:
            S.dma("sp", tsrc[:], pT[r0:r0 + 128, t0:t0 + T], reads=[pTb], writes=[tsb])
            if kind[0] == "o":
                S.op("act", lambda E: E.activation(out=tsrc[:], in_=tsrc[:], func=AF.Sigmoid), reads=[tsb], writes=[tsb])
            src, srcb = tsrc, tsb
        for c in range(NC):
            pm = tcnt % 2
            tcnt += 1
            S.op("pe", lambda E, c=c, pm=pm, src=src: E.transpose(ps_m[pm][0:L, 0:128], src[:, c * L:(c + 1) * L], ident[:]),
                 reads=[srcb, idb], writes=[psmb[pm]])
            if kind == "k":
                dst, dstb_ = ktok[:, c, :], ktokb
            elif kind[0] == "v":
                off = 128 * int(kind[1])
                dst, dstb_ = vext[:, c, off:off + 128], vextb
            else:
                off = 128 * int(kind[1])
                dst, dstb_ = ogt[:, c, off:off + 128], ogb
            eng = "act" if (tcnt % 2) else "dve"
            if eng == "act":
                S.op("act", lambda E, pm=pm, dst=dst: E.copy(out=dst, in_=ps_m[pm][0:L, 0:128]), reads=[psmb[pm]], writes=[dstb_])
            else:
                S.op("dve", lambda E, pm=pm, dst=dst: E.tensor_copy(out=dst, in_=ps_m[pm][0:L, 0:128]), reads=[psmb[pm]], writes=[dstb_])
    ps_s = [C.ps("ml_pss%d" % i, [L, 512]) for i in range(2)]
    ps_sb = [Buf("ml_pss%d" % i) for i in range(2)]
    ps_o = [C.ps("ml_pso%d" % i, [L, 512]) for i in range(2)]
    ps_ob = [Buf("ml_pso%d" % i) for i in range(2)]
    ps_u = C.ps("ml_psu", [128, 512])
    ps_ub = Buf("ml_psu")
    PTt = [C.sb("ml_PT%d" % i, [L, L], F32) for i in range(2)]
    PTb = [Buf("ml_PT%d" % i) for i in range(2)]
    hh = [C.sb("ml_hh%d" % i, [L, DV], F32) for i in range(2)]
    hhb = [Buf("ml_hh%d" % i) for i in range(2)]
    junk = C.sb("ml_junk", [L, DV], F32)
    junkb = Buf("ml_junk")
    sml = [C.sb("ml_sm%d" % i, [L, 4], F32) for i in range(2)]
    smlb = [Buf("ml_sm%d" % i) for i in range(2)]
    yo = [C.sb("ml_yo%d" % i, [L, DV], BF16) for i in range(2)]
    yob = [Buf("ml_yo%d" % i) for i in range(2)]
    for c in range(NC):
        s = c % 2
        cs = slice(c * L, (c + 1) * L)
        S.op("pe", lambda E, cs=cs, s=s: E.matmul(ps_s[s][:, 0:L], lhsT=kT[:, cs], rhs=qT[:, cs], start=True, stop=True),
             reads=[kTb, qTb], writes=[ps_sb[s]])
        S.op("dve", lambda E, s=s: E.tensor_tensor(out=PTt[s][:], in0=ps_s[s][:, 0:L], in1=cmask[:], op=ALU.mult),
             reads=[ps_sb[s], cmb], writes=[PTb[s]])
        S.op("pe", lambda E, c=c, s=s: E.matmul(ps_o[s][:, 0:DV + 1], lhsT=PTt[s][:], rhs=vext[:, c, :], start=True, stop=False),
             reads=[PTb[s], vextb], writes=[ps_ob[s]])
        S.op("pe", lambda E, cs=cs, s=s: E.matmul(ps_o[s][:, 0:DV + 1], lhsT=qT[:, cs], rhs=state[:], start=False, stop=True),
             reads=[qTb, stb], writes=[ps_ob[s]], pe_chain=True)
        S.op("pe", lambda E, c=c: E.matmul(ps_u[:, 0:DV + 1], lhsT=ktok[:, c, :], rhs=vext[:, c, :], start=True, stop=True),
             reads=[ktokb, vextb], writes=[ps_ub])
        S.op("dve", lambda E, c=c: E.tensor_scalar(out=state[:], in0=state[:], scalar1=egb_t[:, c:c + 1], scalar2=None, op0=ALU.mult),
             reads=[stb, egb], writes=[stb])
        S.op("dve", lambda E, c=c: E.scalar_tensor_tensor(out=state[:], in0=ps_u[:, 0:DV + 1], scalar=egb_t[:, c:c + 1], in1=state[:],
                                                          op0=ALU.mult, op1=ALU.add), reads=[ps_ub, egb, stb], writes=[stb])
        sm = sml[s]
        S.op("act", lambda E, s=s, sm=sm: E.activation(out=sm[:, 0:1], in_=ps_o[s][:, DV:DV + 1], func=AF.Abs),
             reads=[ps_ob[s]], writes=[smlb[s]])
        S.op("dve", lambda E, sm=sm: E.tensor_scalar(out=sm[:, 0:1], in0=sm[:, 0:1], scalar1=1.0, scalar2=None, op0=ALU.max),
             reads=[smlb[s]], writes=[smlb[s]])
        S.op("dve", lambda E, sm=sm: E.reciprocal(out=sm[:, 0:1], in_=sm[:, 0:1]), reads=[smlb[s]], writes=[smlb[s]])
        S.op("act", lambda E, s=s, sm=sm: E.activation(out=hh[s][:], in_=ps_o[s][:, 0:DV], func=AF.Identity, scale=sm[:, 0:1]),
             reads=[ps_ob[s], smlb[s]], writes=[hhb[s]])
        S.op("act", lambda E, s=s: E.activation(out=junk[:], in_=hh[s][:], func=AF.Square), reads=[hhb[s]], writes=[junkb])
        S.op("dve", lambda E, sm=sm: E.reduce_sum(out=sm[:, 1:2], in_=junk[:], axis=AX.X), reads=[junkb], writes=[smlb[s]])
        S.op("act", lambda E, sm=sm: E.activation(out=sm[:, 2:3], in_=sm[:, 1:2], func=AF.Sqrt, scale=1.0 / DV, bias=eps[0:L, 0:1]),
             reads=[smlb[s], epsb], writes=[smlb[s]])
        S.op("dve", lambda E, sm=sm: E.reciprocal(out=sm[:, 3:4], in_=sm[:, 2:3]), reads=[smlb[s]], writes=[smlb[s]])
        S.op("dve", lambda E, s=s, sm=sm: E.scalar_tensor_tensor(out=hh[s][:], in0=hh[s][:], scalar=sm[:, 3:4], in1=gain[:], op0=ALU.mult, op1=ALU.mult),
             reads=[hhb[s], smlb[s], prb], writes=[hhb[s]])
        S.op("dve", lambda E, s=s, c=c: E.tensor_tensor(out=yo[s][:], in0=hh[s][:], in1=ogt[:, c, :], op=ALU.mult),
             reads=[hhb[s], ogb], writes=[yob[s]])
        S.dma("sp", y[t0 + c * L:t0 + (c + 1) * L, ycol0:ycol0 + DV], yo[s][:], reads=[yob[s]], writes=[yb], owner=yob[s], is_output=True)


def rwkv_stage(C, pT, pTb, r_r, r_k, r_v, r_wl, r_al, r_gl, prm, mul, lnp, w_up, a_up, g_up, yT, yTb, T, uid="rw"):
    S = C.S
    NP = 3
    TBA = min(512, T)
    Rfm = C.dram(uid + "_Rfm", [NP, 128, T], F32)
    KKfm = C.dram(uid + "_KKfm", [NP, 128, T], F32)
    Wfm = C.dram(uid + "_Wfm", [NP, 128, T], F32)
    BONfm = C.dram(uid + "_BONfm", [NP, 128, T], F32)
    Gfm = C.dram(uid + "_Gfm", [NP, 128, T], F32)
    NBtm = C.dram(uid + "_NBtm", [T, 384], F32)
    KMtm = C.dram(uid + "_KMtm", [T, 384], F32)
    Vtm = C.dram(uid + "_Vtm", [T, 384], F32)
    Yh = C.dram(uid + "_Yh", [6, 64, T], F32)
    scrb = Buf(uid + "_scratchA")
    yhb = Buf(uid + "_Yh")
    DEC = -math.exp(-0.5)
    with C.stage():
        ident, idb = make_ident(C, "ra")
        bones = C.sb("ra_bones", [128, 128], F32)
        bob = Buf("ra_bones")
        S.op("pool", lambda E: E.memset(bones[:], 0.0), writes=[bob])
        S.op("pool", lambda E: E.memset(bones[0:64, 0:64], 1.0), reads=[bob], writes=[bob])
        S.op("pool", lambda E: E.memset(bones[64:128, 64:128], 1.0), reads=[bob], writes=[bob])
        prmt = C.sb("ra_prm", [128, NP, 11], F32)
        mult = C.sb("ra_mul", [128, 6], F32)
        wup = C.sb("ra_wup", [128, 384], F32)
        aup = C.sb("ra_aup", [128, 384], F32)
        gup = C.sb("ra_gup", [128, 4, 384], F32)
        pb = Buf("ra_params")
        S.dma("sp", prmt[:], prm, writes=[pb])
        S.dma("sp", mult[:], mul, writes=[pb])
        S.dma("sp", wup[:], w_up, writes=[pb])
        S.dma("sp", aup[:], a_up, writes=[pb])
        for ki, (k0, ksz) in enumerate(ktiles(480)):
            S.dma("sp", gup[0:ksz, ki, :], g_up[k0:k0 + ksz, :], writes=[pb])
        tiny = C.sb("ra_tiny", [128, 1], F32)
        tnb = Buf("ra_tiny")
        S.op("pool", lambda E: E.memset(tiny[:], 0.0), writes=[tnb])

        def tl(name, shape=None):
            return C.sb("ra_" + name, shape or [128, TBA], F32), Buf("ra_" + name)

        xp, xpb = tl("xp", [128, TBA + 1])
        dd, ddb = tl("dd")
        twl, twlb = tl("twl")
        als, alsb = tl("als")
        sgl, sglb = tl("sgl", [128, 4, TBA])
        rs, rsb = tl("rs")
        ks, ksb = tl("ks")
        vs, vsb = tl("vs")
        Wt, Wtb = tl("W")
        at, atb = tl("a")
        gt, gtb = tl("g")
        kk, kkb = tl("kk")
        t1, t1b = tl("t1")
        t2, t2b = tl("t2")
        km, kmb = tl("km")
        nbt, nbtb = tl("nb")
        bon, bonb = tl("bon")
        tst = [C.sb("ra_tst%d" % i, [128, 128], F32) for i in range(3)]
        tstb = [Buf("ra_tst%d" % i) for i in range(3)]
        psA = [C.ps("ra_ps%d" % i, [128, 512]) for i in range(6)]
        psAb = [Buf("ra_ps%d" % i) for i in range(6)]
        pc = [0]

        def nps():
            i = pc[0] % 6
            pc[0] += 1
            return psA[i], psAb[i]

        def shifted(row0, nrows, t0, tsz, mu_ap, dst, dstb, dsl=None):
            if t0 == 0:
                S.op("pool", lambda E: E.memset(xp[0:nrows, 0:1], 0.0), writes=[xpb])
                S.dma("sp", xp[0:nrows, 1:tsz + 1], pT[row0:row0 + nrows, 0:tsz], reads=[pTb], writes=[xpb])
            else:
                S.dma("sp", xp[0:nrows, 0:tsz + 1], pT[row0:row0 + nrows, t0 - 1:t0 + tsz], reads=[pTb], writes=[xpb])
            S.op("dve", lambda E: E.tensor_tensor(out=dd[0:nrows, 0:tsz], in0=xp[0:nrows, 0:tsz], in1=xp[0:nrows, 1:tsz + 1], op=ALU.subtract),
                 reads=[xpb], writes=[ddb])
            d_ap = dst[0:nrows, 0:tsz] if dsl is None else dsl
            S.op("dve", lambda E: E.scalar_tensor_tensor(out=d_ap, in0=dd[0:nrows, 0:tsz], scalar=mu_ap, in1=xp[0:nrows, 1:tsz + 1],
                                                         op0=ALU.mult, op1=ALU.add), reads=[ddb, xpb, pb], writes=[dstb])

        tr_i = [0]
        for t0 in range(0, T, TBA):
            tsz = min(TBA, T - t0)
            shifted(r_wl, 128, t0, tsz, mult[:, 0:1], twl, twlb)
            S.op("act", lambda E: E.activation(out=twl[:, 0:tsz], in_=twl[:, 0:tsz], func=AF.Tanh), reads=[twlb], writes=[twlb])
            shifted(r_al, 128, t0, tsz, mult[:, 1:2], als, alsb)
            for ki, (k0, ksz) in enumerate(ktiles(480)):
                shifted(r_gl + k0, ksz, t0, tsz, mult[0:ksz, 2 + ki:3 + ki], sgl, sglb, dsl=sgl[0:ksz, ki, 0:tsz])
                S.op("act", lambda E, ki=ki, ksz=ksz: E.activation(out=sgl[0:ksz, ki, 0:tsz], in_=sgl[0:ksz, ki, 0:tsz], func=AF.Sigmoid),
                     reads=[sglb], writes=[sglb])
            for pr in range(NP):
                P = lambda c: prmt[:, pr, c:c + 1]
                cs = slice(pr * 128, (pr + 1) * 128)
                shifted(r_r + pr * 128, 128, t0, tsz, P(0), rs, rsb)
                shifted(r_k + pr * 128, 128, t0, tsz, P(1), ks, ksb)
                shifted(r_v + pr * 128, 128, t0, tsz, P(2), vs, vsb)
                p1, p1b = nps()
                S.op("pe", lambda E: E.matmul(p1[:, 0:tsz], lhsT=wup[:, cs], rhs=twl[:, 0:tsz], start=True, stop=True), reads=[pb, twlb], writes=[p1b])
                S.op("act", lambda E: E.activation(out=Wt[:, 0:tsz], in_=p1[:, 0:tsz], func=AF.Sigmoid, bias=P(3), scale=1.0), reads=[p1b, pb], writes=[Wtb])
                S.op("act", lambda E: E.activation(out=Wt[:, 0:tsz], in_=Wt[:, 0:tsz], func=AF.Exp, scale=DEC), reads=[Wtb], writes=[Wtb])
                p2, p2b = nps()
                S.op("pe", lambda E: E.matmul(p2[:, 0:tsz], lhsT=aup[:, cs], rhs=als[:, 0:tsz], start=True, stop=True), reads=[pb, alsb], writes=[p2b])
                S.op("act", lambda E: E.activation(out=at[:, 0:tsz], in_=p2[:, 0:tsz], func=AF.Sigmoid, bias=P(4), scale=1.0), reads=[p2b, pb], writes=[atb])
                p3, p3b = nps()
                kts = ktiles(480)
                for ki, (k0, ksz) in enumerate(kts):
                    S.op("pe", lambda E, ki=ki, ksz=ksz: E.matmul(p3[:, 0:tsz], lhsT=gup[0:ksz, ki, cs], rhs=sgl[0:ksz, ki, 0:tsz],
                                                                 start=(ki == 0), stop=(ki == len(kts) - 1)), reads=[pb, sglb], writes=[p3b], pe_chain=(ki > 0))
                S.op("act", lambda E: E.copy(out=gt[:, 0:tsz], in_=p3[:, 0:tsz]), reads=[p3b], writes=[gtb])
                S.op("dve", lambda E: E.tensor_scalar(out=kk[:, 0:tsz], in0=ks[:, 0:tsz], scalar1=P(5), scalar2=None, op0=ALU.mult), reads=[ksb, pb], writes=[kkb])
                S.op("act", lambda E: E.activation(out=t1[:, 0:tsz], in_=kk[:, 0:tsz], func=AF.Square), reads=[kkb], writes=[t1b])
                p4, p4b = nps()
                S.op("pe", lambda E: E.matmul(p4[:, 0:tsz], lhsT=bones[:], rhs=t1[:, 0:tsz], start=True, stop=True), reads=[bob, t1b], writes=[p4b])
                S.op("act", lambda E: E.activation(out=t1[:, 0:tsz], in_=p4[:, 0:tsz], func=AF.Sqrt), reads=[p4b], writes=[t1b])
                S.op("dve", lambda E: E.tensor_scalar(out=t1[:, 0:tsz], in0=t1[:, 0:tsz], scalar1=1e-12, scalar2=None, op0=ALU.max), reads=[t1b], writes=[t1b])
                S.op("dve", lambda E: E.reciprocal(out=t1[:, 0:tsz], in_=t1[:, 0:tsz]), reads=[t1b], writes=[t1b])
                S.op("dve", lambda E: E.tensor_tensor(out=kk[:, 0:tsz], in0=kk[:, 0:tsz], in1=t1[:, 0:tsz], op=ALU.mult), reads=[kkb, t1b], writes=[kkb])
                S.op("dve", lambda E: E.tensor_scalar(out=t2[:, 0:tsz], in0=at[:, 0:tsz], scalar1=-1.0, scalar2=P(6), op0=ALU.add, op1=ALU.mult),
                     reads=[atb, pb], writes=[t2b])
                S.op("dve", lambda E: E.scalar_tensor_tensor(out=km[:, 0:tsz], in0=t2[:, 0:tsz], scalar=1.0, in1=ks[:, 0:tsz], op0=ALU.add, op1=ALU.mult),
                     reads=[t2b, ksb], writes=[kmb])
                S.op("dve", lambda E: E.scalar_tensor_tensor(out=nbt[:, 0:tsz], in0=at[:, 0:tsz], scalar=-1.0, in1=kk[:, 0:tsz], op0=ALU.mult, op1=ALU.mult),
                     reads=[atb, kkb], writes=[nbtb])
                S.op("dve", lambda E: E.scalar_tensor_tensor(out=t2[:, 0:tsz], in0=rs[:, 0:tsz], scalar=P(7), in1=km[:, 0:tsz], op0=ALU.mult, op1=ALU.mult),
                     reads=[rsb, kmb, pb], writes=[t2b])
                p5, p5b = nps()
                S.op("pe", lambda E: E.matmul(p5[:, 0:tsz], lhsT=bones[:], rhs=t2[:, 0:tsz], start=True, stop=True), reads=[bob, t2b], writes=[p5b])
                S.op("dve", lambda E: E.tensor_tensor(out=bon[:, 0:tsz], in0=p5[:, 0:tsz], in1=vs[:, 0:tsz], op=ALU.mult), reads=[p5b, vsb], writes=[bonb])
                for (dr, src, srcb) in ((Rfm, rs, rsb), (KKfm, kk, kkb), (Wfm, Wt, Wtb), (BONfm, bon, bonb), (Gfm, gt, gtb)):
                    S.dma("sp", dr[pr, :, t0:t0 + tsz], src[:, 0:tsz], reads=[srcb], writes=[scrb], owner=srcb)
                for (dr, src, srcb) in ((NBtm, nbt, nbtb), (KMtm, km, kmb), (Vtm, vs, vsb)):
                    for c0 in range(0, tsz, 128):
                        pp, ppb = nps()
                        si = tr_i[0] % 3
                        tr_i[0] += 1
                        S.op("pe", lambda E, c0=c0, src=src, pp=pp: E.transpose(pp[:, 0:128], src[:, c0:c0 + 128], ident[:]), reads=[srcb, idb], writes=[ppb])
                        if si == 0:
                            S.op("act", lambda E, pp=pp, si=si: E.copy(out=tst[si][:], in_=pp[:, 0:128]), reads=[ppb], writes=[tstb[si]])
                        else:
                            S.op("dve", lambda E, pp=pp, si=si: E.tensor_copy(out=tst[si][:], in_=pp[:, 0:128]), reads=[ppb], writes=[tstb[si]])
                        S.dma("sp", dr[t0 + c0:t0 + c0 + 128, pr * 128:(pr + 1) * 128], tst[si][:], reads=[tstb[si]], writes=[scrb], owner=tstb[si])
    with C.stage():
        TB2 = min(256, T)
        TBK = 32
        TS = 64
        hmask = C.sb("rb_hmask", [128, 2], F32)
        hmb = Buf("rb_hmask")
        S.op("pool", lambda E: E.memset(hmask[:], 0.0), writes=[hmb])
        S.op("pool", lambda E: E.memset(hmask[0:64, 0:1], 1.0), reads=[hmb], writes=[hmb])
        S.op("pool", lambda E: E.memset(hmask[64:128, 1:2], 1.0), reads=[hmb], writes=[hmb])
        mask6 = C.sb("rb_mask6", [6, 3, 64], F32)
        m6b = Buf("rb_mask6")
        S.op("pool", lambda E: E.memset(mask6[:], 1.0), writes=[m6b])
        S.op("pool", lambda E: E.affine_select(out=mask6[:], in_=mask6[:], pattern=[[-2, 3], [0, 64]], compare_op=ALU.is_ge, fill=0.0,
                                               base=0, channel_multiplier=1), reads=[m6b], writes=[m6b])
        S.op("pool", lambda E: E.affine_select(out=mask6[:], in_=mask6[:], pattern=[[2, 3], [0, 64]], compare_op=ALU.is_ge, fill=0.0,
                                               base=1, channel_multiplier=-1), reads=[m6b], writes=[m6b])
        fm = [[C.sb("rb_fm%d_%d" % (k, i), [128, NP, TB2], F32) for i in range(2)] for k in range(3)]
        fmb = [[Buf("rb_fm%d_%d" % (k, i)) for i in range(2)] for k in range(3)]
        KKZ = [C.sb("rb_kkz%d" % i, [128, TB2, 6], F32) for i in range(2)]
        RZ = [C.sb("rb_rz%d" % i, [128, TB2, 6], F32) for i in range(2)]
        KKZb = [Buf("rb_kkz%d" % i) for i in range(2)]
        RZb = [Buf("rb_rz%d" % i) for i in range(2)]
        LB = [C.sb("rb_lb%d" % i, [6, TBK, 128], F32) for i in range(2)]
        LK = [C.sb("rb_lk%d" % i, [6, TBK, 128], F32) for i in range(2)]
        VM = [C.sb("rb_vm%d" % i, [6, TBK, 192], F32) for i in range(2)]
        LBb = [Buf("rb_lb%d" % i) for i in range(2)]
        LKb = [Buf("rb_lk%d" % i) for i in range(2)]
        VMb = [Buf("rb_vm%d" % i) for i in range(2)]
        for i in range(2):
            S.op("pool", lambda E, i=i: E.memset(LB[i][:], 0.0), writes=[LBb[i]])
            S.op("pool", lambda E, i=i: E.memset(LK[i][:], 0.0), writes=[LKb[i]])
            S.op("pool", lambda E, i=i: E.memset(VM[i][:], 0.0), writes=[VMb[i]])
        St = C.sb("rb_S", [128, 192], F32)
        Sd = C.sb("rb_Sd", [128, 192], F32)
        Sb, Sdb = Buf("rb_S"), Buf("rb_Sd")
        S.op("pool", lambda E: E.memset(St[:], 0.0), writes=[Sb])
        RH = [C.sb("rb_rh%d" % i, [6, 192], F32) for i in range(2)]
        RHb = [Buf("rb_rh%d" % i) for i in range(2)]
        ps_sk = [C.ps("rb_psk%d" % i, [6, 512]) for i in range(2)]
        ps_skb = [Buf("rb_psk%d" % i) for i in range(2)]
        ps_up = [C.ps("rb_pup%d" % i, [128, 512]) for i in range(2)]
        ps_upb = [Buf("rb_pup%d" % i) for i in range(2)]
        ps_y = [C.ps("rb_py%d" % i, [64, 512]) for i in range(2)]
        ps_yb = [Buf("rb_py%d" % i) for i in range(2)]
        Yst = [C.sb("rb_yst%d" % i, [64, 6, TS], F32) for i in range(2)]
        Ystb = [Buf("rb_yst%d" % i) for i in range(2)]
        for t in range(T):
            f2 = (t // TB2) % 2
            tf = t % TB2
            if tf == 0:
                n2 = min(TB2, T - t)
                for k, dr in enumerate((KKfm, Rfm, Wfm)):
                    S.dma("sp", fm[k][f2][:, :, 0:n2], dr[:, :, t:t + n2].rearrange("a p t -> p a t"), reads=[scrb], writes=[fmb[k][f2]])
                for pr in range(NP):
                    for h2 in range(2):
                        S.op("pool", lambda E, pr=pr, h2=h2: E.tensor_scalar(out=KKZ[f2][:, 0:n2, 2 * pr + h2], in0=fm[0][f2][:, pr, 0:n2],
                                                                           scalar1=hmask[:, h2:h2 + 1], scalar2=None, op0=ALU.mult),
                             reads=[fmb[0][f2], hmb], writes=[KKZb[f2]])
                        S.op("pool", lambda E, pr=pr, h2=h2: E.tensor_scalar(out=RZ[f2][:, 0:n2, 2 * pr + h2], in0=fm[1][f2][:, pr, 0:n2],
                                                                           scalar1=hmask[:, h2:h2 + 1], scalar2=None, op0=ALU.mult),
                             reads=[fmb[1][f2], hmb], writes=[RZb[f2]])
            fk = (t // TBK) % 2
            tk = t % TBK
            if tk == 0:
                nk = min(TBK, T - t)
                for h2 in range(2):
                    S.dma("sp", LB[fk][h2:6:2, 0:nk, h2 * 64:(h2 + 1) * 64],
                          NBtm[t:t + nk, :].rearrange("t (pr h j) -> pr h t j", pr=3, h=2)[:, h2], reads=[scrb], writes=[LBb[fk]])
                    S.dma("sp", LK[fk][h2:6:2, 0:nk, h2 * 64:(h2 + 1) * 64],
                          KMtm[t:t + nk, :].rearrange("t (pr h j) -> pr h t j", pr=3, h=2)[:, h2], reads=[scrb], writes=[LKb[fk]])
                for pr in range(NP):
                    S.dma("sp", VM[fk][2 * pr:2 * pr + 2, 0:nk, pr * 64:(pr + 1) * 64],
                          Vtm[t:t + nk, pr * 128:(pr + 1) * 128].rearrange("t (h j) -> h t j", h=2), reads=[scrb], writes=[VMb[fk]])
            s = t % 2
            ys = (t // TS) % 2
            ty = t % TS
            S.op("pe", lambda E, s=s, fk=fk, tk=tk: E.matmul(ps_up[s][:, 0:192], lhsT=LK[fk][:, tk, :], rhs=VM[fk][:, tk, :], start=True, stop=False),
                 reads=[LKb[fk], VMb[fk]], writes=[ps_upb[s]])
            S.op("pe", lambda E, s=s, f2=f2, tf=tf: E.matmul(ps_sk[s][:, 0:192], lhsT=KKZ[f2][:, tf, :], rhs=St[:], start=True, stop=True),
                 reads=[KKZb[f2], Sb], writes=[ps_skb[s]])
            S.op("dve", lambda E, s=s: E.tensor_tensor(out=RH[s][:], in0=ps_sk[s][:, 0:192], in1=mask6[:].rearrange("p a i -> p (a i)"), op=ALU.mult),
                 reads=[ps_skb[s], m6b], writes=[RHb[s]])
            S.op("pe", lambda E, s=s, fk=fk, tk=tk: E.matmul(ps_up[s][:, 0:192], lhsT=LB[fk][:, tk, :], rhs=RH[s][:], start=False, stop=True),
                 reads=[LBb[fk], RHb[s]], writes=[ps_upb[s]], pe_chain=True)
            for pr in range(NP):
                eng = ("pool", "act", "pool")[pr]
                if eng == "act":
                    S.op("act", lambda E, pr=pr, f2=f2, tf=tf: E.activation(out=Sd[:, pr * 64:(pr + 1) * 64], in_=St[:, pr * 64:(pr + 1) * 64], func=AF.Identity,
                                                                          scale=fm[2][f2][:, pr, tf:tf + 1]), reads=[Sb, fmb[2][f2]], writes=[Sdb])
                else:
                    S.op("pool", lambda E, pr=pr, f2=f2, tf=tf: E.tensor_scalar(out=Sd[:, pr * 64:(pr + 1) * 64], in0=St[:, pr * 64:(pr + 1) * 64],
                                                                              scalar1=fm[2][f2][:, pr, tf:tf + 1], scalar2=None, op0=ALU.mult),
                         reads=[Sb, fmb[2][f2]], writes=[Sdb])
            S.op("dve", lambda E, s=s: E.tensor_tensor(out=St[:], in0=Sd[:], in1=ps_up[s][:, 0:192], op=ALU.add), reads=[Sdb, ps_upb[s]], writes=[Sb])
            for pr in range(NP):
                S.op("pe", lambda E, pr=pr, ys=ys, ty=ty, f2=f2, tf=tf: E.matmul(ps_y[ys][:, ty * 6 + 2 * pr:ty * 6 + 2 * pr + 2], lhsT=St[:, pr * 64:(pr + 1) * 64],
                                                                              rhs=RZ[f2][:, tf, 2 * pr:2 * pr + 2], start=True, stop=True),
                     reads=[Sb, RZb[f2]], writes=[ps_yb[ys]], pe_chain=(not (ty == 0 and pr == 0)))
            if ty == TS - 1 or t == T - 1:
                n = ty + 1
                tb0 = t - ty
                S.op("act", lambda E, ys=ys, n=n: E.copy(out=Yst[ys][:, :, 0:n], in_=ps_y[ys][:, 0:n * 6].rearrange("p (t h) -> p h t", h=6)),
                     reads=[ps_yb[ys]], writes=[Ystb[ys]])
                S.dma("sp", Yh[:, :, tb0:tb0 + n].rearrange("h i t -> i h t"), Yst[ys][:, :, 0:n], reads=[Ystb[ys]], writes=[yhb], owner=Ystb[ys])
    with C.stage():
        o64 = C.sb("rc_ones", [64, 64], F32)
        o64b = Buf("rc_ones")
        S.op("pool", lambda E: E.memset(o64[:], 1.0 / 64.0), writes=[o64b])
        lnt = C.sb("rc_ln", [64, 6, 2], F32)
        lnb = Buf("rc_ln")
        S.dma("sp", lnt[:], lnp, writes=[lnb])
        epst = C.sb("rc_eps", [64, 1], F32)
        epstb = Buf("rc_eps")
        S.op("pool", lambda E: E.memset(epst[:], RWKV_LN_EPS), writes=[epstb])
        TC = min(512, T)
        yt = [C.sb("rc_y%d" % i, [64, TC], F32) for i in range(2)]
        bt = [C.sb("rc_b%d" % i, [64, TC], F32) for i in range(2)]
        gg = [C.sb("rc_g%d" % i, [64, TC], F32) for i in range(2)]
        ytb = [Buf("rc_y%d" % i) for i in range(2)]
        btb = [Buf("rc_b%d" % i) for i in range(2)]
        ggb = [Buf("rc_g%d" % i) for i in range(2)]
        yc = C.sb("rc_yc", [64, TC], F32)
        ycb = Buf("rc_yc")
        sq = C.sb("rc_sq", [64, TC], F32)
        sqb = Buf("rc_sq")
        rsd = C.sb("rc_rsd", [64, TC], F32)
        rsdb = Buf("rc_rsd")
        oo = [C.sb("rc_o%d" % i, [64, TC], BF16) for i in range(2)]
        oob = [Buf("rc_o%d" % i) for i in range(2)]
        pm = [C.ps("rc_pm%d" % i, [64, 512]) for i in range(2)]
        pmb = [Buf("rc_pm%d" % i) for i in range(2)]
        pv = [C.ps("rc_pv%d" % i, [64, 512]) for i in range(2)]
        pvb = [Buf("rc_pv%d" % i) for i in range(2)]
        it = 0
        for h in range(6):
            pr, h2 = h // 2, h % 2
            for t0 in range(0, T, TC):
                n = min(TC, T - t0)
                s = it % 2
                it += 1
                S.dma("sp", yt[s][:, 0:n], Yh[h, :, t0:t0 + n], reads=[yhb], writes=[ytb[s]])
                S.dma("sp", bt[s][:, 0:n], BONfm[pr, h2 * 64:(h2 + 1) * 64, t0:t0 + n], reads=[scrb], writes=[btb[s]])
                S.dma("sp", gg[s][:, 0:n], Gfm[pr, h2 * 64:(h2 + 1) * 64, t0:t0 + n], reads=[scrb], writes=[ggb[s]])
                S.op("pe", lambda E, s=s, n=n: E.matmul(pm[s][:, 0:n], lhsT=o64[:], rhs=yt[s][:, 0:n], start=True, stop=True), reads=[o64b, ytb[s]], writes=[pmb[s]])
                S.op("dve", lambda E, s=s, n=n: E.tensor_tensor(out=yc[:, 0:n], in0=yt[s][:, 0:n], in1=pm[s][:, 0:n], op=ALU.subtract), reads=[ytb[s], pmb[s]], writes=[ycb])
                S.op("act", lambda E, n=n: E.activation(out=sq[:, 0:n], in_=yc[:, 0:n], func=AF.Square), reads=[ycb], writes=[sqb])
                S.op("pe", lambda E, s=s, n=n: E.matmul(pv[s][:, 0:n], lhsT=o64[:], rhs=sq[:, 0:n], start=True, stop=True), reads=[o64b, sqb], writes=[pvb[s]])
                S.op("act", lambda E, s=s, n=n: E.activation(out=rsd[:, 0:n], in_=pv[s][:, 0:n], func=AF.Sqrt, bias=epst[:, 0:1], scale=1.0), reads=[pvb[s], epstb], writes=[rsdb])
                S.op("dve", lambda E, n=n: E.reciprocal(out=rsd[:, 0:n], in_=rsd[:, 0:n]), reads=[rsdb], writes=[rsdb])
                S.op("dve", lambda E, n=n: E.tensor_tensor(out=yc[:, 0:n], in0=yc[:, 0:n], in1=rsd[:, 0:n], op=ALU.mult), reads=[ycb, rsdb], writes=[ycb])
                S.op("act", lambda E, n=n, h=h: E.activation(out=yc[:, 0:n], in_=yc[:, 0:n], func=AF.Identity, scale=lnt[:, h, 0:1], bias=lnt[:, h, 1:2]),
                     reads=[ycb, lnb], writes=[ycb])
                S.op("dve", lambda E, s=s, n=n: E.tensor_tensor(out=yc[:, 0:n], in0=yc[:, 0:n], in1=bt[s][:, 0:n], op=ALU.add), reads=[ycb, btb[s]], writes=[ycb])
                S.op("dve", lambda E, s=s, n=n: E.tensor_tensor(out=oo[s][:, 0:n], in0=yc[:, 0:n], in1=gg[s][:, 0:n], op=ALU.mult), reads=[ycb, ggb[s]], writes=[oob[s]])
                S.dma("sp", yT[h * 64:(h + 1) * 64, t0:t0 + n], oo[s][:, 0:n], reads=[oob[s]], writes=[yTb], owner=oob[s], is_output=True)


NMIX = 3813
R_FQ, R_FK, R_FV, R_FF = 0, 384, 768, 1152
R_MQ, R_MK, R_MV, R_MI, R_MF, R_MO = 1155, 1283, 1411, 1667, 1668, 1669
R_RR, R_RK, R_RV, R_RWL, R_RAL, R_RGL = 1925, 2309, 2693, 3077, 3205, 3333
FF_J = D_FF // 4


def mix_cols(j):
    fox0, ml0, rw0 = 0, 4620, 7700
    r = np.arange
    idx = [fox0 + j * 384 + r(384), fox0 + 1536 + j * 384 + r(384), fox0 + 3072 + j * 384 + r(384), fox0 + 4608 + j * 3 + r(3),
           ml0 + j * 128 + r(128), ml0 + 512 + j * 128 + r(128), ml0 + 1024 + j * 256 + r(256), ml0 + 2048 + j + r(1), ml0 + 2052 + j + r(1),
           ml0 + 2056 + j * 256 + r(256),
           rw0 + j * 384 + r(384), rw0 + 1536 + j * 384 + r(384), rw0 + 3072 + j * 384 + r(384), rw0 + 4608 + r(128), rw0 + 4736 + r(128), rw0 + 4864 + r(480)]
    idx = np.concatenate(idx)
    assert idx.shape[0] == NMIX
    return idx


def build_mod(D=D_MODEL, NC=3072, L=DEPTH):
    C = Ctx()
    S = C.S
    KT = D // 128
    cT = C.dram("cT", [128, KT, 2], F32, "ExternalInput")
    aw = C.dram("aw", [L, D, NC], F32, "ExternalInput")
    ab = C.dram("ab", [128, L, NC // 128], F32, "ExternalInput")
    mo = C.dram("modT", [128, L, NC // 128, 2], F32, "ExternalOutput")
    awb, mob = Buf("aw"), Buf("mo")
    sc = C.sb("m_sc", [128, KT, 2], F32)
    scb = Buf("m_sc")
    S.dma("sp", sc[:], cT, writes=[scb])
    S.op("act", lambda E: E.activation(out=sc[:], in_=sc[:], func=AF.Silu), reads=[scb], writes=[scb])
    abt = C.sb("m_ab", [128, L, NC // 128], F32)
    abb = Buf("m_ab")
    S.dma("sp", abt[:], ab, writes=[abb])
    ot = C.sb("m_ot", [128, L, NC // 128, 2], F32)
    otb = Buf("m_ot")
    NCH = 512
    wt = [C.sb("m_w%d" % i, [128, KT, NCH], F32) for i in range(2)]
    wtb = [Buf("m_w%d" % i) for i in range(2)]
    ps = [C.ps("m_ps%d" % i, [128, 512]) for i in range(2)]
    psb = [Buf("m_ps%d" % i) for i in range(2)]
    it = 0
    pi = 0
    for l in range(L):
        for n0 in range(0, NC, NCH):
            s = it % 2
            it += 1
            for half in range(2):
                kh = KT // 2
                hb = _half_bufs.setdefault((id(C), s, half), Buf("m_wh%d_%d" % (s, half)))
                S.dma("sp" if half == 0 else "act", wt[s][:, half * kh:(half + 1) * kh, :],
                      aw[l, half * kh * 128:(half + 1) * kh * 128, n0:n0 + NCH].rearrange("(kt p) n -> p kt n", p=128),
                      reads=[awb], writes=[wtb[s]] if half == 0 else [hb], owner=wtb[s] if half == 0 else hb)
                if half == 1:
                    wtb_extra[(id(C), s)] = hb
            for m in range(NCH // 128):
                p = pi % 2
                pi += 1
                ch = (n0 // 128) + m
                hb = wtb_extra[(id(C), s)]
                for kt in range(KT):
                    S.op("pe", lambda E, kt=kt, m=m, p=p, s=s: E.matmul(ps[p][:, 0:2], lhsT=wt[s][:, kt, m * 128:(m + 1) * 128], rhs=sc[:, kt, :],
                                                                         start=(kt == 0), stop=(kt == KT - 1)),
                         reads=[wtb[s], hb, scb], writes=[psb[p]], pe_chain=(kt > 0))
                S.op("dve", lambda E, p=p, l=l, ch=ch: E.tensor_scalar(out=ot[:, l, ch, :], in0=ps[p][:, 0:2], scalar1=abt[:, l, ch:ch + 1], scalar2=None, op0=ALU.add),
                     reads=[psb[p], abb], writes=[otb])
    S.dma("sp", mo, ot[:], reads=[otb], writes=[mob], owner=otb, is_output=True)
    C.close()
    return C


_half_bufs = {}
wtb_extra = {}


def build_mix(T=SEQ, D=D_MODEL):
    C = Ctx()
    S = C.S
    KT = D // 128
    xT = C.dram("xT", [D, T], F32, "ExternalInput")
    ng = C.dram("ng", [128, KT], F32, "ExternalInput")
    sc = C.dram("sc", [128, KT], F32, "ExternalInput")
    sh = C.dram("sh", [128, KT], F32, "ExternalInput")
    w = C.dram("w", [D, NMIX], F32, "ExternalInput")
    fbias = C.dram("fbias", [1, 3], F32, "ExternalInput")
    fgain = C.dram("fgain", [128, 384], F32, "ExternalInput")
    cw = C.dram("cw", [128, 2, 4], F32, "ExternalInput")
    cb = C.dram("cb", [128, 2], F32, "ExternalInput")
    gb = C.dram("gb", [1, 2], F32, "ExternalInput")
    mgain = C.dram("mgain", [64, 256], F32, "ExternalInput")
    prm = C.dram("prm", [128, 3, 11], F32, "ExternalInput")
    mul = C.dram("mul", [128, 6], F32, "ExternalInput")
    lnp = C.dram("lnp", [64, 6, 2], F32, "ExternalInput")
    w_up = C.dram("w_up", [128, 384], F32, "ExternalInput")
    a_up = C.dram("a_up", [128, 384], F32, "ExternalInput")
    g_up = C.dram("g_up", [480, 384], F32, "ExternalInput")
    y_tm = C.dram("y_tm", [T, 640], BF16, "ExternalOutput")
    yT_rw = C.dram("yT_rw", [384, T], BF16, "ExternalOutput")
    hT = C.dram("hT_s", [D, T], BF16)
    pT = C.dram("pT_s", [NMIX, T], F32)
    xTb, wb, hTb, pTb, ytb, yrb = Buf("xT"), Buf("w"), Buf("hT"), Buf("pT"), Buf("y_tm"), Buf("yT_rw")
    eps_tile(C)
    with C.stage():
        norm_stage(C, xT, xTb, ng, sc, sh, hT, hTb, D, T, BF16, False)
    with C.stage():
        stg = [C.sb("mx_stg%d" % i, [128, 512], F32) for i in range(3)]
        stgb = [Buf("mx_stg%d" % i) for i in range(3)]
        cnt = [0]

        def epi(n0, nsz, t0, tsz, ps, psb):
            s = cnt[0] % 3
            cnt[0] += 1
            if cnt[0] % 2:
                S.op("act", lambda E: E.copy(out=stg[s][0:nsz, 0:tsz], in_=ps[0]), reads=[psb[0]], writes=[stgb[s]])
            else:
                S.op("dve", lambda E: E.tensor_copy(out=stg[s][0:nsz, 0:tsz], in_=ps[0]), reads=[psb[0]], writes=[stgb[s]])
            S.dma("sp", pT[n0:n0 + nsz, t0:t0 + tsz], stg[s][0:nsz, 0:tsz], reads=[stgb[s]], writes=[pTb], owner=stgb[s])
        gemm_fm(C, hT, hTb, [(w, wb)], D, NMIX, T, epi, TB=1024, NCH=512, tag="mx")
    with C.stage():
        fox_stage(C, pT, pTb, R_FQ, R_FK, R_FV, R_FF, 3, fbias, fgain, y_tm, ytb, 0, T)
    with C.stage():
        mlstm_stage(C, pT, pTb, R_MQ, R_MK, R_MV, R_MI, R_MF, R_MO, cw, cb, gb, mgain, y_tm, ytb, 384, T)
    rwkv_stage(C, pT, pTb, R_RR, R_RK, R_RV, R_RWL, R_RAL, R_RGL, prm, mul, lnp, w_up, a_up, g_up, yT_rw, yrb, T)
    C.close()
    return C


def build_resid_gemm(K, N, T, TB, NCH):
    C = Ctx()
    S = C.S
    inT = C.dram("inT", [K, T], BF16, "ExternalInput")
    w = C.dram("w", [K, N], F32, "ExternalInput")
    resT = C.dram("resT", [N, T], F32, "ExternalInput")
    gv = C.dram("gv", [128, N // 128], F32, "ExternalInput")
    outT = C.dram("outT", [N, T], F32, "ExternalOutput")
    inb, wb, rb, ob = Buf("inT"), Buf("w"), Buf("resT"), Buf("outT")
    gt = C.sb("rg_g", [128, N // 128], F32)
    gtb = Buf("rg_g")
    S.dma("sp", gt[:], gv, writes=[gtb])
    rs = [C.sb("rg_r%d" % i, [128, 512], F32) for i in range(3)]
    rsb = [Buf("rg_r%d" % i) for i in range(3)]
    st = [C.sb("rg_s%d" % i, [128, 512], F32) for i in range(3)]
    stb = [Buf("rg_s%d" % i) for i in range(3)]
    cnt = [0]

    def epi(n0, nsz, t0, tsz, ps, psb):
        s = cnt[0] % 3
        cnt[0] += 1
        S.dma("act", rs[s][0:nsz, 0:tsz], resT[n0:n0 + nsz, t0:t0 + tsz], reads=[rb], writes=[rsb[s]])
        ch = n0 // 128
        S.op("dve", lambda E: E.scalar_tensor_tensor(out=st[s][0:nsz, 0:tsz], in0=ps[0], scalar=gt[0:nsz, ch:ch + 1], in1=rs[s][0:nsz, 0:tsz],
                                                     op0=ALU.mult, op1=ALU.add), reads=[psb[0], gtb, rsb[s]], writes=[stb[s]])
        S.dma("sp", outT[n0:n0 + nsz, t0:t0 + tsz], st[s][0:nsz, 0:tsz], reads=[stb[s]], writes=[ob], owner=stb[s], is_output=True)
    gemm_fm(C, inT, inb, [(w, wb)], K, N, T, epi, TB=TB, NCH=NCH, tag="rg")
    C.close()
    return C


def build_ffn_up(T=SEQ, D=D_MODEL, NF=FF_J):
    C = Ctx()
    S = C.S
    KT = D // 128
    xT = C.dram("xT", [D, T], F32, "ExternalInput")
    ng = C.dram("ng", [128, KT], F32, "ExternalInput")
    sc = C.dram("sc", [128, KT], F32, "ExternalInput")
    sh = C.dram("sh", [128, KT], F32, "ExternalInput")
    wg = C.dram("wg", [D, NF], F32, "ExternalInput")
    wu = C.dram("wu", [D, NF], F32, "ExternalInput")
    hid = C.dram("hidT", [NF, T], BF16, "ExternalOutput")
    hT = C.dram("hT_s", [D, T], BF16)
    xTb, wgb, wub, hTb, hidb = Buf("xT"), Buf("wg"), Buf("wu"), Buf("hT"), Buf("hid")
    eps_tile(C)
    with C.stage():
        norm_stage(C, xT, xTb, ng, sc, sh, hT, hTb, D, T, BF16, False)
    with C.stage():
        sg = [C.sb("fu_sg%d" % i, [128, 512], F32) for i in range(2)]
        sgb = [Buf("fu_sg%d" % i) for i in range(2)]
        ho = [C.sb("fu_ho%d" % i, [128, 512], BF16) for i in range(3)]
        hob = [Buf("fu_ho%d" % i) for i in range(3)]
        cnt = [0]

        def epi(n0, nsz, t0, tsz, ps, psb):
            s = cnt[0] % 2
            o = cnt[0] % 3
            cnt[0] += 1
            S.op("act", lambda E: E.activation(out=sg[s][0:nsz, 0:tsz], in_=ps[0], func=AF.Silu), reads=[psb[0]], writes=[sgb[s]])
            S.op("dve", lambda E: E.tensor_tensor(out=ho[o][0:nsz, 0:tsz], in0=sg[s][0:nsz, 0:tsz], in1=ps[1], op=ALU.mult),
                 reads=[sgb[s], psb[1]], writes=[hob[o]])
            S.dma("sp", hid[n0:n0 + nsz, t0:t0 + tsz], ho[o][0:nsz, 0:tsz], reads=[hob[o]], writes=[hidb], owner=hob[o], is_output=True)
        gemm_fm(C, hT, hTb, [(wg, wgb), (wu, wub)], D, NF, T, epi, TB=1024, NCH=256, tag="fu")
    C.close()
    return C


def build_final_norm(T, D=D_MODEL):
    C = Ctx()
    KT = D // 128
    xT = C.dram("xT", [D, T], F32, "ExternalInput")
    ng = C.dram("ng", [128, KT], F32, "ExternalInput")
    oT = C.dram("oT", [D, T], F32, "ExternalOutput")
    eps_tile(C)
    norm_stage(C, xT, Buf("xT"), ng, None, None, oT, Buf("oT"), D, T, F32, True)
    C.close()
    return C


def fmaj(v):
    v = np.asarray(v, np.float32)
    return np.ascontiguousarray(v.reshape(-1, 128).T)


def pack_rwkv_params(mu_r, mu_k, mu_v, mu_wl, mu_al, mu_gl, w0, a0, k_k, k_a, r_k, ln_w, ln_b):
    prm = np.zeros((128, 3, 11), np.float32)
    v = lambda a: np.asarray(a, np.float32).reshape(3, 128).T
    for i, a in enumerate((mu_r, mu_k, mu_v, w0, a0, k_k, k_a, r_k)):
        prm[:, :, i] = v(a)
    mul = np.zeros((128, 6), np.float32)
    mul[:, 0] = mu_wl
    mul[:, 1] = mu_al
    gl = np.zeros(512, np.float32)
    gl[:480] = mu_gl
    mul[:, 2:6] = gl.reshape(4, 128).T
    lnp = np.ascontiguousarray(np.stack([np.asarray(ln_w).reshape(6, 64).T, np.asarray(ln_b).reshape(6, 64).T], axis=-1).astype(np.float32))
    return prm, mul, lnp


_PROGS = {}


def _prog(name, fn):
    if name not in _PROGS:
        _PROGS[name] = fn()
    return _PROGS[name]


def _run(C, in_maps):
    res = run_bass_kernel_spmd(C.nc, in_maps, core_ids=list(range(8)))
    return res.results


def kernel(x, c, ada_w, ada_b, norm1, w_in, fox_f_bias, fox_norm, ml_conv_w, ml_conv_b, ml_i_bias, ml_f_bias, ml_norm,
           rw_mu, rw_w0, rw_w_up, rw_a0, rw_a_up, rw_g_up, rw_k_k, rw_k_a, rw_r_k, rw_ln_w, rw_ln_b, w_out, norm2,
           ffn_gate, ffn_up, ffn_down, final_norm):
    f32 = lambda a: np.asarray(a, np.float32)
    x, c, ada_w, ada_b = f32(x), f32(c), f32(ada_w), f32(ada_b)
    D, T, B, L = D_MODEL, SEQ, BATCH, DEPTH
    KT = D // 128
    Cm = _prog("mod", build_mod)
    cT = np.ascontiguousarray(c.T.reshape(KT, 128, B).transpose(1, 0, 2))
    ims = []
    for core in range(8):
        cols = slice(core * 3072, (core + 1) * 3072)
        ims.append({"cT": cT, "aw": np.ascontiguousarray(ada_w[:, :, cols]),
                    "ab": np.ascontiguousarray(ada_b[:, cols].reshape(L, 24, 128).transpose(2, 0, 1))})
    res = _run(Cm, ims)
    mod = np.zeros((L, B, 6 * D), np.float32)
    for core in range(8):
        m = res[core]["modT"]
        mod[:, :, core * 3072:(core + 1) * 3072] = m.transpose(1, 3, 2, 0).reshape(L, B, 3072)
    xT = [np.ascontiguousarray(x[b].T) for b in range(B)]
    idxs = [mix_cols(j) for j in range(4)]
    for l in range(L):
        sh1, sc1, g1, sh2, sc2, g2 = [mod[l][:, i * D:(i + 1) * D] for i in range(6)]
        Cx = _prog("mix", build_mix)
        ims = []
        for core in range(8):
            b, j = core // 4, core % 4
            mu = f32(rw_mu[l])
            rs = slice(j * 384, (j + 1) * 384)
            prm, mul, lnp = pack_rwkv_params(mu[0:1536][rs], mu[1536:3072][rs], mu[3072:4608][rs], mu[4608:4736], mu[4736:4864], mu[4864:5344],
                                             f32(rw_w0[l])[rs], f32(rw_a0[l])[rs], f32(rw_k_k[l])[rs], f32(rw_k_a[l])[rs],
                                             f32(rw_r_k[l]).reshape(-1)[rs], f32(rw_ln_w[l])[rs], f32(rw_ln_b[l])[rs])
            cwl = f32(ml_conv_w[l])
            cbl = f32(ml_conv_b[l])
            qs, ks_ = slice(j * 128, (j + 1) * 128), slice(512 + j * 128, 512 + (j + 1) * 128)
            ims.append({
                "xT": xT[b], "ng": fmaj(norm1[l]), "sc": fmaj(sc1[b]), "sh": fmaj(sh1[b]),
                "w": np.ascontiguousarray(f32(w_in[l])[:, idxs[j]]),
                "fbias": np.ascontiguousarray(f32(fox_f_bias[l])[None, j * 3:(j + 1) * 3]),
                "fgain": np.ascontiguousarray(np.tile(f32(fox_norm[l])[None, j * 384:(j + 1) * 384], (128, 1))),
                "cw": np.ascontiguousarray(np.stack([cwl[:, qs].T, cwl[:, ks_].T], axis=1)),
                "cb": np.ascontiguousarray(np.stack([cbl[qs], cbl[ks_]], axis=1)),
                "gb": np.array([[f32(ml_i_bias[l])[j], f32(ml_f_bias[l])[j]]], np.float32),
                "mgain": np.ascontiguousarray(np.tile(f32(ml_norm[l])[None, j * 256:(j + 1) * 256], (64, 1))),
                "prm": prm, "mul": mul, "lnp": lnp,
                "w_up": np.ascontiguousarray(f32(rw_w_up[l])[:, rs]), "a_up": np.ascontiguousarray(f32(rw_a_up[l])[:, rs]),
                "g_up": np.ascontiguousarray(f32(rw_g_up[l])[:, rs]),
            })
        res = _run(Cx, ims)
        yT = [np.zeros((D, T), NPBF16) for _ in range(B)]
        for core in range(8):
            b, j = core // 4, core % 4
            ytm = res[core]["y_tm"]
            yT[b][j * 384:(j + 1) * 384] = ytm[:, 0:384].T
            yT[b][1536 + j * 256:1536 + (j + 1) * 256] = ytm[:, 384:640].T
            yT[b][2560 + j * 384:2560 + (j + 1) * 384] = res[core]["yT_rw"]
        del res
        Co = _prog("oproj", lambda: build_resid_gemm(D, 1024, T, 1024, 512))
        ims = []
        for core in range(8):
            b, j = core // 4, core % 4
            cs = slice(j * 1024, (j + 1) * 1024)
            ims.append({"inT": yT[b], "w": np.ascontiguousarray(f32(w_out[l])[:, cs]), "resT": np.ascontiguousarray(xT[b][cs]),
                        "gv": fmaj(g1[b][cs])})
        res = _run(Co, ims)
        xT = [np.ascontiguousarray(np.concatenate([res[b * 4 + j]["outT"] for j in range(4)], axis=0)) for b in range(B)]
        del res, yT
        Cu = _prog("ffup", build_ffn_up)
        ims = []
        for core in range(8):
            b, j = core // 4, core % 4
            fs = slice(j * FF_J, (j + 1) * FF_J)
            ims.append({"xT": xT[b], "ng": fmaj(norm2[l]), "sc": fmaj(sc2[b]), "sh": fmaj(sh2[b]),
                        "wg": np.ascontiguousarray(f32(ffn_gate[l])[:, fs]), "wu": np.ascontiguousarray(f32(ffn_up[l])[:, fs])})
        res = _run(Cu, ims)
        hidT = [np.ascontiguousarray(np.concatenate([res[b * 4 + j]["hidT"] for j in range(4)], axis=0)) for b in range(B)]
        del res
        Cd = _prog("ffdown", lambda: build_resid_gemm(D_FF, 1024, T, 512, 128))
        ims = []
        for core in range(8):
            b, j = core // 4, core % 4
            cs = slice(j * 1024, (j + 1) * 1024)
            ims.append({"inT": hidT[b], "w": np.ascontiguousarray(f32(ffn_down[l])[:, cs]), "resT": np.ascontiguousarray(xT[b][cs]),
                        "gv": fmaj(g2[b][cs])})
        res = _run(Cd, ims)
        xT = [np.ascontiguousarray(np.concatenate([res[b * 4 + j]["outT"] for j in range(4)], axis=0)) for b in range(B)]
        del res, hidT
    Cf = _prog("fnorm", lambda: build_final_norm(1024))
    ims = []
    for core in range(8):
        b, q = core // 4, core % 4
        ims.append({"xT": np.ascontiguousarray(xT[b][:, q * 1024:(q + 1) * 1024]), "ng": fmaj(final_norm)})
    res = _run(Cf, ims)
    out = np.zeros((B, T, D), np.float32)
    for core in range(8):
        b, q = core // 4, core % 4
        out[b, q * 1024:(q + 1) * 1024, :] = res[core]["oT"].T
    return out
```

```python
import math
from contextlib import ExitStack
import numpy as np
import ml_dtypes
import concourse.bass as bass
import concourse.mybir as mybir
from concourse.bass_utils import run_bass_kernel_spmd

F32 = mybir.dt.float32
BF16 = mybir.dt.bfloat16
AF = mybir.ActivationFunctionType
ALU = mybir.AluOpType
AX = mybir.AxisListType
NPBF16 = ml_dtypes.bfloat16

D_MODEL = 4096
SEQ = 4096
BATCH = 2
DEPTH = 2
D_FF = 11008
NORM_EPS = 1e-6
RWKV_LN_EPS = 64e-5


class Buf:
    __slots__ = ("name", "writer", "readers", "dsem", "dcount")

    def __init__(self, name):
        self.name = name
        self.writer = None
        self.readers = {}
        self.dsem = None
        self.dcount = 0


class Sched:
    def __init__(self, nc, es):
        self.nc = nc
        self.es = es
        self.engs = {"pe": nc.tensor, "dve": nc.vector, "act": nc.scalar, "pool": nc.gpsimd, "sp": nc.sync}
        self.sem = {}
        self.cnt = {}
        for k in ("pe", "dve", "act", "pool"):
            self.sem[k] = es.enter_context(nc.semaphore("prog_" + k))
            self.cnt[k] = 0
        self.waited = {}
        self.nsem = 4
        self.out_events = []
        self.n_ins = 0
        self.dma_events = {}
        self.sem_pool = []
        self.stage_bufs = [[]]

    def _wait(self, eng, evs, skip_sem=None, defer_last=False):
        need = {}
        for ev in evs:
            if ev is None:
                continue
            sem, val = ev
            if skip_sem is not None and sem is skip_sem:
                continue
            if need.get(sem, (None, 0))[1] < val:
                need[sem] = (sem, val)
        E = self.engs[eng]
        todo = [(sem, val) for sem, val in need.values() if self.waited.get((eng, sem), 0) < val]
        last = None
        if defer_last and todo:
            last = todo.pop()
        for sem, val in todo:
            E.wait_ge(sem, val)
            self.waited[(eng, sem)] = val
            self.n_ins += 1
        if last is not None:
            self.waited[(eng, last[0])] = last[1]
        return last

    def _deps(self, reads, writes):
        evs = []
        for b in reads:
            evs.append(b.writer)
        for b in writes:
            evs.append(b.writer)
            for s, v in b.readers.items():
                evs.append((s, v))
        return evs

    def _record(self, ev, reads, writes):
        for b in writes:
            b.writer = ev
            b.readers = {}
        for b in reads:
            if b.readers.get(ev[0], 0) < ev[1]:
                b.readers[ev[0]] = ev[1]

    def op(self, eng, fn, reads=(), writes=(), pe_chain=False):
        evs = self._deps(reads, writes)
        last = self._wait(eng, evs, skip_sem=self.sem[eng] if eng == "pe" else None, defer_last=True)
        ins = fn(self.engs[eng])
        if last is not None:
            ins._wait_ge(last[0], last[1])
        self.cnt[eng] += 1
        ins.then_inc(self.sem[eng], 1)
        ev = (self.sem[eng], self.cnt[eng])
        self._record(ev, reads, writes)
        self.n_ins += 1
        return ev

    def dma(self, q, out_ap, in_ap, reads=(), writes=(), owner=None, is_output=False, **kw):
        if owner is None:
            owner = writes[0]
        evs = self._deps(reads, writes)
        if owner.dsem is not None and owner.dcount > 0:
            evs.append((owner.dsem, owner.dcount))
        self._wait(q, evs)
        if owner.dsem is None:
            if self.sem_pool:
                owner.dsem, owner.dcount = self.sem_pool.pop()
            else:
                owner.dsem = self.es.enter_context(self.nc.semaphore("d_%s_%d" % (owner.name, self.nsem)))
                self.nsem += 1
            self.stage_bufs[-1].append(owner)
        ins = self.engs[q].dma_start(out=out_ap, in_=in_ap, **kw)
        owner.dcount += 16
        ins.then_inc(owner.dsem, 16)
        ev = (owner.dsem, owner.dcount)
        self.dma_events[owner.dsem] = ev
        self._record(ev, reads, writes)
        if is_output:
            self.out_events.append(ev)
        self.n_ins += 1
        return ev

    def barrier(self, bufs=()):
        evs = [(self.sem[k], self.cnt[k]) for k in self.sem if self.cnt[k] > 0]
        evs += list(self.dma_events.values())
        for e in ("pe", "dve", "act", "pool", "sp"):
            self._wait(e, evs)

    def finish(self):
        evs = list(self.out_events) + [(self.sem[k], self.cnt[k]) for k in self.sem if self.cnt[k] > 0]
        self._wait("sp", evs)


class _Stage:
    def __init__(self, C):
        self.C = C

    def __enter__(self):
        self.prev = self.C.stack
        self.C.stack = ExitStack()
        self.C.S.stage_bufs.append([])
        return self

    def __exit__(self, *a):
        S = self.C.S
        S.barrier()
        for b in S.stage_bufs.pop():
            S.sem_pool.append((b.dsem, b.dcount))
            b.dsem = None
        self.C.stack.close()
        self.C.stack = self.prev
        return False


def ktiles(K):
    return [(k0, min(128, K - k0)) for k0 in range(0, K, 128)]


class Ctx:
    def __init__(self):
        self.nc = bass.Bass("TRN2", target_bir_lowering=False)
        self.es = ExitStack()
        self.S = Sched(self.nc, self.es)
        self.uid = 0
        self.stack = self.es
        eps_tile(self)

    def sb(self, name, shape, dt):
        self.uid += 1
        return self.stack.enter_context(self.nc.sbuf_tensor("%s_%d" % (name, self.uid), list(shape), dt))

    def ps(self, name, shape, dt=F32):
        self.uid += 1
        return self.stack.enter_context(self.nc.psum_tensor("%s_%d" % (name, self.uid), list(shape), dt))

    def stage(self):
        return _Stage(self)

    def dram(self, name, shape, dt, kind="Internal"):
        return self.nc.dram_tensor(name, list(shape), dt, kind=kind).ap()

    def close(self):
        self.S.finish()
        self.es.close()


def gemm_fm(C, xT, xT_buf, w_list, K, N, T, epi, TB=1024, NCH=512, tag="g"):
    S = C.S
    kts = ktiles(K)
    KT = len(kts)
    TB = min(TB, T)
    nW = len(w_list)
    xt = C.sb(tag + "_x", [128, KT, TB], BF16)
    xt_b = Buf(tag + "_x")
    wts = [[C.sb(tag + "_w%d_%d" % (wi, i), [128, KT, NCH], BF16) for i in range(2)] for wi in range(nW)]
    wt_b = [[Buf(tag + "_w%d_%d" % (wi, i)) for i in range(2)] for wi in range(nW)]
    NPS = 2 if nW > 1 else 4
    pss = [[C.ps(tag + "_ps%d_%d" % (wi, i), [128, 512]) for i in range(NPS)] for wi in range(nW)]
    ps_b = [[Buf(tag + "_ps%d_%d" % (wi, i)) for i in range(NPS)] for wi in range(nW)]
    full_k = (K % 128 == 0)
    it = 0
    pi = 0
    for t0 in range(0, T, TB):
        tsz_b = min(TB, T - t0)
        if full_k:
            S.dma("sp", xt[:, :, 0:tsz_b], xT[:, t0:t0 + tsz_b].rearrange("(kt p) t -> p kt t", p=128),
                  reads=[xT_buf], writes=[xt_b])
        else:
            for ki, (k0, ksz) in enumerate(kts):
                S.dma("sp", xt[0:ksz, ki, 0:tsz_b], xT[k0:k0 + ksz, t0:t0 + tsz_b], reads=[xT_buf], writes=[xt_b])
        for n0 in range(0, N, NCH):
            nsz_c = min(NCH, N - n0)
            slot = it % 2
            it += 1
            for wi, (w, w_buf) in enumerate(w_list):
                if full_k:
                    S.dma("pool", wts[wi][slot][:, :, 0:nsz_c],
                          w[:, n0:n0 + nsz_c].rearrange("(kt p) n -> p kt n", p=128),
                          reads=[w_buf], writes=[wt_b[wi][slot]])
                else:
                    for ki, (k0, ksz) in enumerate(kts):
                        S.dma("pool", wts[wi][slot][0:ksz, ki, 0:nsz_c], w[k0:k0 + ksz, n0:n0 + nsz_c],
                              reads=[w_buf], writes=[wt_b[wi][slot]])
            for m0 in range(0, nsz_c, 128):
                msz = min(128, nsz_c - m0)
                for tt in range(0, tsz_b, 512):
                    tsz = min(512, tsz_b - tt)
                    p = pi % NPS
                    pi += 1
                    for wi in range(nW):
                        for ki, (k0, ksz) in enumerate(kts):
                            S.op("pe", lambda E, wi=wi, ki=ki, ksz=ksz: E.matmul(
                                pss[wi][p][0:msz, 0:tsz], lhsT=wts[wi][slot][0:ksz, ki, m0:m0 + msz],
                                rhs=xt[0:ksz, ki, tt:tt + tsz], start=(ki == 0), stop=(ki == KT - 1)),
                                reads=[wt_b[wi][slot], xt_b], writes=[ps_b[wi][p]], pe_chain=(ki > 0))
                    epi(n0 + m0, msz, t0 + tt, tsz, [pss[wi][p][0:msz, 0:tsz] for wi in range(nW)],
                        [ps_b[wi][p] for wi in range(nW)])


def norm_stage(C, xT, xT_buf, g_ap, sc_ap, sh_ap, out_ap, out_buf, D, T, out_dt, is_output, tag="n"):
    S = C.S
    KT = D // 128
    gt = C.sb(tag + "_g", [128, KT], F32)
    A = C.sb(tag + "_A", [128, KT], F32)
    gb = Buf(tag + "_g")
    Ab = Buf(tag + "_A")
    S.dma("sp", gt[:], g_ap, writes=[gb])
    if sc_ap is not None:
        sct = C.sb(tag + "_sc", [128, KT], F32)
        sht = C.sb(tag + "_sh", [128, KT], F32)
        scb = Buf(tag + "_sc")
        shb = Buf(tag + "_sh")
        S.dma("sp", sct[:], sc_ap, writes=[scb])
        S.dma("sp", sht[:], sh_ap, writes=[shb])
        S.op("dve", lambda E: E.scalar_tensor_tensor(out=A[:], in0=sct[:], scalar=1.0, in1=gt[:], op0=ALU.add, op1=ALU.mult),
             reads=[scb, gb], writes=[Ab])
    else:
        S.op("dve", lambda E: E.tensor_copy(out=A[:], in_=gt[:]), reads=[gb], writes=[Ab])
    ones = C.sb(tag + "_ones", [128, 128], F32)
    onb = Buf(tag + "_ones")
    S.op("pool", lambda E: E.memset(ones[:], 1.0), writes=[onb])
    TBN = 256
    xs = [C.sb(tag + "_xs%d" % i, [128, KT, TBN], F32) for i in range(2)]
    xb = [Buf(tag + "_xs%d" % i) for i in range(2)]
    sq = [C.sb(tag + "_sq%d" % i, [128, TBN], F32) for i in range(2)]
    sqb = [Buf(tag + "_sq%d" % i) for i in range(2)]
    ho = [C.sb(tag + "_ho%d" % i, [128, KT, TBN], out_dt) for i in range(2)]
    hob = [Buf(tag + "_ho%d" % i) for i in range(2)]
    pss = C.ps(tag + "_ps", [128, TBN])
    psb = Buf(tag + "_ps")
    rs = C.sb(tag + "_rs", [128, TBN], F32)
    rsb = Buf(tag + "_rs")
    tmp = C.sb(tag + "_tmp", [128, TBN], F32)
    tmpb = Buf(tag + "_tmp")
    for bi, t0 in enumerate(range(0, T, TBN)):
        tsz = min(TBN, T - t0)
        s = bi % 2
        S.dma("sp", xs[s][:, :, 0:tsz], xT[:, t0:t0 + tsz].rearrange("(kt p) t -> p kt t", p=128),
              reads=[xT_buf], writes=[xb[s]])
        for kt in range(KT):
            q = kt % 2
            S.op("act", lambda E, kt=kt, q=q: E.activation(out=sq[q][:, 0:tsz], in_=xs[s][:, kt, 0:tsz], func=AF.Square),
                 reads=[xb[s]], writes=[sqb[q]])
            S.op("pe", lambda E, kt=kt, q=q: E.matmul(pss[:, 0:tsz], lhsT=ones[:], rhs=sq[q][:, 0:tsz],
                                                      start=(kt == 0), stop=(kt == KT - 1)),
                 reads=[onb, sqb[q]], writes=[psb], pe_chain=(kt > 0))
        S.op("act", lambda E: E.activation(out=rs[:, 0:tsz], in_=pss[:, 0:tsz], func=AF.Sqrt, scale=1.0 / D, bias=eps_tile(C)[:, 0:1]),
             reads=[psb, eps_buf(C)], writes=[rsb])
        S.op("dve", lambda E: E.reciprocal(out=rs[:, 0:tsz], in_=rs[:, 0:tsz]), reads=[rsb], writes=[rsb])
        for kt in range(KT):
            S.op("dve", lambda E, kt=kt: E.tensor_tensor(out=tmp[:, 0:tsz], in0=xs[s][:, kt, 0:tsz], in1=rs[:, 0:tsz], op=ALU.mult),
                 reads=[xb[s], rsb], writes=[tmpb])
            if sc_ap is not None:
                S.op("act", lambda E, kt=kt: E.activation(out=ho[s][:, kt, 0:tsz], in_=tmp[:, 0:tsz], func=AF.Identity,
                                                          scale=A[:, kt:kt + 1], bias=sht[:, kt:kt + 1]),
                     reads=[tmpb, Ab, shb], writes=[hob[s]])
            else:
                S.op("act", lambda E, kt=kt: E.activation(out=ho[s][:, kt, 0:tsz], in_=tmp[:, 0:tsz], func=AF.Identity,
                                                          scale=A[:, kt:kt + 1]),
                     reads=[tmpb, Ab], writes=[hob[s]])
        S.dma("sp", out_ap[:, t0:t0 + tsz].rearrange("(kt p) t -> p kt t", p=128), ho[s][:, :, 0:tsz],
              reads=[hob[s]], writes=[out_buf], owner=hob[s], is_output=is_output)


def eps_tile(C):
    if not hasattr(C, "_eps"):
        C._eps = C.es.enter_context(C.nc.sbuf_tensor("eps_const", [128, 1], F32))
        C._epsb = Buf("eps")
        C.S.op("pool", lambda E: E.memset(C._eps[:], NORM_EPS), writes=[C._epsb])
    return C._eps


def eps_buf(C):
    eps_tile(C)
    return C._epsb


def make_ident(C, tag):
    S = C.S
    ident = C.sb(tag + "_id", [128, 128], F32)
    idb = Buf(tag + "_id")
    S.op("pool", lambda E: E.memset(ident[:], 1.0), writes=[idb])
    S.op("pool", lambda E: E.affine_select(out=ident[:], in_=ident[:], pattern=[[1, 128]], compare_op=ALU.is_equal,
                                           fill=0.0, base=0, channel_multiplier=-1), reads=[idb], writes=[idb])
    return ident, idb


def fox_stage(C, pT, pTb, r_q, r_k, r_v, r_f, nh, fbias, gain_rep, y, yb, ycol0, T, fm_out=False):
    S = C.S
    scale = 128 ** -0.5
    NQ = (T + 511) // 512
    NK = T // 128
    ident, idb = make_ident(C, "fx")
    ones_row = C.sb("fx_ones", [1, max(T, 128)], F32)
    onb = Buf("fx_ones")
    S.op("pool", lambda E: E.memset(ones_row[:], 1.0), writes=[onb])
    negone = C.sb("fx_neg1", [1, 1], F32)
    n1b = Buf("fx_neg1")
    S.op("pool", lambda E: E.memset(negone[:], -1.0), writes=[n1b])
    fb = C.sb("fx_fb", [1, nh], F32)
    fbb = Buf("fx_fb")
    S.dma("sp", fb[:], fbias, writes=[fbb])
    S.op("dve", lambda E: E.tensor_scalar(out=fb[:], in0=fb[:], scalar1=-1.0, scalar2=None, op0=ALU.mult), reads=[fbb], writes=[fbb])
    gain = C.sb("fx_gain", [128, nh * 128], F32)
    gnb = Buf("fx_gain")
    S.dma("sp", gain[:], gain_rep, writes=[gnb])
    epsb = eps_buf(C)
    eps = eps_tile(C)
    QT = C.sb("fx_q", [128, T], BF16)
    KTt = C.sb("fx_k", [128, T], BF16)
    VT = C.sb("fx_v", [128, T], F32)
    Vext = C.sb("fx_ve", [128, NK, 130], BF16)
    frow = C.sb("fx_f", [1, T], F32)
    Frow = C.sb("fx_F", [1, T], F32)
    Fbc = C.sb("fx_Fbc", [128, T], F32)
    negF = C.sb("fx_nF", [128, NK], F32)
    QTb, KTb, VTb, Vxb, frb, Frb, Fbb, nFb = [Buf("fx_b%d" % i) for i in range(8)]
    E1 = [C.sb("fx_e%d" % i, [128, 512], F32) for i in range(2)]
    E1b = [Buf("fx_e%d" % i) for i in range(2)]
    PT = [C.sb("fx_p%d" % i, [128, 512], BF16) for i in range(2)]
    PTb = [Buf("fx_p%d" % i) for i in range(2)]
    ps_s = [C.ps("fx_ps%d" % i, [128, 512]) for i in range(2)]
    ps_sb = [Buf("fx_ps%d" % i) for i in range(2)]
    ps_o = [C.ps("fx_po%d" % i, [128, 512]) for i in range(4)]
    ps_ob = [Buf("fx_po%d" % i) for i in range(4)]
    ps_m = C.ps("fx_pm", [128, 512])
    psmb = Buf("fx_pm")
    yst = [C.sb("fx_ys%d" % i, [128, 128], F32) for i in range(2)]
    ystb = [Buf("fx_ys%d" % i) for i in range(2)]
    junk = C.sb("fx_junk", [128, 128], F32)
    junkb = Buf("fx_junk")
    sml = [C.sb("fx_sm%d" % i, [128, 4], F32) for i in range(2)]
    smlb = [Buf("fx_sm%d" % i) for i in range(2)]
    yo = [C.sb("fx_yo%d" % i, [128, 128], BF16) for i in range(2)]
    yob = [Buf("fx_yo%d" % i) for i in range(2)]
    blk = 0
    fin = 0
    neg_fill = C.nc.gpsimd.to_reg(-30000.0)
    if fm_out:
        idbf = C.sb("fx_idbf", [128, 128], BF16)
        idbfb = Buf("fx_idbf")
        S.op("dve", lambda E: E.tensor_copy(out=idbf[:], in_=ident[:]), reads=[idb], writes=[idbfb])
        ps_t = C.ps("fx_pt", [128, 128], BF16)
        ps_tb = Buf("fx_pt")
        yt2 = [C.sb("fx_yt%d" % i, [128, 128], BF16) for i in range(2)]
        yt2b = [Buf("fx_yt%d" % i) for i in range(2)]
    for h in range(nh):
        S.dma("pool", QT[:], pT[r_q + h * 128:r_q + (h + 1) * 128, :], reads=[pTb], writes=[QTb])
        S.dma("pool", KTt[:], pT[r_k + h * 128:r_k + (h + 1) * 128, :], reads=[pTb], writes=[KTb])
        S.dma("sp", VT[:], pT[r_v + h * 128:r_v + (h + 1) * 128, :], reads=[pTb], writes=[VTb])
        S.dma("sp", frow[:], pT[r_f + h:r_f + h + 1, :], reads=[pTb], writes=[frb])
        for kt in range(NK):
            S.op("pe", lambda E, kt=kt: E.transpose(ps_m[:, 0:128], VT[:, kt * 128:(kt + 1) * 128], ident[:]),
                 reads=[VTb, idb], writes=[psmb])
            S.op("act", lambda E, kt=kt: E.copy(out=Vext[:, kt, 0:128], in_=ps_m[:, 0:128]), reads=[psmb], writes=[Vxb])
        S.op("pool", lambda E: E.memset(Vext[:, :, 128:129], 1.0), writes=[Vxb])
        S.op("act", lambda E, h=h: E.activation(out=frow[:], in_=frow[:], func=AF.Exp, scale=-1.0, bias=fb[0:1, h:h + 1]),
             reads=[frb, fbb], writes=[frb])
        S.op("act", lambda E: E.activation(out=frow[:], in_=frow[:], func=AF.Ln, scale=1.0, bias=ones_row[0:1, 0:1]),
             reads=[frb, onb], writes=[frb])
        S.op("dve", lambda E: E.tensor_tensor_scan(out=Frow[:], data0=ones_row[0:1, 0:T], data1=frow[:], initial=0.0,
                                                  op0=ALU.mult, op1=ALU.subtract), reads=[frb, onb], writes=[Frb])
        for c in range(NQ):
            csz = min(512, T - c * 512)
            S.op("pe", lambda E, c=c, csz=csz: E.matmul(ps_m[:, 0:csz], lhsT=ones_row[0:1, 0:128], rhs=Frow[0:1, c * 512:c * 512 + csz],
                                                        start=True, stop=True), reads=[onb, Frb], writes=[psmb])
            S.op("act", lambda E, c=c, csz=csz: E.copy(out=Fbc[:, c * 512:c * 512 + csz], in_=ps_m[:, 0:csz]), reads=[psmb], writes=[Fbb])
        for kt in range(NK):
            S.op("pe", lambda E, kt=kt: E.matmul(ps_m[:, kt:kt + 1], lhsT=Frow[0:1, kt * 128:(kt + 1) * 128], rhs=negone[0:1, 0:1],
                                                 start=True, stop=True), reads=[Frb, n1b], writes=[psmb], pe_chain=(kt > 0))
        S.op("dve", lambda E: E.tensor_copy(out=negF[:, 0:NK], in_=ps_m[:, 0:NK]), reads=[psmb], writes=[nFb])
        for qt in range(NQ):
            q0 = qt * 512
            qsz = min(512, T - q0)
            nsub = qsz // 128
            nkt = (q0 + qsz) // 128
            for kt in range(nkt):
                c = kt - 4 * qt
                qs = 128 * c if c >= 0 else 0
                w = qsz - qs
                s = blk % 2
                blk += 1
                S.op("pe", lambda E, kt=kt, qs=qs, w=w, s=s: E.matmul(ps_s[s][:, 0:w], lhsT=KTt[:, kt * 128:(kt + 1) * 128],
                                                                     rhs=QT[:, q0 + qs:q0 + qs + w], start=True, stop=True),
                     reads=[KTb, QTb], writes=[ps_sb[s]])
                S.op("dve", lambda E, qs=qs, w=w, s=s: E.scalar_tensor_tensor(out=E1[s][:, 0:w], in0=ps_s[s][:, 0:w], scalar=scale,
                                                                            in1=Fbc[:, q0 + qs:q0 + qs + w], op0=ALU.mult, op1=ALU.add),
                     reads=[ps_sb[s], Fbb], writes=[E1b[s]])
                if c >= 0:
                    S.op("pool", lambda E, s=s: E.affine_select(out=E1[s][:, 0:128], in_=E1[s][:, 0:128], pattern=[[1, 128]],
                                                               compare_op=ALU.is_ge, fill=neg_fill, base=0, channel_multiplier=-1),
                         reads=[E1b[s]], writes=[E1b[s]])
                S.op("act", lambda E, kt=kt, w=w, s=s: E.activation(out=PT[s][:, 0:w], in_=E1[s][:, 0:w], func=AF.Exp,
                                                                  bias=negF[:, kt:kt + 1], scale=1.0),
                     reads=[E1b[s], nFb], writes=[PTb[s]])
                for qi in range(qs // 128, nsub):
                    last = (kt == 4 * qt + qi)
                    off = qi * 128 - qs
                    S.op("pe", lambda E, kt=kt, qi=qi, off=off, s=s, last=last: E.matmul(
                        ps_o[qi][:, 0:129], lhsT=PT[s][:, off:off + 128], rhs=Vext[:, kt, 0:129], start=(kt == 0), stop=last),
                        reads=[PTb[s], Vxb], writes=[ps_ob[qi]], pe_chain=(kt > 0))
                    if last:
                        f = fin % 2
                        fin += 1
                        sm = sml[f]
                        S.op("dve", lambda E, qi=qi, sm=sm: E.reciprocal(out=sm[:, 0:1], in_=ps_o[qi][:, 128:129]),
                             reads=[ps_ob[qi]], writes=[smlb[f]])
                        S.op("dve", lambda E, qi=qi, sm=sm, f=f: E.tensor_scalar(out=yst[f][:], in0=ps_o[qi][:, 0:128], scalar1=sm[:, 0:1],
                                                                                 scalar2=None, op0=ALU.mult),
                             reads=[ps_ob[qi], smlb[f]], writes=[ystb[f]])
                        S.op("act", lambda E, f=f: E.activation(out=junk[:], in_=yst[f][:], func=AF.Square), reads=[ystb[f]], writes=[junkb])
                        S.op("dve", lambda E, sm=sm: E.reduce_sum(out=sm[:, 1:2], in_=junk[:], axis=AX.X), reads=[junkb], writes=[smlb[f]])
                        S.op("act", lambda E, sm=sm: E.activation(out=sm[:, 2:3], in_=sm[:, 1:2], func=AF.Sqrt, scale=1.0 / 128, bias=eps[:, 0:1]),
                             reads=[smlb[f], epsb], writes=[smlb[f]])
                        S.op("dve", lambda E, sm=sm: E.reciprocal(out=sm[:, 3:4], in_=sm[:, 2:3]), reads=[smlb[f]], writes=[smlb[f]])
                        S.op("dve", lambda E, sm=sm, f=f, h=h: E.scalar_tensor_tensor(out=yo[f][:], in0=yst[f][:], scalar=sm[:, 3:4],
                                                                                     in1=gain[:, h * 128:(h + 1) * 128], op0=ALU.mult, op1=ALU.mult),
                             reads=[ystb[f], smlb[f], gnb], writes=[yob[f]])
                        tok0 = q0 + qi * 128
                        if fm_out:
                            S.op("pe", lambda E, f=f: E.transpose(ps_t[:], yo[f][:], idbf[:]), reads=[yob[f], idbfb], writes=[ps_tb])
                            S.op("act", lambda E, f=f: E.copy(out=yt2[f][:], in_=ps_t[:]), reads=[ps_tb], writes=[yt2b[f]])
                            S.dma("sp", y[ycol0 + h * 128:ycol0 + (h + 1) * 128, tok0:tok0 + 128], yt2[f][:], reads=[yt2b[f]], writes=[yb],
                                  owner=yt2b[f])
                        else:
                            S.dma("sp", y[tok0:tok0 + 128, ycol0 + h * 128:ycol0 + (h + 1) * 128], yo[f][:], reads=[yob[f]], writes=[yb],
                                  owner=yob[f], is_output=True)


def mlstm_stage(C, pT, pTb, r_q, r_k, r_v, r_i, r_f, r_o, cw, cb, gbias, gain_rep, y, yb, ycol0, T, fm_out=False):
    TSEG = min(1024, T)
    state = C.sb("ml_state", [128, 257], F32)
    stb = Buf("ml_state")
    C.S.op("pool", lambda E: E.memset(state[:], 0.0), writes=[stb])
    for t0 in range(0, T, TSEG):
        with C.stage():
            _mlstm_seg(C, pT, pTb, r_q, r_k, r_v, r_i, r_f, r_o, cw, cb, gbias, gain_rep, y, yb, ycol0, min(TSEG, T - t0), t0, state, stb, fm_out)


def _mlstm_seg(C, pT, pTb, r_q, r_k, r_v, r_i, r_f, r_o, cw, cb, gbias, gain_rep, y, yb, ycol0, T, t0, state, stb, fm_out=False):
    S = C.S
    L = 64
    NC = T // L
    DK, DV = 128, 256
    ident, idb = make_ident(C, "ml")
    ones_row = C.sb("ml_ones", [1, max(T, 128)], F32)
    onb = Buf("ml_ones")
    S.op("pool", lambda E: E.memset(ones_row[:], 1.0), writes=[onb])
    rmask = C.sb("ml_rmask", [1, T], F32)
    rmb = Buf("ml_rmask")
    S.op("pool", lambda E: E.memset(rmask[:], 1.0), writes=[rmb])
    S.op("pool", lambda E: E.memset(rmask[:].rearrange("p (c l) -> p c l", l=L)[:, :, 0:1], 0.0), reads=[rmb], writes=[rmb])
    cmask = C.sb("ml_cmask", [L, L], F32)
    cmb = Buf("ml_cmask")
    S.op("pool", lambda E: E.memset(cmask[:], 1.0), writes=[cmb])
    S.op("pool", lambda E: E.affine_select(out=cmask[:], in_=cmask[:], pattern=[[1, L]], compare_op=ALU.is_ge, fill=0.0,
                                           base=0, channel_multiplier=-1), reads=[cmb], writes=[cmb])
    cwt = C.sb("ml_cw", [128, 2, 4], F32)
    cbt = C.sb("ml_cb", [128, 2], F32)
    gbt = C.sb("ml_gb", [1, 2], F32)
    gain = C.sb("ml_gain", [L, DV], F32)
    prb = Buf("ml_params")
    S.dma("sp", cwt[:], cw, writes=[prb])
    S.dma("sp", cbt[:], cb, writes=[prb])
    S.dma("sp", gbt[:], gbias, writes=[prb])
    S.dma("sp", gain[:], gain_rep, writes=[prb])
    S.op("dve", lambda E: E.tensor_scalar(out=gbt[:], in0=gbt[:], scalar1=1.0 / 15.0, scalar2=None, op0=ALU.mult), reads=[prb], writes=[prb])
    eps = eps_tile(C)
    epsb = eps_buf(C)
    ps_m = [C.ps("ml_pm%d" % i, [128, 512]) for i in range(2)]
    psmb = [Buf("ml_pm%d" % i) for i in range(2)]
    irow = C.sb("ml_i", [1, T], F32)
    frow = C.sb("ml_f", [1, T], F32)
    brow = C.sb("ml_b", [1, T], F32)
    irb, frb, brb = Buf("ml_i"), Buf("ml_f"), Buf("ml_b")
    S.dma("sp", irow[:], pT[r_i:r_i + 1, t0:t0 + T], reads=[pTb], writes=[irb])
    S.dma("sp", frow[:], pT[r_f:r_f + 1, t0:t0 + T], reads=[pTb], writes=[frb])
    S.op("act", lambda E: E.activation(out=irow[:], in_=irow[:], func=AF.Tanh, scale=1.0 / 15.0, bias=gbt[0:1, 0:1]), reads=[irb, prb], writes=[irb])
    S.op("act", lambda E: E.activation(out=frow[:], in_=frow[:], func=AF.Tanh, scale=1.0 / 15.0, bias=gbt[0:1, 1:2]), reads=[frb, prb], writes=[frb])
    S.op("act", lambda E: E.activation(out=frow[:], in_=frow[:], func=AF.Exp, scale=-15.0), reads=[frb], writes=[frb])
    S.op("act", lambda E: E.activation(out=frow[:], in_=frow[:], func=AF.Ln, scale=1.0, bias=ones_row[0:1, 0:1]), reads=[frb, onb], writes=[frb])
    S.op("dve", lambda E: E.tensor_tensor_scan(out=brow[:], data0=rmask[:], data1=frow[:], initial=0.0, op0=ALU.mult, op1=ALU.subtract),
         reads=[rmb, frb], writes=[brb])
    S.op("dve", lambda E: E.scalar_tensor_tensor(out=irow[:], in0=irow[:], scalar=15.0, in1=brow[:], op0=ALU.mult, op1=ALU.subtract),
         reads=[irb, brb], writes=[irb])
    S.op("act", lambda E: E.activation(out=irow[:], in_=irow[:], func=AF.Exp), reads=[irb], writes=[irb])
    S.op("act", lambda E: E.activation(out=frow[:], in_=brow[:], func=AF.Exp), reads=[brb], writes=[frb])
    egb_t = C.sb("ml_eg", [128, NC], F32)
    egb = Buf("ml_eg")
    S.op("pe", lambda E: E.matmul(ps_m[0][:, 0:NC], lhsT=ones_row[0:1, 0:128],
                                  rhs=frow[:].rearrange("p (c l) -> p c l", l=L)[:, :, L - 1], start=True, stop=True),
         reads=[onb, frb], writes=[psmb[0]])
    S.op("dve", lambda E: E.tensor_copy(out=egb_t[:], in_=ps_m[0][:, 0:NC]), reads=[psmb[0]], writes=[egb])
    xp = C.sb("ml_xp", [128, T + 3], F32)
    xpb = Buf("ml_xp")
    acc = C.sb("ml_acc", [128, T], F32)
    accb = Buf("ml_acc")
    qT = C.sb("ml_qT", [128, T], F32)
    kT = C.sb("ml_kT", [128, T], F32)
    qTb, kTb = Buf("ml_qT"), Buf("ml_kT")
    for which, (r0, dst, dstb, srow, srb) in enumerate(((r_q, qT, qTb, frow, frb), (r_k, kT, kTb, irow, irb))):
        if t0 == 0:
            S.op("pool", lambda E: E.memset(xp[:, 0:3], 0.0), reads=[], writes=[xpb])
            S.dma("sp", xp[:, 3:T + 3], pT[r0:r0 + 128, 0:T], reads=[pTb], writes=[xpb])
        else:
            S.dma("sp", xp[:, 0:T + 3], pT[r0:r0 + 128, t0 - 3:t0 + T], reads=[pTb], writes=[xpb])
        S.op("dve", lambda E, which=which: E.tensor_scalar(out=acc[:], in0=xp[:, 3:T + 3], scalar1=cwt[:, which, 3:4], scalar2=cbt[:, which:which + 1],
                                                          op0=ALU.mult, op1=ALU.add), reads=[xpb, prb], writes=[accb])
        for i in range(3):
            S.op("dve", lambda E, which=which, i=i: E.scalar_tensor_tensor(out=acc[:], in0=xp[:, i:T + i], scalar=cwt[:, which, i:i + 1], in1=acc[:],
                                                                          op0=ALU.mult, op1=ALU.add), reads=[xpb, prb, accb], writes=[accb])
        S.op("act", lambda E: E.activation(out=acc[:], in_=acc[:], func=AF.Silu), reads=[accb], writes=[accb])
        sc = (DK ** -0.5) if which == 0 else 1.0
        for c0 in range(0, T, 512):
            csz = min(512, T - c0)
            pm = (c0 // 512) % 2
            S.op("pe", lambda E, c0=c0, csz=csz, pm=pm, srow=srow: E.matmul(ps_m[pm][:, 0:csz], lhsT=ones_row[0:1, 0:128], rhs=srow[0:1, c0:c0 + csz],
                                                                           start=True, stop=True), reads=[onb, srb], writes=[psmb[pm]])
            S.op("dve", lambda E, c0=c0, csz=csz, pm=pm, dst=dst, sc=sc: E.scalar_tensor_tensor(out=dst[:, c0:c0 + csz], in0=acc[:, c0:c0 + csz], scalar=sc,
                                                                                                in1=ps_m[pm][:, 0:csz], op0=ALU.mult, op1=ALU.mult),
                 reads=[accb, psmb[pm]], writes=[dstb])
    ktok = C.sb("ml_ktok", [L, NC, DK], F32)
    vext = C.sb("ml_vext", [L, NC, DV + 1], F32)
    ogt = C.sb("ml_og", [L, NC, DV], F32)
    ktokb, vextb, ogb = Buf("ml_ktok"), Buf("ml_vext"), Buf("ml_og")
    S.op("pool", lambda E: E.memset(vext[:, :, DV:DV + 1], 1.0), writes=[vextb])
    tsrc = C.sb("ml_tsrc", [128, T], F32)
    tsb = Buf("ml_tsrc")
    tcnt = 0
    for (r0, kind) in ((r_v, "v0"), (r_v + 128, "v1"), (r_o, "o0"), (r_o + 128, "o1"), (None, "k")):
        if kind == "k":
            src, srcb = kT, kTb
        else:
            S.dma("sp", tsrc[:], pT[r0:r0 + 128, t0:t0 + T], reads=[pTb], writes=[tsb])
            if kind[0] == "o":
                S.op("act", lambda E: E.activation(out=tsrc[:], in_=tsrc[:], func=AF.Sigmoid), reads=[tsb], writes=[tsb])
            src, srcb = tsrc, tsb
        for c in range(NC):
            pm = tcnt % 2
            tcnt += 1
            S.op("pe", lambda E, c=c, pm=pm, src=src: E.transpose(ps_m[pm][0:L, 0:128], src[:, c * L:(c + 1) * L], ident[:]),
                 reads=[srcb, idb], writes=[psmb[pm]])
            if kind == "k":
                dst, dstb_ = ktok[:, c, :], ktokb
            elif kind[0] == "v":
                off = 128 * int(kind[1])
                dst, dstb_ = vext[:, c, off:off + 128], vextb
            else:
                off = 128 * int(kind[1])
                dst, dstb_ = ogt[:, c, off:off + 128], ogb
            eng = "act" if (tcnt % 2) else "dve"
            if eng == "act":
                S.op("act", lambda E, pm=pm, dst=dst: E.copy(out=dst, in_=ps_m[pm][0:L, 0:128]), reads=[psmb[pm]], writes=[dstb_])
            else:
                S.op("dve", lambda E, pm=pm, dst=dst: E.tensor_copy(out=dst, in_=ps_m[pm][0:L, 0:128]), reads=[psmb[pm]], writes=[dstb_])
    ps_s = [C.ps("ml_pss%d" % i, [L, 512]) for i in range(2)]
    ps_sb = [Buf("ml_pss%d" % i) for i in range(2)]
    ps_o = [C.ps("ml_pso%d" % i, [L, 512]) for i in range(2)]
    ps_ob = [Buf("ml_pso%d" % i) for i in range(2)]
    ps_u = C.ps("ml_psu", [128, 512])
    ps_ub = Buf("ml_psu")
    PTt = [C.sb("ml_PT%d" % i, [L, L], F32) for i in range(2)]
    PTb = [Buf("ml_PT%d" % i) for i in range(2)]
    hh = [C.sb("ml_hh%d" % i, [L, DV], F32) for i in range(2)]
    hhb = [Buf("ml_hh%d" % i) for i in range(2)]
    junk = C.sb("ml_junk", [L, DV], F32)
    junkb = Buf("ml_junk")
    sml = [C.sb("ml_sm%d" % i, [L, 4], F32) for i in range(2)]
    smlb = [Buf("ml_sm%d" % i) for i in range(2)]
    yo = [C.sb("ml_yo%d" % i, [L, DV], BF16) for i in range(2)]
    yob = [Buf("ml_yo%d" % i) for i in range(2)]
    if fm_out:
        idbf = C.sb("ml_idbf", [128, 128], BF16)
        idbfb = Buf("ml_idbf")
        S.op("dve", lambda E: E.tensor_copy(out=idbf[:], in_=ident[:]), reads=[idb], writes=[idbfb])
        ps_t = C.ps("ml_pt", [128, 2, L], BF16)
        ps_tb = Buf("ml_pt")
        yt2 = [C.sb("ml_yt%d" % i, [128, 2, L], BF16) for i in range(2)]
        yt2b = [Buf("ml_yt%d" % i) for i in range(2)]
    for c in range(NC):
        s = c % 2
        cs = slice(c * L, (c + 1) * L)
        S.op("pe", lambda E, cs=cs, s=s: E.matmul(ps_s[s][:, 0:L], lhsT=kT[:, cs], rhs=qT[:, cs], start=True, stop=True),
             reads=[kTb, qTb], writes=[ps_sb[s]])
        S.op("dve", lambda E, s=s: E.tensor_tensor(out=PTt[s][:], in0=ps_s[s][:, 0:L], in1=cmask[:], op=ALU.mult),
             reads=[ps_sb[s], cmb], writes=[PTb[s]])
        S.op("pe", lambda E, c=c, s=s: E.matmul(ps_o[s][:, 0:DV + 1], lhsT=PTt[s][:], rhs=vext[:, c, :], start=True, stop=False),
             reads=[PTb[s], vextb], writes=[ps_ob[s]])
        S.op("pe", lambda E, cs=cs, s=s: E.matmul(ps_o[s][:, 0:DV + 1], lhsT=qT[:, cs], rhs=state[:], start=False, stop=True),
             reads=[qTb, stb], writes=[ps_ob[s]], pe_chain=True)
        S.op("pe", lambda E, c=c: E.matmul(ps_u[:, 0:DV + 1], lhsT=ktok[:, c, :], rhs=vext[:, c, :], start=True, stop=True),
             reads=[ktokb, vextb], writes=[ps_ub])
        S.op("dve", lambda E, c=c: E.tensor_scalar(out=state[:], in0=state[:], scalar1=egb_t[:, c:c + 1], scalar2=None, op0=ALU.mult),
             reads=[stb, egb], writes=[stb])
        S.op("dve", lambda E, c=c: E.scalar_tensor_tensor(out=state[:], in0=ps_u[:, 0:DV + 1], scalar=egb_t[:, c:c + 1], in1=state[:],
                                                          op0=ALU.mult, op1=ALU.add), reads=[ps_ub, egb, stb], writes=[stb])
        sm = sml[s]
        S.op("act", lambda E, s=s, sm=sm: E.activation(out=sm[:, 0:1], in_=ps_o[s][:, DV:DV + 1], func=AF.Abs),
             reads=[ps_ob[s]], writes=[smlb[s]])
        S.op("dve", lambda E, sm=sm: E.tensor_scalar(out=sm[:, 0:1], in0=sm[:, 0:1], scalar1=1.0, scalar2=None, op0=ALU.max),
             reads=[smlb[s]], writes=[smlb[s]])
        S.op("dve", lambda E, sm=sm: E.reciprocal(out=sm[:, 0:1], in_=sm[:, 0:1]), reads=[smlb[s]], writes=[smlb[s]])
        S.op("act", lambda E, s=s, sm=sm: E.activation(out=hh[s][:], in_=ps_o[s][:, 0:DV], func=AF.Identity, scale=sm[:, 0:1]),
             reads=[ps_ob[s], smlb[s]], writes=[hhb[s]])
        S.op("act", lambda E, s=s: E.activation(out=junk[:], in_=hh[s][:], func=AF.Square), reads=[hhb[s]], writes=[junkb])
        S.op("dve", lambda E, sm=sm: E.reduce_sum(out=sm[:, 1:2], in_=junk[:], axis=AX.X), reads=[junkb], writes=[smlb[s]])
        S.op("act", lambda E, sm=sm: E.activation(out=sm[:, 2:3], in_=sm[:, 1:2], func=AF.Sqrt, scale=1.0 / DV, bias=eps[0:L, 0:1]),
             reads=[smlb[s], epsb], writes=[smlb[s]])
        S.op("dve", lambda E, sm=sm: E.reciprocal(out=sm[:, 3:4], in_=sm[:, 2:3]), reads=[smlb[s]], writes=[smlb[s]])
        S.op("dve", lambda E, s=s, sm=sm: E.scalar_tensor_tensor(out=hh[s][:], in0=hh[s][:], scalar=sm[:, 3:4], in1=gain[:], op0=ALU.mult, op1=ALU.mult),
             reads=[hhb[s], smlb[s], prb], writes=[hhb[s]])
        S.op("dve", lambda E, s=s, c=c: E.tensor_tensor(out=yo[s][:], in0=hh[s][:], in1=ogt[:, c, :], op=ALU.mult),
             reads=[hhb[s], ogb], writes=[yob[s]])
        if fm_out:
            for hf in range(2):
                S.op("pe", lambda E, s=s, hf=hf: E.transpose(ps_t[:, hf, :], yo[s][:, hf * 128:(hf + 1) * 128], idbf[0:L, 0:L]),
                     reads=[yob[s], idbfb], writes=[ps_tb], pe_chain=(hf > 0))
            S.op("act", lambda E, s=s: E.copy(out=yt2[s][:], in_=ps_t[:]), reads=[ps_tb], writes=[yt2b[s]])
            S.dma("sp", y[ycol0:ycol0 + DV, t0 + c * L:t0 + (c + 1) * L].rearrange("(hf p) t -> p hf t", p=128), yt2[s][:],
                  reads=[yt2b[s]], writes=[yb], owner=yt2b[s])
        else:
            S.dma("sp", y[t0 + c * L:t0 + (c + 1) * L, ycol0:ycol0 + DV], yo[s][:], reads=[yob[s]], writes=[yb], owner=yob[s], is_output=True)


def rwkv_stage(C, pT, pTb, r_r, r_k, r_v, r_wl, r_al, r_gl, prm, mul, lnp, w_up, a_up, g_up, yT, yTb, T, uid="rw", final_out=True):
    S = C.S
    NP = 3
    TBA = min(512, T)
    if not hasattr(C, "_rw_scratch"):
        C._rw_scratch = ([C.dram("rws_%s" % n, [NP, 128, T], F32) for n in ("Rfm", "KKfm", "Wfm", "BONfm", "Gfm")]
                         + [C.dram("rws_%s" % n, [T, 384], F32) for n in ("NBtm", "KMtm", "Vtm")]
                         + [C.dram("rws_Yh", [6, 64, T], F32)], Buf("rws_A"), Buf("rws_Yh"))
    (Rfm, KKfm, Wfm, BONfm, Gfm, NBtm, KMtm, Vtm, Yh), scrb, yhb = C._rw_scratch
    DEC = -math.exp(-0.5)
    with C.stage():
        ident, idb = make_ident(C, "ra")
        bones = C.sb("ra_bones", [128, 128], F32)
        bob = Buf("ra_bones")
        S.op("pool", lambda E: E.memset(bones[:], 0.0), writes=[bob])
        S.op("pool", lambda E: E.memset(bones[0:64, 0:64], 1.0), reads=[bob], writes=[bob])
        S.op("pool", lambda E: E.memset(bones[64:128, 64:128], 1.0), reads=[bob], writes=[bob])
        prmt = C.sb("ra_prm", [128, NP, 11], F32)
        mult = C.sb("ra_mul", [128, 6], F32)
        wup = C.sb("ra_wup", [128, 384], F32)
        aup = C.sb("ra_aup", [128, 384], F32)
        gup = C.sb("ra_gup", [128, 4, 384], F32)
        pb = Buf("ra_params")
        S.dma("sp", prmt[:], prm, writes=[pb])
        S.dma("sp", mult[:], mul, writes=[pb])
        S.dma("sp", wup[:], w_up, writes=[pb])
        S.dma("sp", aup[:], a_up, writes=[pb])
        for ki, (k0, ksz) in enumerate(ktiles(480)):
            S.dma("sp", gup[0:ksz, ki, :], g_up[k0:k0 + ksz, :], writes=[pb])
        tiny = C.sb("ra_tiny", [128, 1], F32)
        tnb = Buf("ra_tiny")
        S.op("pool", lambda E: E.memset(tiny[:], 0.0), writes=[tnb])

        def tl(name, shape=None):
            return C.sb("ra_" + name, shape or [128, TBA], F32), Buf("ra_" + name)

        xp, xpb = tl("xp", [128, TBA + 1])
        dd, ddb = tl("dd")
        twl, twlb = tl("twl")
        als, alsb = tl("als")
        sgl, sglb = tl("sgl", [128, 4, TBA])
        rs, rsb = tl("rs")
        ks, ksb = tl("ks")
        vs, vsb = tl("vs")
        Wt, Wtb = tl("W")
        at, atb = tl("a")
        gt, gtb = tl("g")
        kk, kkb = tl("kk")
        t1, t1b = tl("t1")
        t2, t2b = tl("t2")
        km, kmb = tl("km")
        nbt, nbtb = tl("nb")
        bon, bonb = tl("bon")
        tst = [C.sb("ra_tst%d" % i, [128, 128], F32) for i in range(3)]
        tstb = [Buf("ra_tst%d" % i) for i in range(3)]
        psA = [C.ps("ra_ps%d" % i, [128, 512]) for i in range(6)]
        psAb = [Buf("ra_ps%d" % i) for i in range(6)]
        pc = [0]

        def nps():
            i = pc[0] % 6
            pc[0] += 1
            return psA[i], psAb[i]

        def shifted(row0, nrows, t0, tsz, mu_ap, dst, dstb, dsl=None):
            if t0 == 0:
                S.op("pool", lambda E: E.memset(xp[0:nrows, 0:1], 0.0), writes=[xpb])
                S.dma("sp", xp[0:nrows, 1:tsz + 1], pT[row0:row0 + nrows, 0:tsz], reads=[pTb], writes=[xpb])
            else:
                S.dma("sp", xp[0:nrows, 0:tsz + 1], pT[row0:row0 + nrows, t0 - 1:t0 + tsz], reads=[pTb], writes=[xpb])
            S.op("dve", lambda E: E.tensor_tensor(out=dd[0:nrows, 0:tsz], in0=xp[0:nrows, 0:tsz], in1=xp[0:nrows, 1:tsz + 1], op=ALU.subtract),
                 reads=[xpb], writes=[ddb])
            d_ap = dst[0:nrows, 0:tsz] if dsl is None else dsl
            S.op("dve", lambda E: E.scalar_tensor_tensor(out=d_ap, in0=dd[0:nrows, 0:tsz], scalar=mu_ap, in1=xp[0:nrows, 1:tsz + 1],
                                                         op0=ALU.mult, op1=ALU.add), reads=[ddb, xpb, pb], writes=[dstb])

        tr_i = [0]
        for t0 in range(0, T, TBA):
            tsz = min(TBA, T - t0)
            shifted(r_wl, 128, t0, tsz, mult[:, 0:1], twl, twlb)
            S.op("act", lambda E: E.activation(out=twl[:, 0:tsz], in_=twl[:, 0:tsz], func=AF.Tanh), reads=[twlb], writes=[twlb])
            shifted(r_al, 128, t0, tsz, mult[:, 1:2], als, alsb)
            for ki, (k0, ksz) in enumerate(ktiles(480)):
                shifted(r_gl + k0, ksz, t0, tsz, mult[0:ksz, 2 + ki:3 + ki], sgl, sglb, dsl=sgl[0:ksz, ki, 0:tsz])
                S.op("act", lambda E, ki=ki, ksz=ksz: E.activation(out=sgl[0:ksz, ki, 0:tsz], in_=sgl[0:ksz, ki, 0:tsz], func=AF.Sigmoid),
                     reads=[sglb], writes=[sglb])
            for pr in range(NP):
                P = lambda c: prmt[:, pr, c:c + 1]
                cs = slice(pr * 128, (pr + 1) * 128)
                shifted(r_r + pr * 128, 128, t0, tsz, P(0), rs, rsb)
                shifted(r_k + pr * 128, 128, t0, tsz, P(1), ks, ksb)
                shifted(r_v + pr * 128, 128, t0, tsz, P(2), vs, vsb)
                p1, p1b = nps()
                S.op("pe", lambda E: E.matmul(p1[:, 0:tsz], lhsT=wup[:, cs], rhs=twl[:, 0:tsz], start=True, stop=True), reads=[pb, twlb], writes=[p1b])
                S.op("act", lambda E: E.activation(out=Wt[:, 0:tsz], in_=p1[:, 0:tsz], func=AF.Sigmoid, bias=P(3), scale=1.0), reads=[p1b, pb], writes=[Wtb])
                S.op("act", lambda E: E.activation(out=Wt[:, 0:tsz], in_=Wt[:, 0:tsz], func=AF.Exp, scale=DEC), reads=[Wtb], writes=[Wtb])
                p2, p2b = nps()
                S.op("pe", lambda E: E.matmul(p2[:, 0:tsz], lhsT=aup[:, cs], rhs=als[:, 0:tsz], start=True, stop=True), reads=[pb, alsb], writes=[p2b])
                S.op("act", lambda E: E.activation(out=at[:, 0:tsz], in_=p2[:, 0:tsz], func=AF.Sigmoid, bias=P(4), scale=1.0), reads=[p2b, pb], writes=[atb])
                p3, p3b = nps()
                kts = ktiles(480)
                for ki, (k0, ksz) in enumerate(kts):
                    S.op("pe", lambda E, ki=ki, ksz=ksz: E.matmul(p3[:, 0:tsz], lhsT=gup[0:ksz, ki, cs], rhs=sgl[0:ksz, ki, 0:tsz],
                                                                 start=(ki == 0), stop=(ki == len(kts) - 1)), reads=[pb, sglb], writes=[p3b], pe_chain=(ki > 0))
                S.op("act", lambda E: E.copy(out=gt[:, 0:tsz], in_=p3[:, 0:tsz]), reads=[p3b], writes=[gtb])
                S.op("dve", lambda E: E.tensor_scalar(out=kk[:, 0:tsz], in0=ks[:, 0:tsz], scalar1=P(5), scalar2=None, op0=ALU.mult), reads=[ksb, pb], writes=[kkb])
                S.op("act", lambda E: E.activation(out=t1[:, 0:tsz], in_=kk[:, 0:tsz], func=AF.Square), reads=[kkb], writes=[t1b])
                p4, p4b = nps()
                S.op("pe", lambda E: E.matmul(p4[:, 0:tsz], lhsT=bones[:], rhs=t1[:, 0:tsz], start=True, stop=True), reads=[bob, t1b], writes=[p4b])
                S.op("act", lambda E: E.activation(out=t1[:, 0:tsz], in_=p4[:, 0:tsz], func=AF.Sqrt), reads=[p4b], writes=[t1b])
                S.op("dve", lambda E: E.tensor_scalar(out=t1[:, 0:tsz], in0=t1[:, 0:tsz], scalar1=1e-12, scalar2=None, op0=ALU.max), reads=[t1b], writes=[t1b])
                S.op("dve", lambda E: E.reciprocal(out=t1[:, 0:tsz], in_=t1[:, 0:tsz]), reads=[t1b], writes=[t1b])
                S.op("dve", lambda E: E.tensor_tensor(out=kk[:, 0:tsz], in0=kk[:, 0:tsz], in1=t1[:, 0:tsz], op=ALU.mult), reads=[kkb, t1b], writes=[kkb])
                S.op("dve", lambda E: E.tensor_scalar(out=t2[:, 0:tsz], in0=at[:, 0:tsz], scalar1=-1.0, scalar2=P(6), op0=ALU.add, op1=ALU.mult),
                     reads=[atb, pb], writes=[t2b])
                S.op("dve", lambda E: E.scalar_tensor_tensor(out=km[:, 0:tsz], in0=t2[:, 0:tsz], scalar=1.0, in1=ks[:, 0:tsz], op0=ALU.add, op1=ALU.mult),
                     reads=[t2b, ksb], writes=[kmb])
                S.op("dve", lambda E: E.scalar_tensor_tensor(out=nbt[:, 0:tsz], in0=at[:, 0:tsz], scalar=-1.0, in1=kk[:, 0:tsz], op0=ALU.mult, op1=ALU.mult),
                     reads=[atb, kkb], writes=[nbtb])
                S.op("dve", lambda E: E.scalar_tensor_tensor(out=t2[:, 0:tsz], in0=rs[:, 0:tsz], scalar=P(7), in1=km[:, 0:tsz], op0=ALU.mult, op1=ALU.mult),
                     reads=[rsb, kmb, pb], writes=[t2b])
                p5, p5b = nps()
                S.op("pe", lambda E: E.matmul(p5[:, 0:tsz], lhsT=bones[:], rhs=t2[:, 0:tsz], start=True, stop=True), reads=[bob, t2b], writes=[p5b])
                S.op("dve", lambda E: E.tensor_tensor(out=bon[:, 0:tsz], in0=p5[:, 0:tsz], in1=vs[:, 0:tsz], op=ALU.mult), reads=[p5b, vsb], writes=[bonb])
                for (dr, src, srcb) in ((Rfm, rs, rsb), (KKfm, kk, kkb), (Wfm, Wt, Wtb), (BONfm, bon, bonb), (Gfm, gt, gtb)):
                    S.dma("sp", dr[pr, :, t0:t0 + tsz], src[:, 0:tsz], reads=[srcb], writes=[scrb], owner=srcb)
                for (dr, src, srcb) in ((NBtm, nbt, nbtb), (KMtm, km, kmb), (Vtm, vs, vsb)):
                    for c0 in range(0, tsz, 128):
                        pp, ppb = nps()
                        si = tr_i[0] % 3
                        tr_i[0] += 1
                        S.op("pe", lambda E, c0=c0, src=src, pp=pp: E.transpose(pp[:, 0:128], src[:, c0:c0 + 128], ident[:]), reads=[srcb, idb], writes=[ppb])
                        if si == 0:
                            S.op("act", lambda E, pp=pp, si=si: E.copy(out=tst[si][:], in_=pp[:, 0:128]), reads=[ppb], writes=[tstb[si]])
                        else:
                            S.op("dve", lambda E, pp=pp, si=si: E.tensor_copy(out=tst[si][:], in_=pp[:, 0:128]), reads=[ppb], writes=[tstb[si]])
                        S.dma("sp", dr[t0 + c0:t0 + c0 + 128, pr * 128:(pr + 1) * 128], tst[si][:], reads=[tstb[si]], writes=[scrb], owner=tstb[si])
    with C.stage():
        TB2 = min(256, T)
        TBK = 32
        TS = 64
        hmask = C.sb("rb_hmask", [128, 2], F32)
        hmb = Buf("rb_hmask")
        S.op("pool", lambda E: E.memset(hmask[:], 0.0), writes=[hmb])
        S.op("pool", lambda E: E.memset(hmask[0:64, 0:1], 1.0), reads=[hmb], writes=[hmb])
        S.op("pool", lambda E: E.memset(hmask[64:128, 1:2], 1.0), reads=[hmb], writes=[hmb])
        mask6 = C.sb("rb_mask6", [6, 3, 64], F32)
        m6b = Buf("rb_mask6")
        S.op("pool", lambda E: E.memset(mask6[:], 1.0), writes=[m6b])
        S.op("pool", lambda E: E.affine_select(out=mask6[:], in_=mask6[:], pattern=[[-2, 3], [0, 64]], compare_op=ALU.is_ge, fill=0.0,
                                               base=0, channel_multiplier=1), reads=[m6b], writes=[m6b])
        S.op("pool", lambda E: E.affine_select(out=mask6[:], in_=mask6[:], pattern=[[2, 3], [0, 64]], compare_op=ALU.is_ge, fill=0.0,
                                               base=1, channel_multiplier=-1), reads=[m6b], writes=[m6b])
        fm = [[C.sb("rb_fm%d_%d" % (k, i), [128, NP, TB2], F32) for i in range(2)] for k in range(3)]
        fmb = [[Buf("rb_fm%d_%d" % (k, i)) for i in range(2)] for k in range(3)]
        KKZ = [C.sb("rb_kkz%d" % i, [128, TB2, 6], F32) for i in range(2)]
        RZ = [C.sb("rb_rz%d" % i, [128, TB2, 6], F32) for i in range(2)]
        KKZb = [Buf("rb_kkz%d" % i) for i in range(2)]
        RZb = [Buf("rb_rz%d" % i) for i in range(2)]
        LB = [C.sb("rb_lb%d" % i, [6, TBK, 128], F32) for i in range(2)]
        LK = [C.sb("rb_lk%d" % i, [6, TBK, 128], F32) for i in range(2)]
        VM = [C.sb("rb_vm%d" % i, [6, TBK, 192], F32) for i in range(2)]
        LBb = [Buf("rb_lb%d" % i) for i in range(2)]
        LKb = [Buf("rb_lk%d" % i) for i in range(2)]
        VMb = [Buf("rb_vm%d" % i) for i in range(2)]
        for i in range(2):
            S.op("pool", lambda E, i=i: E.memset(LB[i][:], 0.0), writes=[LBb[i]])
            S.op("pool", lambda E, i=i: E.memset(LK[i][:], 0.0), writes=[LKb[i]])
            S.op("pool", lambda E, i=i: E.memset(VM[i][:], 0.0), writes=[VMb[i]])
        St = C.sb("rb_S", [128, 192], F32)
        Sd = C.sb("rb_Sd", [128, 192], F32)
        Sb = Buf("rb_S")
        Sdb3 = [Buf("rb_Sd%d" % i) for i in range(3)]
        S.op("pool", lambda E: E.memset(St[:], 0.0), writes=[Sb])
        RH = [C.sb("rb_rh%d" % i, [6, 192], F32) for i in range(2)]
        RHb = [Buf("rb_rh%d" % i) for i in range(2)]
        ps_sk = [C.ps("rb_psk%d" % i, [6, 512]) for i in range(2)]
        ps_skb = [Buf("rb_psk%d" % i) for i in range(2)]
        ps_up = [C.ps("rb_pup%d" % i, [128, 512]) for i in range(2)]
        ps_upb = [Buf("rb_pup%d" % i) for i in range(2)]
        ps_y = [C.ps("rb_py%d" % i, [64, 512]) for i in range(2)]
        ps_yb = [Buf("rb_py%d" % i) for i in range(2)]
        Yst = [C.sb("rb_yst%d" % i, [64, 6, TS], F32) for i in range(2)]
        Ystb = [Buf("rb_yst%d" % i) for i in range(2)]
        for t in range(T):
            f2 = (t // TB2) % 2
            tf = t % TB2
            if tf == 0:
                n2 = min(TB2, T - t)
                for k, dr in enumerate((KKfm, Rfm, Wfm)):
                    S.dma("sp", fm[k][f2][:, :, 0:n2], dr[:, :, t:t + n2].rearrange("a p t -> p a t"), reads=[scrb], writes=[fmb[k][f2]])
                for pr in range(NP):
                    for h2 in range(2):
                        S.op("pool", lambda E, pr=pr, h2=h2: E.tensor_scalar(out=KKZ[f2][:, 0:n2, 2 * pr + h2], in0=fm[0][f2][:, pr, 0:n2],
                                                                           scalar1=hmask[:, h2:h2 + 1], scalar2=None, op0=ALU.mult),
                             reads=[fmb[0][f2], hmb], writes=[KKZb[f2]])
                        S.op("pool", lambda E, pr=pr, h2=h2: E.tensor_scalar(out=RZ[f2][:, 0:n2, 2 * pr + h2], in0=fm[1][f2][:, pr, 0:n2],
                                                                           scalar1=hmask[:, h2:h2 + 1], scalar2=None, op0=ALU.mult),
                             reads=[fmb[1][f2], hmb], writes=[RZb[f2]])
            fk = (t // TBK) % 2
            tk = t % TBK
            if tk == 0:
                nk = min(TBK, T - t)
                for h2 in range(2):
                    S.dma("sp", LB[fk][h2:6:2, 0:nk, h2 * 64:(h2 + 1) * 64],
                          NBtm[t:t + nk, :].rearrange("t (pr h j) -> pr h t j", pr=3, h=2)[:, h2], reads=[scrb], writes=[LBb[fk]])
                    S.dma("sp", LK[fk][h2:6:2, 0:nk, h2 * 64:(h2 + 1) * 64],
                          KMtm[t:t + nk, :].rearrange("t (pr h j) -> pr h t j", pr=3, h=2)[:, h2], reads=[scrb], writes=[LKb[fk]])
                for pr in range(NP):
                    S.dma("sp", VM[fk][2 * pr:2 * pr + 2, 0:nk, pr * 64:(pr + 1) * 64],
                          Vtm[t:t + nk, pr * 128:(pr + 1) * 128].rearrange("t (h j) -> h t j", h=2), reads=[scrb], writes=[VMb[fk]])
            s = t % 2
            ys = (t // TS) % 2
            ty = t % TS
            S.op("pe", lambda E, s=s, fk=fk, tk=tk: E.matmul(ps_up[s][:, 0:192], lhsT=LK[fk][:, tk, :], rhs=VM[fk][:, tk, :], start=True, stop=False),
                 reads=[LKb[fk], VMb[fk]], writes=[ps_upb[s]])
            S.op("pe", lambda E, s=s, f2=f2, tf=tf: E.matmul(ps_sk[s][:, 0:192], lhsT=KKZ[f2][:, tf, :], rhs=St[:], start=True, stop=True),
                 reads=[KKZb[f2], Sb], writes=[ps_skb[s]])
            S.op("dve", lambda E, s=s: E.tensor_tensor(out=RH[s][:], in0=ps_sk[s][:, 0:192], in1=mask6[:].rearrange("p a i -> p (a i)"), op=ALU.mult),
                 reads=[ps_skb[s], m6b], writes=[RHb[s]])
            S.op("pe", lambda E, s=s, fk=fk, tk=tk: E.matmul(ps_up[s][:, 0:192], lhsT=LB[fk][:, tk, :], rhs=RH[s][:], start=False, stop=True),
                 reads=[LBb[fk], RHb[s]], writes=[ps_upb[s]], pe_chain=True)
            for pr in range(NP):
                eng = ("pool", "act", "pool")[pr]
                if eng == "act":
                    S.op("act", lambda E, pr=pr, f2=f2, tf=tf: E.activation(out=Sd[:, pr * 64:(pr + 1) * 64], in_=St[:, pr * 64:(pr + 1) * 64], func=AF.Identity,
                                                                          scale=fm[2][f2][:, pr, tf:tf + 1]), reads=[Sb, fmb[2][f2]], writes=[Sdb3[pr]])
                else:
                    S.op("pool", lambda E, pr=pr, f2=f2, tf=tf: E.tensor_scalar(out=Sd[:, pr * 64:(pr + 1) * 64], in0=St[:, pr * 64:(pr + 1) * 64],
                                                                              scalar1=fm[2][f2][:, pr, tf:tf + 1], scalar2=None, op0=ALU.mult),
                         reads=[Sb, fmb[2][f2]], writes=[Sdb3[pr]])
            S.op("dve", lambda E, s=s: E.tensor_tensor(out=St[:], in0=Sd[:], in1=ps_up[s][:, 0:192], op=ALU.add), reads=Sdb3 + [ps_upb[s]], writes=[Sb])
            for pr in range(NP):
                S.op("pe", lambda E, pr=pr, ys=ys, ty=ty, f2=f2, tf=tf: E.matmul(ps_y[ys][:, ty * 6 + 2 * pr:ty * 6 + 2 * pr + 2], lhsT=St[:, pr * 64:(pr + 1) * 64],
                                                                              rhs=RZ[f2][:, tf, 2 * pr:2 * pr + 2], start=True, stop=True),
                     reads=[Sb, RZb[f2]], writes=[ps_yb[ys]], pe_chain=(not (ty == 0 and pr == 0)))
            if ty == TS - 1 or t == T - 1:
                n = ty + 1
                tb0 = t - ty
                S.op("act", lambda E, ys=ys, n=n: E.copy(out=Yst[ys][:, :, 0:n], in_=ps_y[ys][:, 0:n * 6].rearrange("p (t h) -> p h t", h=6)),
                     reads=[ps_yb[ys]], writes=[Ystb[ys]])
                S.dma("sp", Yh[:, :, tb0:tb0 + n].rearrange("h i t -> i h t"), Yst[ys][:, :, 0:n], reads=[Ystb[ys]], writes=[yhb], owner=Ystb[ys])
    with C.stage():
        o64 = C.sb("rc_ones", [64, 64], F32)
        o64b = Buf("rc_ones")
        S.op("pool", lambda E: E.memset(o64[:], 1.0 / 64.0), writes=[o64b])
        lnt = C.sb("rc_ln", [64, 6, 2], F32)
        lnb = Buf("rc_ln")
        S.dma("sp", lnt[:], lnp, writes=[lnb])
        epst = C.sb("rc_eps", [64, 1], F32)
        epstb = Buf("rc_eps")
        S.op("pool", lambda E: E.memset(epst[:], RWKV_LN_EPS), writes=[epstb])
        TC = min(512, T)
        yt = [C.sb("rc_y%d" % i, [64, TC], F32) for i in range(2)]
        bt = [C.sb("rc_b%d" % i, [64, TC], F32) for i in range(2)]
        gg = [C.sb("rc_g%d" % i, [64, TC], F32) for i in range(2)]
        ytb = [Buf("rc_y%d" % i) for i in range(2)]
        btb = [Buf("rc_b%d" % i) for i in range(2)]
        ggb = [Buf("rc_g%d" % i) for i in range(2)]
        yc = C.sb("rc_yc", [64, TC], F32)
        ycb = Buf("rc_yc")
        sq = C.sb("rc_sq", [64, TC], F32)
        sqb = Buf("rc_sq")
        rsd = C.sb("rc_rsd", [64, TC], F32)
        rsdb = Buf("rc_rsd")
        oo = [C.sb("rc_o%d" % i, [64, TC], BF16) for i in range(2)]
        oob = [Buf("rc_o%d" % i) for i in range(2)]
        pm = [C.ps("rc_pm%d" % i, [64, 512]) for i in range(2)]
        pmb = [Buf("rc_pm%d" % i) for i in range(2)]
        pv = [C.ps("rc_pv%d" % i, [64, 512]) for i in range(2)]
        pvb = [Buf("rc_pv%d" % i) for i in range(2)]
        it = 0
        for h in range(6):
            pr, h2 = h // 2, h % 2
            for t0 in range(0, T, TC):
                n = min(TC, T - t0)
                s = it % 2
                it += 1
                S.dma("sp", yt[s][:, 0:n], Yh[h, :, t0:t0 + n], reads=[yhb], writes=[ytb[s]])
                S.dma("sp", bt[s][:, 0:n], BONfm[pr, h2 * 64:(h2 + 1) * 64, t0:t0 + n], reads=[scrb], writes=[btb[s]])
                S.dma("sp", gg[s][:, 0:n], Gfm[pr, h2 * 64:(h2 + 1) * 64, t0:t0 + n], reads=[scrb], writes=[ggb[s]])
                S.op("pe", lambda E, s=s, n=n: E.matmul(pm[s][:, 0:n], lhsT=o64[:], rhs=yt[s][:, 0:n], start=True, stop=True), reads=[o64b, ytb[s]], writes=[pmb[s]])
                S.op("dve", lambda E, s=s, n=n: E.tensor_tensor(out=yc[:, 0:n], in0=yt[s][:, 0:n], in1=pm[s][:, 0:n], op=ALU.subtract), reads=[ytb[s], pmb[s]], writes=[ycb])
                S.op("act", lambda E, n=n: E.activation(out=sq[:, 0:n], in_=yc[:, 0:n], func=AF.Square), reads=[ycb], writes=[sqb])
                S.op("pe", lambda E, s=s, n=n: E.matmul(pv[s][:, 0:n], lhsT=o64[:], rhs=sq[:, 0:n], start=True, stop=True), reads=[o64b, sqb], writes=[pvb[s]])
                S.op("act", lambda E, s=s, n=n: E.activation(out=rsd[:, 0:n], in_=pv[s][:, 0:n], func=AF.Sqrt, bias=epst[:, 0:1], scale=1.0), reads=[pvb[s], epstb], writes=[rsdb])
                S.op("dve", lambda E, n=n: E.reciprocal(out=rsd[:, 0:n], in_=rsd[:, 0:n]), reads=[rsdb], writes=[rsdb])
                S.op("dve", lambda E, n=n: E.tensor_tensor(out=yc[:, 0:n], in0=yc[:, 0:n], in1=rsd[:, 0:n], op=ALU.mult), reads=[ycb, rsdb], writes=[ycb])
                S.op("act", lambda E, n=n, h=h: E.activation(out=yc[:, 0:n], in_=yc[:, 0:n], func=AF.Identity, scale=lnt[:, h, 0:1], bias=lnt[:, h, 1:2]),
                     reads=[ycb, lnb], writes=[ycb])
                S.op("dve", lambda E, s=s, n=n: E.tensor_tensor(out=yc[:, 0:n], in0=yc[:, 0:n], in1=bt[s][:, 0:n], op=ALU.add), reads=[ycb, btb[s]], writes=[ycb])
                S.op("dve", lambda E, s=s, n=n: E.tensor_tensor(out=oo[s][:, 0:n], in0=yc[:, 0:n], in1=gg[s][:, 0:n], op=ALU.mult), reads=[ycb, ggb[s]], writes=[oob[s]])
                S.dma("sp", yT[h * 64:(h + 1) * 64, t0:t0 + n], oo[s][:, 0:n], reads=[oob[s]], writes=[yTb], owner=oob[s], is_output=final_out)


NMIX = 3813
R_FQ, R_FK, R_FV, R_FF = 0, 384, 768, 1152
R_MQ, R_MK, R_MV, R_MI, R_MF, R_MO = 1155, 1283, 1411, 1667, 1668, 1669
R_RR, R_RK, R_RV, R_RWL, R_RAL, R_RGL = 1925, 2309, 2693, 3077, 3205, 3333
FF_J = D_FF // 4


def mix_cols(j):
    fox0, ml0, rw0 = 0, 4620, 7700
    r = np.arange
    idx = [fox0 + j * 384 + r(384), fox0 + 1536 + j * 384 + r(384), fox0 + 3072 + j * 384 + r(384), fox0 + 4608 + j * 3 + r(3),
           ml0 + j * 128 + r(128), ml0 + 512 + j * 128 + r(128), ml0 + 1024 + j * 256 + r(256), ml0 + 2048 + j + r(1), ml0 + 2052 + j + r(1),
           ml0 + 2056 + j * 256 + r(256),
           rw0 + j * 384 + r(384), rw0 + 1536 + j * 384 + r(384), rw0 + 3072 + j * 384 + r(384), rw0 + 4608 + r(128), rw0 + 4736 + r(128), rw0 + 4864 + r(480)]
    idx = np.concatenate(idx)
    assert idx.shape[0] == NMIX
    return idx


def build_mod(D=D_MODEL, NC=3072, L=DEPTH):
    C = Ctx()
    S = C.S
    KT = D // 128
    cT = C.dram("cT", [128, KT, 2], F32, "ExternalInput")
    aw = C.dram("aw", [L, D, NC], F32, "ExternalInput")
    ab = C.dram("ab", [128, L, NC // 128], F32, "ExternalInput")
    mo = C.dram("modT", [128, L, NC // 128, 2], F32, "ExternalOutput")
    awb, mob = Buf("aw"), Buf("mo")
    sc = C.sb("m_sc", [128, KT, 2], F32)
    scb = Buf("m_sc")
    S.dma("sp", sc[:], cT, writes=[scb])
    S.op("act", lambda E: E.activation(out=sc[:], in_=sc[:], func=AF.Silu), reads=[scb], writes=[scb])
    abt = C.sb("m_ab", [128, L, NC // 128], F32)
    abb = Buf("m_ab")
    S.dma("sp", abt[:], ab, writes=[abb])
    ot = C.sb("m_ot", [128, L, NC // 128, 2], F32)
    otb = Buf("m_ot")
    NCH = 512
    wt = [C.sb("m_w%d" % i, [128, KT, NCH], F32) for i in range(2)]
    wtb = [Buf("m_w%d" % i) for i in range(2)]
    ps = [C.ps("m_ps%d" % i, [128, 512]) for i in range(2)]
    psb = [Buf("m_ps%d" % i) for i in range(2)]
    it = 0
    pi = 0
    for l in range(L):
        for n0 in range(0, NC, NCH):
            s = it % 2
            it += 1
            for half in range(2):
                kh = KT // 2
                hb = _half_bufs.setdefault((id(C), s, half), Buf("m_wh%d_%d" % (s, half)))
                S.dma("sp" if half == 0 else "act", wt[s][:, half * kh:(half + 1) * kh, :],
                      aw[l, half * kh * 128:(half + 1) * kh * 128, n0:n0 + NCH].rearrange("(kt p) n -> p kt n", p=128),
                      reads=[awb], writes=[wtb[s]] if half == 0 else [hb], owner=wtb[s] if half == 0 else hb)
                if half == 1:
                    wtb_extra[(id(C), s)] = hb
            for m in range(NCH // 128):
                p = pi % 2
                pi += 1
                ch = (n0 // 128) + m
                hb = wtb_extra[(id(C), s)]
                for kt in range(KT):
                    S.op("pe", lambda E, kt=kt, m=m, p=p, s=s: E.matmul(ps[p][:, 0:2], lhsT=wt[s][:, kt, m * 128:(m + 1) * 128], rhs=sc[:, kt, :],
                                                                         start=(kt == 0), stop=(kt == KT - 1)),
                         reads=[wtb[s], hb, scb], writes=[psb[p]], pe_chain=(kt > 0))
                S.op("dve", lambda E, p=p, l=l, ch=ch: E.tensor_scalar(out=ot[:, l, ch, :], in0=ps[p][:, 0:2], scalar1=abt[:, l, ch:ch + 1], scalar2=None, op0=ALU.add),
                     reads=[psb[p], abb], writes=[otb])
    S.dma("sp", mo, ot[:], reads=[otb], writes=[mob], owner=otb, is_output=True)
    C.close()
    return C


_half_bufs = {}
wtb_extra = {}


def build_mix(T=SEQ, D=D_MODEL):
    C = Ctx()
    S = C.S
    KT = D // 128
    xT = C.dram("xT", [D, T], F32, "ExternalInput")
    ng = C.dram("ng", [128, KT], F32, "ExternalInput")
    sc = C.dram("sc", [128, KT], F32, "ExternalInput")
    sh = C.dram("sh", [128, KT], F32, "ExternalInput")
    w = C.dram("w", [D, NMIX], F32, "ExternalInput")
    fbias = C.dram("fbias", [1, 3], F32, "ExternalInput")
    fgain = C.dram("fgain", [128, 384], F32, "ExternalInput")
    cw = C.dram("cw", [128, 2, 4], F32, "ExternalInput")
    cb = C.dram("cb", [128, 2], F32, "ExternalInput")
    gb = C.dram("gb", [1, 2], F32, "ExternalInput")
    mgain = C.dram("mgain", [64, 256], F32, "ExternalInput")
    prm = C.dram("prm", [128, 3, 11], F32, "ExternalInput")
    mul = C.dram("mul", [128, 6], F32, "ExternalInput")
    lnp = C.dram("lnp", [64, 6, 2], F32, "ExternalInput")
    w_up = C.dram("w_up", [128, 384], F32, "ExternalInput")
    a_up = C.dram("a_up", [128, 384], F32, "ExternalInput")
    g_up = C.dram("g_up", [480, 384], F32, "ExternalInput")
    y_tm = C.dram("y_tm", [T, 640], BF16, "ExternalOutput")
    yT_rw = C.dram("yT_rw", [384, T], BF16, "ExternalOutput")
    hT = C.dram("hT_s", [D, T], BF16)
    pT = C.dram("pT_s", [NMIX, T], F32)
    xTb, wb, hTb, pTb, ytb, yrb = Buf("xT"), Buf("w"), Buf("hT"), Buf("pT"), Buf("y_tm"), Buf("yT_rw")
    eps_tile(C)
    with C.stage():
        norm_stage(C, xT, xTb, ng, sc, sh, hT, hTb, D, T, BF16, False)
    with C.stage():
        stg = [C.sb("mx_stg%d" % i, [128, 512], F32) for i in range(3)]
        stgb = [Buf("mx_stg%d" % i) for i in range(3)]
        cnt = [0]

        def epi(n0, nsz, t0, tsz, ps, psb):
            s = cnt[0] % 3
            cnt[0] += 1
            if cnt[0] % 2:
                S.op("act", lambda E: E.copy(out=stg[s][0:nsz, 0:tsz], in_=ps[0]), reads=[psb[0]], writes=[stgb[s]])
            else:
                S.op("dve", lambda E: E.tensor_copy(out=stg[s][0:nsz, 0:tsz], in_=ps[0]), reads=[psb[0]], writes=[stgb[s]])
            S.dma("sp", pT[n0:n0 + nsz, t0:t0 + tsz], stg[s][0:nsz, 0:tsz], reads=[stgb[s]], writes=[pTb], owner=stgb[s])
        gemm_fm(C, hT, hTb, [(w, wb)], D, NMIX, T, epi, TB=1024, NCH=512, tag="mx")
    with C.stage():
        fox_stage(C, pT, pTb, R_FQ, R_FK, R_FV, R_FF, 3, fbias, fgain, y_tm, ytb, 0, T)
    with C.stage():
        mlstm_stage(C, pT, pTb, R_MQ, R_MK, R_MV, R_MI, R_MF, R_MO, cw, cb, gb, mgain, y_tm, ytb, 384, T)
    rwkv_stage(C, pT, pTb, R_RR, R_RK, R_RV, R_RWL, R_RAL, R_RGL, prm, mul, lnp, w_up, a_up, g_up, yT_rw, yrb, T)
    C.close()
    return C


def build_resid_gemm(K, N, T, TB, NCH):
    C = Ctx()
    S = C.S
    inT = C.dram("inT", [K, T], BF16, "ExternalInput")
    w = C.dram("w", [K, N], F32, "ExternalInput")
    resT = C.dram("resT", [N, T], F32, "ExternalInput")
    gv = C.dram("gv", [128, N // 128], F32, "ExternalInput")
    outT = C.dram("outT", [N, T], F32, "ExternalOutput")
    inb, wb, rb, ob = Buf("inT"), Buf("w"), Buf("resT"), Buf("outT")
    gt = C.sb("rg_g", [128, N // 128], F32)
    gtb = Buf("rg_g")
    S.dma("sp", gt[:], gv, writes=[gtb])
    rs = [C.sb("rg_r%d" % i, [128, 512], F32) for i in range(3)]
    rsb = [Buf("rg_r%d" % i) for i in range(3)]
    st = [C.sb("rg_s%d" % i, [128, 512], F32) for i in range(3)]
    stb = [Buf("rg_s%d" % i) for i in range(3)]
    cnt = [0]

    def epi(n0, nsz, t0, tsz, ps, psb):
        s = cnt[0] % 3
        cnt[0] += 1
        S.dma("act", rs[s][0:nsz, 0:tsz], resT[n0:n0 + nsz, t0:t0 + tsz], reads=[rb], writes=[rsb[s]])
        ch = n0 // 128
        S.op("dve", lambda E: E.scalar_tensor_tensor(out=st[s][0:nsz, 0:tsz], in0=ps[0], scalar=gt[0:nsz, ch:ch + 1], in1=rs[s][0:nsz, 0:tsz],
                                                     op0=ALU.mult, op1=ALU.add), reads=[psb[0], gtb, rsb[s]], writes=[stb[s]])
        S.dma("sp", outT[n0:n0 + nsz, t0:t0 + tsz], st[s][0:nsz, 0:tsz], reads=[stb[s]], writes=[ob], owner=stb[s], is_output=True)
    gemm_fm(C, inT, inb, [(w, wb)], K, N, T, epi, TB=TB, NCH=NCH, tag="rg")
    C.close()
    return C


def build_ffn_up(T=SEQ, D=D_MODEL, NF=FF_J):
    C = Ctx()
    S = C.S
    KT = D // 128
    xT = C.dram("xT", [D, T], F32, "ExternalInput")
    ng = C.dram("ng", [128, KT], F32, "ExternalInput")
    sc = C.dram("sc", [128, KT], F32, "ExternalInput")
    sh = C.dram("sh", [128, KT], F32, "ExternalInput")
    wg = C.dram("wg", [D, NF], F32, "ExternalInput")
    wu = C.dram("wu", [D, NF], F32, "ExternalInput")
    hid = C.dram("hidT", [NF, T], BF16, "ExternalOutput")
    hT = C.dram("hT_s", [D, T], BF16)
    xTb, wgb, wub, hTb, hidb = Buf("xT"), Buf("wg"), Buf("wu"), Buf("hT"), Buf("hid")
    eps_tile(C)
    with C.stage():
        norm_stage(C, xT, xTb, ng, sc, sh, hT, hTb, D, T, BF16, False)
    with C.stage():
        sg = [C.sb("fu_sg%d" % i, [128, 512], F32) for i in range(2)]
        sgb = [Buf("fu_sg%d" % i) for i in range(2)]
        ho = [C.sb("fu_ho%d" % i, [128, 512], BF16) for i in range(3)]
        hob = [Buf("fu_ho%d" % i) for i in range(3)]
        cnt = [0]

        def epi(n0, nsz, t0, tsz, ps, psb):
            s = cnt[0] % 2
            o = cnt[0] % 3
            cnt[0] += 1
            S.op("act", lambda E: E.activation(out=sg[s][0:nsz, 0:tsz], in_=ps[0], func=AF.Silu), reads=[psb[0]], writes=[sgb[s]])
            S.op("dve", lambda E: E.tensor_tensor(out=ho[o][0:nsz, 0:tsz], in0=sg[s][0:nsz, 0:tsz], in1=ps[1], op=ALU.mult),
                 reads=[sgb[s], psb[1]], writes=[hob[o]])
            S.dma("sp", hid[n0:n0 + nsz, t0:t0 + tsz], ho[o][0:nsz, 0:tsz], reads=[hob[o]], writes=[hidb], owner=hob[o], is_output=True)
        gemm_fm(C, hT, hTb, [(wg, wgb), (wu, wub)], D, NF, T, epi, TB=1024, NCH=256, tag="fu")
    C.close()
    return C


def build_final_norm(T, D=D_MODEL):
    C = Ctx()
    KT = D // 128
    xT = C.dram("xT", [D, T], F32, "ExternalInput")
    ng = C.dram("ng", [128, KT], F32, "ExternalInput")
    oT = C.dram("oT", [D, T], F32, "ExternalOutput")
    eps_tile(C)
    norm_stage(C, xT, Buf("xT"), ng, None, None, oT, Buf("oT"), D, T, F32, True)
    C.close()
    return C


def fmaj(v):
    v = np.asarray(v, np.float32)
    return np.ascontiguousarray(v.reshape(-1, 128).T)


def pack_rwkv_params(mu_r, mu_k, mu_v, mu_wl, mu_al, mu_gl, w0, a0, k_k, k_a, r_k, ln_w, ln_b):
    prm = np.zeros((128, 3, 11), np.float32)
    v = lambda a: np.asarray(a, np.float32).reshape(3, 128).T
    for i, a in enumerate((mu_r, mu_k, mu_v, w0, a0, k_k, k_a, r_k)):
        prm[:, :, i] = v(a)
    mul = np.zeros((128, 6), np.float32)
    mul[:, 0] = mu_wl
    mul[:, 1] = mu_al
    gl = np.zeros(512, np.float32)
    gl[:480] = mu_gl
    mul[:, 2:6] = gl.reshape(4, 128).T
    lnp = np.ascontiguousarray(np.stack([np.asarray(ln_w).reshape(6, 64).T, np.asarray(ln_b).reshape(6, 64).T], axis=-1).astype(np.float32))
    return prm, mul, lnp


_PROGS = {}


def _prog(name, fn):
    if name not in _PROGS:
        _PROGS[name] = fn()
    return _PROGS[name]


def _run(C, in_maps):
    res = run_bass_kernel_spmd(C.nc, in_maps, core_ids=list(range(8)))
    return res.results


def kernel(x, c, ada_w, ada_b, norm1, w_in, fox_f_bias, fox_norm, ml_conv_w, ml_conv_b, ml_i_bias, ml_f_bias, ml_norm,
           rw_mu, rw_w0, rw_w_up, rw_a0, rw_a_up, rw_g_up, rw_k_k, rw_k_a, rw_r_k, rw_ln_w, rw_ln_b, w_out, norm2,
           ffn_gate, ffn_up, ffn_down, final_norm):
    f32 = lambda a: np.asarray(a, np.float32)
    x, c, ada_w, ada_b = f32(x), f32(c), f32(ada_w), f32(ada_b)
    D, T, B, L = D_MODEL, SEQ, BATCH, DEPTH
    KT = D // 128
    Cm = _prog("mod", build_mod)
    cT = np.ascontiguousarray(c.T.reshape(KT, 128, B).transpose(1, 0, 2))
    ims = []
    for core in range(8):
        cols = slice(core * 3072, (core + 1) * 3072)
        ims.append({"cT": cT, "aw": np.ascontiguousarray(ada_w[:, :, cols]),
                    "ab": np.ascontiguousarray(ada_b[:, cols].reshape(L, 24, 128).transpose(2, 0, 1))})
    res = _run(Cm, ims)
    mod = np.zeros((L, B, 6 * D), np.float32)
    for core in range(8):
        m = res[core]["modT"]
        mod[:, :, core * 3072:(core + 1) * 3072] = m.transpose(1, 3, 2, 0).reshape(L, B, 3072)
    xT = [np.ascontiguousarray(x[b].T) for b in range(B)]
    idxs = [mix_cols(j) for j in range(4)]
    for l in range(L):
        sh1, sc1, g1, sh2, sc2, g2 = [mod[l][:, i * D:(i + 1) * D] for i in range(6)]
        Cx = _prog("mix", build_mix)
        ims = []
        for core in range(8):
            b, j = core // 4, core % 4
            mu = f32(rw_mu[l])
            rs = slice(j * 384, (j + 1) * 384)
            prm, mul, lnp = pack_rwkv_params(mu[0:1536][rs], mu[1536:3072][rs], mu[3072:4608][rs], mu[4608:4736], mu[4736:4864], mu[4864:5344],
                                             f32(rw_w0[l])[rs], f32(rw_a0[l])[rs], f32(rw_k_k[l])[rs], f32(rw_k_a[l])[rs],
                                             f32(rw_r_k[l]).reshape(-1)[rs], f32(rw_ln_w[l])[rs], f32(rw_ln_b[l])[rs])
            cwl = f32(ml_conv_w[l])
            cbl = f32(ml_conv_b[l])
            qs, ks_ = slice(j * 128, (j + 1) * 128), slice(512 + j * 128, 512 + (j + 1) * 128)
            ims.append({
                "xT": xT[b], "ng": fmaj(norm1[l]), "sc": fmaj(sc1[b]), "sh": fmaj(sh1[b]),
                "w": np.ascontiguousarray(f32(w_in[l])[:, idxs[j]]),
                "fbias": np.ascontiguousarray(f32(fox_f_bias[l])[None, j * 3:(j + 1) * 3]),
                "fgain": np.ascontiguousarray(np.tile(f32(fox_norm[l])[None, j * 384:(j + 1) * 384], (128, 1))),
                "cw": np.ascontiguousarray(np.stack([cwl[:, qs].T, cwl[:, ks_].T], axis=1)),
                "cb": np.ascontiguousarray(np.stack([cbl[qs], cbl[ks_]], axis=1)),
                "gb": np.array([[f32(ml_i_bias[l])[j], f32(ml_f_bias[l])[j]]], np.float32),
                "mgain": np.ascontiguousarray(np.tile(f32(ml_norm[l])[None, j * 256:(j + 1) * 256], (64, 1))),
                "prm": prm, "mul": mul, "lnp": lnp,
                "w_up": np.ascontiguousarray(f32(rw_w_up[l])[:, rs]), "a_up": np.ascontiguousarray(f32(rw_a_up[l])[:, rs]),
                "g_up": np.ascontiguousarray(f32(rw_g_up[l])[:, rs]),
            })
        res = _run(Cx, ims)
        yT = [np.zeros((D, T), NPBF16) for _ in range(B)]
        for core in range(8):
            b, j = core // 4, core % 4
            ytm = res[core]["y_tm"]
            yT[b][j * 384:(j + 1) * 384] = ytm[:, 0:384].T
            yT[b][1536 + j * 256:1536 + (j + 1) * 256] = ytm[:, 384:640].T
            yT[b][2560 + j * 384:2560 + (j + 1) * 384] = res[core]["yT_rw"]
        del res
        Co = _prog("oproj", lambda: build_resid_gemm(D, 1024, T, 1024, 512))
        ims = []
        for core in range(8):
            b, j = core // 4, core % 4
            cs = slice(j * 1024, (j + 1) * 1024)
            ims.append({"inT": yT[b], "w": np.ascontiguousarray(f32(w_out[l])[:, cs]), "resT": np.ascontiguousarray(xT[b][cs]),
                        "gv": fmaj(g1[b][cs])})
        res = _run(Co, ims)
        xT = [np.ascontiguousarray(np.concatenate([res[b * 4 + j]["outT"] for j in range(4)], axis=0)) for b in range(B)]
        del res, yT
        Cu = _prog("ffup", build_ffn_up)
        ims = []
        for core in range(8):
            b, j = core // 4, core % 4
            fs = slice(j * FF_J, (j + 1) * FF_J)
            ims.append({"xT": xT[b], "ng": fmaj(norm2[l]), "sc": fmaj(sc2[b]), "sh": fmaj(sh2[b]),
                        "wg": np.ascontiguousarray(f32(ffn_gate[l])[:, fs]), "wu": np.ascontiguousarray(f32(ffn_up[l])[:, fs])})
        res = _run(Cu, ims)
        hidT = [np.ascontiguousarray(np.concatenate([res[b * 4 + j]["hidT"] for j in range(4)], axis=0)) for b in range(B)]
        del res
        Cd = _prog("ffdown", lambda: build_resid_gemm(D_FF, 1024, T, 512, 128))
        ims = []
        for core in range(8):
            b, j = core // 4, core % 4
            cs = slice(j * 1024, (j + 1) * 1024)
            ims.append({"inT": hidT[b], "w": np.ascontiguousarray(f32(ffn_down[l])[:, cs]), "resT": np.ascontiguousarray(xT[b][cs]),
                        "gv": fmaj(g2[b][cs])})
        res = _run(Cd, ims)
        xT = [np.ascontiguousarray(np.concatenate([res[b * 4 + j]["outT"] for j in range(4)], axis=0)) for b in range(B)]
        del res, hidT
    Cf = _prog("fnorm", lambda: build_final_norm(1024))
    ims = []
    for core in range(8):
        b, q = core // 4, core % 4
        ims.append({"xT": np.ascontiguousarray(xT[b][:, q * 1024:(q + 1) * 1024]), "ng": fmaj(final_norm)})
    res = _run(Cf, ims)
    out = np.zeros((B, T, D), np.float32)
    for core in range(8):
        b, q = core // 4, core % 4
        out[b, q * 1024:(q + 1) * 1024, :] = res[core]["oT"].T
    return out
```

```python
import math
from contextlib import ExitStack
import numpy as np
import ml_dtypes
import concourse.bass as bass
import concourse.mybir as mybir
from concourse.bass_utils import run_bass_kernel_spmd

F32 = mybir.dt.float32
BF16 = mybir.dt.bfloat16
AF = mybir.ActivationFunctionType
ALU = mybir.AluOpType
AX = mybir.AxisListType
NPBF16 = ml_dtypes.bfloat16

D_MODEL = 4096
SEQ = 4096
BATCH = 2
DEPTH = 2
D_FF = 11008
NORM_EPS = 1e-6
RWKV_LN_EPS = 64e-5


class Buf:
    __slots__ = ("name", "writer", "readers", "dsem", "dcount")

    def __init__(self, name):
        self.name = name
        self.writer = None
        self.readers = {}
        self.dsem = None
        self.dcount = 0


class Sched:
    def __init__(self, nc, es):
        self.nc = nc
        self.es = es
        self.engs = {"pe": nc.tensor, "dve": nc.vector, "act": nc.scalar, "pool": nc.gpsimd, "sp": nc.sync}
        self.sem = {}
        self.cnt = {}
        for k in ("pe", "dve", "act", "pool"):
            self.sem[k] = es.enter_context(nc.semaphore("prog_" + k))
            self.cnt[k] = 0
        self.waited = {}
        self.nsem = 4
        self.out_events = []
        self.n_ins = 0
        self.dma_events = {}
        self.sem_pool = []
        self.stage_bufs = [[]]

    def _wait(self, eng, evs, skip_sem=None, defer_last=False):
        need = {}
        for ev in evs:
            if ev is None:
                continue
            sem, val = ev
            if skip_sem is not None and sem is skip_sem:
                continue
            if need.get(sem, (None, 0))[1] < val:
                need[sem] = (sem, val)
        E = self.engs[eng]
        todo = [(sem, val) for sem, val in need.values() if self.waited.get((eng, sem), 0) < val]
        last = None
        if defer_last and todo:
            last = todo.pop()
        for sem, val in todo:
            E.wait_ge(sem, val)
            self.waited[(eng, sem)] = val
            self.n_ins += 1
        if last is not None:
            self.waited[(eng, last[0])] = last[1]
        return last

    def _deps(self, reads, writes):
        evs = []
        for b in reads:
            evs.append(b.writer)
        for b in writes:
            evs.append(b.writer)
            for s, v in b.readers.items():
                evs.append((s, v))
        return evs

    def _record(self, ev, reads, writes):
        for b in writes:
            b.writer = ev
            b.readers = {}
        for b in reads:
            if b.readers.get(ev[0], 0) < ev[1]:
                b.readers[ev[0]] = ev[1]

    def op(self, eng, fn, reads=(), writes=(), pe_chain=False):
        evs = self._deps(reads, writes)
        last = self._wait(eng, evs, skip_sem=self.sem[eng] if eng == "pe" else None, defer_last=True)
        ins = fn(self.engs[eng])
        if last is not None:
            ins._wait_ge(last[0], last[1])
        self.cnt[eng] += 1
        ins.then_inc(self.sem[eng], 1)
        ev = (self.sem[eng], self.cnt[eng])
        self._record(ev, reads, writes)
        self.n_ins += 1
        return ev

    def dma(self, q, out_ap, in_ap, reads=(), writes=(), owner=None, is_output=False, **kw):
        if owner is None:
            owner = writes[0]
        evs = self._deps(reads, writes)
        if owner.dsem is not None and owner.dcount > 0:
            evs.append((owner.dsem, owner.dcount))
        self._wait(q, evs)
        if owner.dsem is None:
            if self.sem_pool:
                owner.dsem, owner.dcount = self.sem_pool.pop()
            else:
                owner.dsem = self.es.enter_context(self.nc.semaphore("d_%s_%d" % (owner.name, self.nsem)))
                self.nsem += 1
            self.stage_bufs[-1].append(owner)
        ins = self.engs[q].dma_start(out=out_ap, in_=in_ap, **kw)
        owner.dcount += 16
        ins.then_inc(owner.dsem, 16)
        ev = (owner.dsem, owner.dcount)
        self.dma_events[owner.dsem] = ev
        self._record(ev, reads, writes)
        if is_output:
            self.out_events.append(ev)
        self.n_ins += 1
        return ev

    def barrier(self, bufs=()):
        evs = [(self.sem[k], self.cnt[k]) for k in self.sem if self.cnt[k] > 0]
        evs += list(self.dma_events.values())
        for e in ("pe", "dve", "act", "pool", "sp"):
            self._wait(e, evs)

    def finish(self):
        evs = list(self.out_events) + [(self.sem[k], self.cnt[k]) for k in self.sem if self.cnt[k] > 0]
        self._wait("sp", evs)


class _Stage:
    def __init__(self, C):
        self.C = C

    def __enter__(self):
        self.prev = self.C.stack
        self.C.stack = ExitStack()
        self.C.S.stage_bufs.append([])
        return self

    def __exit__(self, *a):
        S = self.C.S
        S.barrier()
        for b in S.stage_bufs.pop():
            S.sem_pool.append((b.dsem, b.dcount))
            b.dsem = None
        self.C.stack.close()
        self.C.stack = self.prev
        return False


def ktiles(K):
    return [(k0, min(128, K - k0)) for k0 in range(0, K, 128)]


class Ctx:
    def __init__(self):
        self.nc = bass.Bass("TRN2", target_bir_lowering=False)
        self.es = ExitStack()
        self.S = Sched(self.nc, self.es)
        self.uid = 0
        self.stack = self.es
        eps_tile(self)

    def sb(self, name, shape, dt):
        self.uid += 1
        return self.stack.enter_context(self.nc.sbuf_tensor("%s_%d" % (name, self.uid), list(shape), dt))

    def ps(self, name, shape, dt=F32):
        self.uid += 1
        return self.stack.enter_context(self.nc.psum_tensor("%s_%d" % (name, self.uid), list(shape), dt))

    def stage(self):
        return _Stage(self)

    def dram(self, name, shape, dt, kind="Internal"):
        return self.nc.dram_tensor(name, list(shape), dt, kind=kind).ap()

    def close(self):
        self.S.finish()
        self.es.close()


def gemm_fm(C, xT, xT_buf, w_list, K, N, T, epi, TB=1024, NCH=512, tag="g", w_pre=False):
    S = C.S
    kts = ktiles(K)
    KT = len(kts)
    TB = min(TB, T)
    nW = len(w_list)
    xt = C.sb(tag + "_x", [128, KT, TB], BF16)
    xt_b = Buf(tag + "_x")
    wts = [[C.sb(tag + "_w%d_%d" % (wi, i), [128, KT, NCH], BF16) for i in range(2)] for wi in range(nW)]
    wt_b = [[Buf(tag + "_w%d_%d" % (wi, i)) for i in range(2)] for wi in range(nW)]
    NPS = 2 if nW > 1 else 4
    pss = [[C.ps(tag + "_ps%d_%d" % (wi, i), [128, 512]) for i in range(NPS)] for wi in range(nW)]
    ps_b = [[Buf(tag + "_ps%d_%d" % (wi, i)) for i in range(NPS)] for wi in range(nW)]
    full_k = (K % 128 == 0)
    it = 0
    pi = 0
    for t0 in range(0, T, TB):
        tsz_b = min(TB, T - t0)
        if full_k:
            S.dma("sp", xt[:, :, 0:tsz_b], xT[:, t0:t0 + tsz_b].rearrange("(kt p) t -> p kt t", p=128),
                  reads=[xT_buf], writes=[xt_b])
        else:
            for ki, (k0, ksz) in enumerate(kts):
                S.dma("sp", xt[0:ksz, ki, 0:tsz_b], xT[k0:k0 + ksz, t0:t0 + tsz_b], reads=[xT_buf], writes=[xt_b])
        for n0 in range(0, N, NCH):
            nsz_c = min(NCH, N - n0)
            slot = it % 2
            it += 1
            for wi, (w, w_buf) in enumerate(w_list):
                if w_pre:
                    S.dma("pool", wts[wi][slot][:], w[n0 // NCH], reads=[w_buf], writes=[wt_b[wi][slot]])
                elif full_k:
                    S.dma("pool", wts[wi][slot][:, :, 0:nsz_c],
                          w[:, n0:n0 + nsz_c].rearrange("(kt p) n -> p kt n", p=128),
                          reads=[w_buf], writes=[wt_b[wi][slot]])
                else:
                    for ki, (k0, ksz) in enumerate(kts):
                        S.dma("pool", wts[wi][slot][0:ksz, ki, 0:nsz_c], w[k0:k0 + ksz, n0:n0 + nsz_c],
                              reads=[w_buf], writes=[wt_b[wi][slot]])
            for m0 in range(0, nsz_c, 128):
                msz = min(128, nsz_c - m0)
                for tt in range(0, tsz_b, 512):
                    tsz = min(512, tsz_b - tt)
                    p = pi % NPS
                    pi += 1
                    for wi in range(nW):
                        for ki, (k0, ksz) in enumerate(kts):
                            S.op("pe", lambda E, wi=wi, ki=ki, ksz=ksz: E.matmul(
                                pss[wi][p][0:msz, 0:tsz], lhsT=wts[wi][slot][0:ksz, ki, m0:m0 + msz],
                                rhs=xt[0:ksz, ki, tt:tt + tsz], start=(ki == 0), stop=(ki == KT - 1)),
                                reads=[wt_b[wi][slot], xt_b], writes=[ps_b[wi][p]], pe_chain=(ki > 0))
                    epi(n0 + m0, msz, t0 + tt, tsz, [pss[wi][p][0:msz, 0:tsz] for wi in range(nW)],
                        [ps_b[wi][p] for wi in range(nW)])


def norm_stage(C, xT, xT_buf, g_ap, sc_ap, sh_ap, out_ap, out_buf, D, T, out_dt, is_output, tag="n"):
    S = C.S
    KT = D // 128
    gt = C.sb(tag + "_g", [128, KT], F32)
    A = C.sb(tag + "_A", [128, KT], F32)
    gb = Buf(tag + "_g")
    Ab = Buf(tag + "_A")
    S.dma("sp", gt[:], g_ap, writes=[gb])
    if sc_ap is not None:
        sct = C.sb(tag + "_sc", [128, KT], F32)
        sht = C.sb(tag + "_sh", [128, KT], F32)
        scb = Buf(tag + "_sc")
        shb = Buf(tag + "_sh")
        S.dma("sp", sct[:], sc_ap, writes=[scb])
        S.dma("sp", sht[:], sh_ap, writes=[shb])
        S.op("dve", lambda E: E.scalar_tensor_tensor(out=A[:], in0=sct[:], scalar=1.0, in1=gt[:], op0=ALU.add, op1=ALU.mult),
             reads=[scb, gb], writes=[Ab])
    else:
        S.op("dve", lambda E: E.tensor_copy(out=A[:], in_=gt[:]), reads=[gb], writes=[Ab])
    ones = C.sb(tag + "_ones", [128, 128], F32)
    onb = Buf(tag + "_ones")
    S.op("pool", lambda E: E.memset(ones[:], 1.0), writes=[onb])
    TBN = 256
    xs = [C.sb(tag + "_xs%d" % i, [128, KT, TBN], F32) for i in range(2)]
    xb = [Buf(tag + "_xs%d" % i) for i in range(2)]
    sq = [C.sb(tag + "_sq%d" % i, [128, TBN], F32) for i in range(2)]
    sqb = [Buf(tag + "_sq%d" % i) for i in range(2)]
    ho = [C.sb(tag + "_ho%d" % i, [128, KT, TBN], out_dt) for i in range(2)]
    hob = [Buf(tag + "_ho%d" % i) for i in range(2)]
    pss = C.ps(tag + "_ps", [128, TBN])
    psb = Buf(tag + "_ps")
    rs = C.sb(tag + "_rs", [128, TBN], F32)
    rsb = Buf(tag + "_rs")
    tmp = C.sb(tag + "_tmp", [128, TBN], F32)
    tmpb = Buf(tag + "_tmp")
    for bi, t0 in enumerate(range(0, T, TBN)):
        tsz = min(TBN, T - t0)
        s = bi % 2
        S.dma("sp", xs[s][:, :, 0:tsz], xT[:, t0:t0 + tsz].rearrange("(kt p) t -> p kt t", p=128),
              reads=[xT_buf], writes=[xb[s]])
        for kt in range(KT):
            q = kt % 2
            S.op("act", lambda E, kt=kt, q=q: E.activation(out=sq[q][:, 0:tsz], in_=xs[s][:, kt, 0:tsz], func=AF.Square),
                 reads=[xb[s]], writes=[sqb[q]])
            S.op("pe", lambda E, kt=kt, q=q: E.matmul(pss[:, 0:tsz], lhsT=ones[:], rhs=sq[q][:, 0:tsz],
                                                      start=(kt == 0), stop=(kt == KT - 1)),
                 reads=[onb, sqb[q]], writes=[psb], pe_chain=(kt > 0))
        S.op("act", lambda E: E.activation(out=rs[:, 0:tsz], in_=pss[:, 0:tsz], func=AF.Sqrt, scale=1.0 / D, bias=eps_tile(C)[:, 0:1]),
             reads=[psb, eps_buf(C)], writes=[rsb])
        S.op("dve", lambda E: E.reciprocal(out=rs[:, 0:tsz], in_=rs[:, 0:tsz]), reads=[rsb], writes=[rsb])
        for kt in range(KT):
            S.op("dve", lambda E, kt=kt: E.tensor_tensor(out=tmp[:, 0:tsz], in0=xs[s][:, kt, 0:tsz], in1=rs[:, 0:tsz], op=ALU.mult),
                 reads=[xb[s], rsb], writes=[tmpb])
            if sc_ap is not None:
                S.op("act", lambda E, kt=kt: E.activation(out=ho[s][:, kt, 0:tsz], in_=tmp[:, 0:tsz], func=AF.Identity,
                                                          scale=A[:, kt:kt + 1], bias=sht[:, kt:kt + 1]),
                     reads=[tmpb, Ab, shb], writes=[hob[s]])
            else:
                S.op("act", lambda E, kt=kt: E.activation(out=ho[s][:, kt, 0:tsz], in_=tmp[:, 0:tsz], func=AF.Identity,
                                                          scale=A[:, kt:kt + 1]),
                     reads=[tmpb, Ab], writes=[hob[s]])
        S.dma("sp", out_ap[:, t0:t0 + tsz].rearrange("(kt p) t -> p kt t", p=128), ho[s][:, :, 0:tsz],
              reads=[hob[s]], writes=[out_buf], owner=hob[s], is_output=is_output)


def eps_tile(C):
    if not hasattr(C, "_eps"):
        C._eps = C.es.enter_context(C.nc.sbuf_tensor("eps_const", [128, 1], F32))
        C._epsb = Buf("eps")
        C.S.op("pool", lambda E: E.memset(C._eps[:], NORM_EPS), writes=[C._epsb])
    return C._eps


def eps_buf(C):
    eps_tile(C)
    return C._epsb


def make_ident(C, tag):
    S = C.S
    ident = C.sb(tag + "_id", [128, 128], F32)
    idb = Buf(tag + "_id")
    S.op("pool", lambda E: E.memset(ident[:], 1.0), writes=[idb])
    S.op("pool", lambda E: E.affine_select(out=ident[:], in_=ident[:], pattern=[[1, 128]], compare_op=ALU.is_equal,
                                           fill=0.0, base=0, channel_multiplier=-1), reads=[idb], writes=[idb])
    return ident, idb


def fox_stage(C, pT, pTb, r_q, r_k, r_v, r_f, nh, fbias, gain_rep, y, yb, ycol0, T, fm_out=False):
    S = C.S
    scale = 128 ** -0.5
    NQ = (T + 511) // 512
    NK = T // 128
    ident, idb = make_ident(C, "fx")
    ones_row = C.sb("fx_ones", [1, max(T, 128)], F32)
    onb = Buf("fx_ones")
    S.op("pool", lambda E: E.memset(ones_row[:], 1.0), writes=[onb])
    negone = C.sb("fx_neg1", [1, 1], F32)
    n1b = Buf("fx_neg1")
    S.op("pool", lambda E: E.memset(negone[:], -1.0), writes=[n1b])
    fb = C.sb("fx_fb", [1, nh], F32)
    fbb = Buf("fx_fb")
    S.dma("sp", fb[:], fbias, writes=[fbb])
    S.op("dve", lambda E: E.tensor_scalar(out=fb[:], in0=fb[:], scalar1=-1.0, scalar2=None, op0=ALU.mult), reads=[fbb], writes=[fbb])
    gain = C.sb("fx_gain", [128, nh * 128], F32)
    gnb = Buf("fx_gain")
    S.dma("sp", gain[:], gain_rep, writes=[gnb])
    epsb = eps_buf(C)
    eps = eps_tile(C)
    QT = C.sb("fx_q", [128, T], BF16)
    KTt = C.sb("fx_k", [128, T], BF16)
    VT = C.sb("fx_v", [128, T], F32)
    Vext = C.sb("fx_ve", [128, NK, 130], BF16)
    frow = C.sb("fx_f", [1, T], F32)
    Frow = C.sb("fx_F", [1, T], F32)
    Fbc = C.sb("fx_Fbc", [128, T], F32)
    negF = C.sb("fx_nF", [128, NK], F32)
    QTb, KTb, VTb, Vxb, frb, Frb, Fbb, nFb = [Buf("fx_b%d" % i) for i in range(8)]
    E1 = [C.sb("fx_e%d" % i, [128, 512], F32) for i in range(2)]
    E1b = [Buf("fx_e%d" % i) for i in range(2)]
    PT = [C.sb("fx_p%d" % i, [128, 512], BF16) for i in range(2)]
    PTb = [Buf("fx_p%d" % i) for i in range(2)]
    ps_s = [C.ps("fx_ps%d" % i, [128, 512]) for i in range(2)]
    ps_sb = [Buf("fx_ps%d" % i) for i in range(2)]
    ps_o = [C.ps("fx_po%d" % i, [128, 512]) for i in range(4)]
    ps_ob = [Buf("fx_po%d" % i) for i in range(4)]
    ps_m = C.ps("fx_pm", [128, 512])
    psmb = Buf("fx_pm")
    yst = [C.sb("fx_ys%d" % i, [128, 128], F32) for i in range(2)]
    ystb = [Buf("fx_ys%d" % i) for i in range(2)]
    junk = C.sb("fx_junk", [128, 128], F32)
    junkb = Buf("fx_junk")
    sml = [C.sb("fx_sm%d" % i, [128, 4], F32) for i in range(2)]
    smlb = [Buf("fx_sm%d" % i) for i in range(2)]
    yo = [C.sb("fx_yo%d" % i, [128, 128], BF16) for i in range(2)]
    yob = [Buf("fx_yo%d" % i) for i in range(2)]
    blk = 0
    fin = 0
    neg_fill = C.nc.gpsimd.to_reg(-30000.0)
    if fm_out:
        idbf = C.sb("fx_idbf", [128, 128], BF16)
        idbfb = Buf("fx_idbf")
        S.op("dve", lambda E: E.tensor_copy(out=idbf[:], in_=ident[:]), reads=[idb], writes=[idbfb])
        ps_t = C.ps("fx_pt", [128, 128], BF16)
        ps_tb = Buf("fx_pt")
        yt2 = [C.sb("fx_yt%d" % i, [128, 128], BF16) for i in range(2)]
        yt2b = [Buf("fx_yt%d" % i) for i in range(2)]
    for h in range(nh):
        S.dma("pool", QT[:], pT[r_q + h * 128:r_q + (h + 1) * 128, :], reads=[pTb], writes=[QTb])
        S.dma("pool", KTt[:], pT[r_k + h * 128:r_k + (h + 1) * 128, :], reads=[pTb], writes=[KTb])
        S.dma("sp", VT[:], pT[r_v + h * 128:r_v + (h + 1) * 128, :], reads=[pTb], writes=[VTb])
        S.dma("sp", frow[:], pT[r_f + h:r_f + h + 1, :], reads=[pTb], writes=[frb])
        for kt in range(NK):
            S.op("pe", lambda E, kt=kt: E.transpose(ps_m[:, 0:128], VT[:, kt * 128:(kt + 1) * 128], ident[:]),
                 reads=[VTb, idb], writes=[psmb])
            S.op("act", lambda E, kt=kt: E.copy(out=Vext[:, kt, 0:128], in_=ps_m[:, 0:128]), reads=[psmb], writes=[Vxb])
        S.op("pool", lambda E: E.memset(Vext[:, :, 128:129], 1.0), writes=[Vxb])
        S.op("act", lambda E, h=h: E.activation(out=frow[:], in_=frow[:], func=AF.Exp, scale=-1.0, bias=fb[0:1, h:h + 1]),
             reads=[frb, fbb], writes=[frb])
        S.op("act", lambda E: E.activation(out=frow[:], in_=frow[:], func=AF.Ln, scale=1.0, bias=ones_row[0:1, 0:1]),
             reads=[frb, onb], writes=[frb])
        S.op("dve", lambda E: E.tensor_tensor_scan(out=Frow[:], data0=ones_row[0:1, 0:T], data1=frow[:], initial=0.0,
                                                  op0=ALU.mult, op1=ALU.subtract), reads=[frb, onb], writes=[Frb])
        for c in range(NQ):
            csz = min(512, T - c * 512)
            S.op("pe", lambda E, c=c, csz=csz: E.matmul(ps_m[:, 0:csz], lhsT=ones_row[0:1, 0:128], rhs=Frow[0:1, c * 512:c * 512 + csz],
                                                        start=True, stop=True), reads=[onb, Frb], writes=[psmb])
            S.op("act", lambda E, c=c, csz=csz: E.copy(out=Fbc[:, c * 512:c * 512 + csz], in_=ps_m[:, 0:csz]), reads=[psmb], writes=[Fbb])
        for kt in range(NK):
            S.op("pe", lambda E, kt=kt: E.matmul(ps_m[:, kt:kt + 1], lhsT=Frow[0:1, kt * 128:(kt + 1) * 128], rhs=negone[0:1, 0:1],
                                                 start=True, stop=True), reads=[Frb, n1b], writes=[psmb], pe_chain=(kt > 0))
        S.op("dve", lambda E: E.tensor_copy(out=negF[:, 0:NK], in_=ps_m[:, 0:NK]), reads=[psmb], writes=[nFb])
        for qt in range(NQ):
            q0 = qt * 512
            qsz = min(512, T - q0)
            nsub = qsz // 128
            nkt = (q0 + qsz) // 128
            for kt in range(nkt):
                c = kt - 4 * qt
                qs = 128 * c if c >= 0 else 0
                w = qsz - qs
                s = blk % 2
                blk += 1
                S.op("pe", lambda E, kt=kt, qs=qs, w=w, s=s: E.matmul(ps_s[s][:, 0:w], lhsT=KTt[:, kt * 128:(kt + 1) * 128],
                                                                     rhs=QT[:, q0 + qs:q0 + qs + w], start=True, stop=True),
                     reads=[KTb, QTb], writes=[ps_sb[s]])
                S.op("dve", lambda E, qs=qs, w=w, s=s: E.scalar_tensor_tensor(out=E1[s][:, 0:w], in0=ps_s[s][:, 0:w], scalar=scale,
                                                                            in1=Fbc[:, q0 + qs:q0 + qs + w], op0=ALU.mult, op1=ALU.add),
                     reads=[ps_sb[s], Fbb], writes=[E1b[s]])
                if c >= 0:
                    S.op("pool", lambda E, s=s: E.affine_select(out=E1[s][:, 0:128], in_=E1[s][:, 0:128], pattern=[[1, 128]],
                                                               compare_op=ALU.is_ge, fill=neg_fill, base=0, channel_multiplier=-1),
                         reads=[E1b[s]], writes=[E1b[s]])
                S.op("act", lambda E, kt=kt, w=w, s=s: E.activation(out=PT[s][:, 0:w], in_=E1[s][:, 0:w], func=AF.Exp,
                                                                  bias=negF[:, kt:kt + 1], scale=1.0),
                     reads=[E1b[s], nFb], writes=[PTb[s]])
                for qi in range(qs // 128, nsub):
                    last = (kt == 4 * qt + qi)
                    off = qi * 128 - qs
                    S.op("pe", lambda E, kt=kt, qi=qi, off=off, s=s, last=last: E.matmul(
                        ps_o[qi][:, 0:129], lhsT=PT[s][:, off:off + 128], rhs=Vext[:, kt, 0:129], start=(kt == 0), stop=last),
                        reads=[PTb[s], Vxb], writes=[ps_ob[qi]], pe_chain=(kt > 0))
                    if last:
                        f = fin % 2
                        fin += 1
                        sm = sml[f]
                        S.op("dve", lambda E, qi=qi, sm=sm: E.reciprocal(out=sm[:, 0:1], in_=ps_o[qi][:, 128:129]),
                             reads=[ps_ob[qi]], writes=[smlb[f]])
                        S.op("dve", lambda E, qi=qi, sm=sm, f=f: E.tensor_scalar(out=yst[f][:], in0=ps_o[qi][:, 0:128], scalar1=sm[:, 0:1],
                                                                                 scalar2=None, op0=ALU.mult),
                             reads=[ps_ob[qi], smlb[f]], writes=[ystb[f]])
                        S.op("act", lambda E, f=f: E.activation(out=junk[:], in_=yst[f][:], func=AF.Square), reads=[ystb[f]], writes=[junkb])
                        S.op("dve", lambda E, sm=sm: E.reduce_sum(out=sm[:, 1:2], in_=junk[:], axis=AX.X), reads=[junkb], writes=[smlb[f]])
                        S.op("act", lambda E, sm=sm: E.activation(out=sm[:, 2:3], in_=sm[:, 1:2], func=AF.Sqrt, scale=1.0 / 128, bias=eps[:, 0:1]),
                             reads=[smlb[f], epsb], writes=[smlb[f]])
                        S.op("dve", lambda E, sm=sm: E.reciprocal(out=sm[:, 3:4], in_=sm[:, 2:3]), reads=[smlb[f]], writes=[smlb[f]])
                        S.op("dve", lambda E, sm=sm, f=f, h=h: E.scalar_tensor_tensor(out=yo[f][:], in0=yst[f][:], scalar=sm[:, 3:4],
                                                                                     in1=gain[:, h * 128:(h + 1) * 128], op0=ALU.mult, op1=ALU.mult),
                             reads=[ystb[f], smlb[f], gnb], writes=[yob[f]])
                        tok0 = q0 + qi * 128
                        if fm_out:
                            S.op("pe", lambda E, f=f: E.transpose(ps_t[:], yo[f][:], idbf[:]), reads=[yob[f], idbfb], writes=[ps_tb])
                            S.op("act", lambda E, f=f: E.copy(out=yt2[f][:], in_=ps_t[:]), reads=[ps_tb], writes=[yt2b[f]])
                            S.dma("sp", y[ycol0 + h * 128:ycol0 + (h + 1) * 128, tok0:tok0 + 128], yt2[f][:], reads=[yt2b[f]], writes=[yb],
                                  owner=yt2b[f])
                        else:
                            S.dma("sp", y[tok0:tok0 + 128, ycol0 + h * 128:ycol0 + (h + 1) * 128], yo[f][:], reads=[yob[f]], writes=[yb],
                                  owner=yob[f], is_output=True)


def mlstm_stage(C, pT, pTb, r_q, r_k, r_v, r_i, r_f, r_o, cw, cb, gbias, gain_rep, y, yb, ycol0, T, fm_out=False):
    TSEG = min(1024, T)
    state = C.sb("ml_state", [128, 257], F32)
    stb = Buf("ml_state")
    C.S.op("pool", lambda E: E.memset(state[:], 0.0), writes=[stb])
    for t0 in range(0, T, TSEG):
        with C.stage():
            _mlstm_seg(C, pT, pTb, r_q, r_k, r_v, r_i, r_f, r_o, cw, cb, gbias, gain_rep, y, yb, ycol0, min(TSEG, T - t0), t0, state, stb, fm_out)


def _mlstm_seg(C, pT, pTb, r_q, r_k, r_v, r_i, r_f, r_o, cw, cb, gbias, gain_rep, y, yb, ycol0, T, t0, state, stb, fm_out=False):
    S = C.S
    L = 64
    NC = T // L
    DK, DV = 128, 256
    ident, idb = make_ident(C, "ml")
    ones_row = C.sb("ml_ones", [1, max(T, 128)], F32)
    onb = Buf("ml_ones")
    S.op("pool", lambda E: E.memset(ones_row[:], 1.0), writes=[onb])
    rmask = C.sb("ml_rmask", [1, T], F32)
    rmb = Buf("ml_rmask")
    S.op("pool", lambda E: E.memset(rmask[:], 1.0), writes=[rmb])
    S.op("pool", lambda E: E.memset(rmask[:].rearrange("p (c l) -> p c l", l=L)[:, :, 0:1], 0.0), reads=[rmb], writes=[rmb])
    cmask = C.sb("ml_cmask", [L, L], F32)
    cmb = Buf("ml_cmask")
    S.op("pool", lambda E: E.memset(cmask[:], 1.0), writes=[cmb])
    S.op("pool", lambda E: E.affine_select(out=cmask[:], in_=cmask[:], pattern=[[1, L]], compare_op=ALU.is_ge, fill=0.0,
                                           base=0, channel_multiplier=-1), reads=[cmb], writes=[cmb])
    cwt = C.sb("ml_cw", [128, 2, 4], F32)
    cbt = C.sb("ml_cb", [128, 2], F32)
    gbt = C.sb("ml_gb", [1, 2], F32)
    gain = C.sb("ml_gain", [L, DV], F32)
    prb = Buf("ml_params")
    S.dma("sp", cwt[:], cw, writes=[prb])
    S.dma("sp", cbt[:], cb, writes=[prb])
    S.dma("sp", gbt[:], gbias, writes=[prb])
    S.dma("sp", gain[:], gain_rep, writes=[prb])
    S.op("dve", lambda E: E.tensor_scalar(out=gbt[:], in0=gbt[:], scalar1=1.0 / 15.0, scalar2=None, op0=ALU.mult), reads=[prb], writes=[prb])
    eps = eps_tile(C)
    epsb = eps_buf(C)
    ps_m = [C.ps("ml_pm%d" % i, [128, 512]) for i in range(2)]
    psmb = [Buf("ml_pm%d" % i) for i in range(2)]
    irow = C.sb("ml_i", [1, T], F32)
    frow = C.sb("ml_f", [1, T], F32)
    brow = C.sb("ml_b", [1, T], F32)
    irb, frb, brb = Buf("ml_i"), Buf("ml_f"), Buf("ml_b")
    S.dma("sp", irow[:], pT[r_i:r_i + 1, t0:t0 + T], reads=[pTb], writes=[irb])
    S.dma("sp", frow[:], pT[r_f:r_f + 1, t0:t0 + T], reads=[pTb], writes=[frb])
    S.op("act", lambda E: E.activation(out=irow[:], in_=irow[:], func=AF.Tanh, scale=1.0 / 15.0, bias=gbt[0:1, 0:1]), reads=[irb, prb], writes=[irb])
    S.op("act", lambda E: E.activation(out=frow[:], in_=frow[:], func=AF.Tanh, scale=1.0 / 15.0, bias=gbt[0:1, 1:2]), reads=[frb, prb], writes=[frb])
    S.op("act", lambda E: E.activation(out=frow[:], in_=frow[:], func=AF.Exp, scale=-15.0), reads=[frb], writes=[frb])
    S.op("act", lambda E: E.activation(out=frow[:], in_=frow[:], func=AF.Ln, scale=1.0, bias=ones_row[0:1, 0:1]), reads=[frb, onb], writes=[frb])
    S.op("dve", lambda E: E.tensor_tensor_scan(out=brow[:], data0=rmask[:], data1=frow[:], initial=0.0, op0=ALU.mult, op1=ALU.subtract),
         reads=[rmb, frb], writes=[brb])
    S.op("dve", lambda E: E.scalar_tensor_tensor(out=irow[:], in0=irow[:], scalar=15.0, in1=brow[:], op0=ALU.mult, op1=ALU.subtract),
         reads=[irb, brb], writes=[irb])
    S.op("act", lambda E: E.activation(out=irow[:], in_=irow[:], func=AF.Exp), reads=[irb], writes=[irb])
    S.op("act", lambda E: E.activation(out=frow[:], in_=brow[:], func=AF.Exp), reads=[brb], writes=[frb])
    egb_t = C.sb("ml_eg", [128, NC], F32)
    egb = Buf("ml_eg")
    S.op("pe", lambda E: E.matmul(ps_m[0][:, 0:NC], lhsT=ones_row[0:1, 0:128],
                                  rhs=frow[:].rearrange("p (c l) -> p c l", l=L)[:, :, L - 1], start=True, stop=True),
         reads=[onb, frb], writes=[psmb[0]])
    S.op("dve", lambda E: E.tensor_copy(out=egb_t[:], in_=ps_m[0][:, 0:NC]), reads=[psmb[0]], writes=[egb])
    xp = C.sb("ml_xp", [128, T + 3], F32)
    xpb = Buf("ml_xp")
    acc = C.sb("ml_acc", [128, T], F32)
    accb = Buf("ml_acc")
    qT = C.sb("ml_qT", [128, T], F32)
    kT = C.sb("ml_kT", [128, T], F32)
    qTb, kTb = Buf("ml_qT"), Buf("ml_kT")
    for which, (r0, dst, dstb, srow, srb) in enumerate(((r_q, qT, qTb, frow, frb), (r_k, kT, kTb, irow, irb))):
        if t0 == 0:
            S.op("pool", lambda E: E.memset(xp[:, 0:3], 0.0), reads=[], writes=[xpb])
            S.dma("sp", xp[:, 3:T + 3], pT[r0:r0 + 128, 0:T], reads=[pTb], writes=[xpb])
        else:
            S.dma("sp", xp[:, 0:T + 3], pT[r0:r0 + 128, t0 - 3:t0 + T], reads=[pTb], writes=[xpb])
        S.op("dve", lambda E, which=which: E.tensor_scalar(out=acc[:], in0=xp[:, 3:T + 3], scalar1=cwt[:, which, 3:4], scalar2=cbt[:, which:which + 1],
                                                          op0=ALU.mult, op1=ALU.add), reads=[xpb, prb], writes=[accb])
        for i in range(3):
            S.op("dve", lambda E, which=which, i=i: E.scalar_tensor_tensor(out=acc[:], in0=xp[:, i:T + i], scalar=cwt[:, which, i:i + 1], in1=acc[:],
                                                                          op0=ALU.mult, op1=ALU.add), reads=[xpb, prb, accb], writes=[accb])
        S.op("act", lambda E: E.activation(out=acc[:], in_=acc[:], func=AF.Silu), reads=[accb], writes=[accb])
        sc = (DK ** -0.5) if which == 0 else 1.0
        for c0 in range(0, T, 512):
            csz = min(512, T - c0)
            pm = (c0 // 512) % 2
            S.op("pe", lambda E, c0=c0, csz=csz, pm=pm, srow=srow: E.matmul(ps_m[pm][:, 0:csz], lhsT=ones_row[0:1, 0:128], rhs=srow[0:1, c0:c0 + csz],
                                                                           start=True, stop=True), reads=[onb, srb], writes=[psmb[pm]])
            S.op("dve", lambda E, c0=c0, csz=csz, pm=pm, dst=dst, sc=sc: E.scalar_tensor_tensor(out=dst[:, c0:c0 + csz], in0=acc[:, c0:c0 + csz], scalar=sc,
                                                                                                in1=ps_m[pm][:, 0:csz], op0=ALU.mult, op1=ALU.mult),
                 reads=[accb, psmb[pm]], writes=[dstb])
    ktok = C.sb("ml_ktok", [L, NC, DK], F32)
    vext = C.sb("ml_vext", [L, NC, DV + 1], F32)
    ogt = C.sb("ml_og", [L, NC, DV], F32)
    ktokb, vextb, ogb = Buf("ml_ktok"), Buf("ml_vext"), Buf("ml_og")
    S.op("pool", lambda E: E.memset(vext[:, :, DV:DV + 1], 1.0), writes=[vextb])
    tsrc = C.sb("ml_tsrc", [128, T], F32)
    tsb = Buf("ml_tsrc")
    tcnt = 0
    for (r0, kind) in ((r_v, "v0"), (r_v + 128, "v1"), (r_o, "o0"), (r_o + 128, "o1"), (None, "k")):
        if kind == "k":
            src, srcb = kT, kTb
        else:
            S.dma("sp", tsrc[:], pT[r0:r0 + 128, t0:t0 + T], reads=[pTb], writes=[tsb])
            if kind[0] == "o":
                S.op("act", lambda E: E.activation(out=tsrc[:], in_=tsrc[:], func=AF.Sigmoid), reads=[tsb], writes=[tsb])
            src, srcb = tsrc, tsb
        for c in range(NC):
            pm = tcnt % 2
            tcnt += 1
            S.op("pe", lambda E, c=c, pm=pm, src=src: E.transpose(ps_m[pm][0:L, 0:128], src[:, c * L:(c + 1) * L], ident[:]),
                 reads=[srcb, idb], writes=[psmb[pm]])
            if kind == "k":
                dst, dstb_ = ktok[:, c, :], ktokb
            elif kind[0] == "v":
                off = 128 * int(kind[1])
                dst, dstb_ = vext[:, c, off:off + 128], vextb
            else:
                off = 128 * int(kind[1])
                dst, dstb_ = ogt[:, c, off:off + 128], ogb
            eng = "act" if (tcnt % 2) else "dve"
            if eng == "act":
                S.op("act", lambda E, pm=pm, dst=dst: E.copy(out=dst, in_=ps_m[pm][0:L, 0:128]), reads=[psmb[pm]], writes=[dstb_])
            else:
                S.op("dve", lambda E, pm=pm, dst=dst: E.tensor_copy(out=dst, in_=ps_m[pm][0:L, 0:128]), reads=[psmb[pm]], writes=[dstb_])
    ps_s = [C.ps("ml_pss%d" % i, [L, 512]) for i in range(2)]
    ps_sb = [Buf("ml_pss%d" % i) for i in range(2)]
    ps_o = [C.ps("ml_pso%d" % i, [L, 512]) for i in range(2)]
    ps_ob = [Buf("ml_pso%d" % i) for i in range(2)]
    ps_u = C.ps("ml_psu", [128, 512])
    ps_ub = Buf("ml_psu")
    PTt = [C.sb("ml_PT%d" % i, [L, L], F32) for i in range(2)]
    PTb = [Buf("ml_PT%d" % i) for i in range(2)]
    hh = [C.sb("ml_hh%d" % i, [L, DV], F32) for i in range(2)]
    hhb = [Buf("ml_hh%d" % i) for i in range(2)]
    junk = C.sb("ml_junk", [L, DV], F32)
    junkb = Buf("ml_junk")
    sml = [C.sb("ml_sm%d" % i, [L, 4], F32) for i in range(2)]
    smlb = [Buf("ml_sm%d" % i) for i in range(2)]
    yo = [C.sb("ml_yo%d" % i, [L, DV], BF16) for i in range(2)]
    yob = [Buf("ml_yo%d" % i) for i in range(2)]
    if fm_out:
        idbf = C.sb("ml_idbf", [128, 128], BF16)
        idbfb = Buf("ml_idbf")
        S.op("dve", lambda E: E.tensor_copy(out=idbf[:], in_=ident[:]), reads=[idb], writes=[idbfb])
        ps_t = C.ps("ml_pt", [128, 2, L], BF16)
        ps_tb = Buf("ml_pt")
        yt2 = [C.sb("ml_yt%d" % i, [128, 2, L], BF16) for i in range(2)]
        yt2b = [Buf("ml_yt%d" % i) for i in range(2)]
    for c in range(NC):
        s = c % 2
        cs = slice(c * L, (c + 1) * L)
        S.op("pe", lambda E, cs=cs, s=s: E.matmul(ps_s[s][:, 0:L], lhsT=kT[:, cs], rhs=qT[:, cs], start=True, stop=True),
             reads=[kTb, qTb], writes=[ps_sb[s]])
        S.op("dve", lambda E, s=s: E.tensor_tensor(out=PTt[s][:], in0=ps_s[s][:, 0:L], in1=cmask[:], op=ALU.mult),
             reads=[ps_sb[s], cmb], writes=[PTb[s]])
        S.op("pe", lambda E, c=c, s=s: E.matmul(ps_o[s][:, 0:DV + 1], lhsT=PTt[s][:], rhs=vext[:, c, :], start=True, stop=False),
             reads=[PTb[s], vextb], writes=[ps_ob[s]])
        S.op("pe", lambda E, cs=cs, s=s: E.matmul(ps_o[s][:, 0:DV + 1], lhsT=qT[:, cs], rhs=state[:], start=False, stop=True),
             reads=[qTb, stb], writes=[ps_ob[s]], pe_chain=True)
        S.op("pe", lambda E, c=c: E.matmul(ps_u[:, 0:DV + 1], lhsT=ktok[:, c, :], rhs=vext[:, c, :], start=True, stop=True),
             reads=[ktokb, vextb], writes=[ps_ub])
        S.op("dve", lambda E, c=c: E.tensor_scalar(out=state[:], in0=state[:], scalar1=egb_t[:, c:c + 1], scalar2=None, op0=ALU.mult),
             reads=[stb, egb], writes=[stb])
        S.op("dve", lambda E, c=c: E.scalar_tensor_tensor(out=state[:], in0=ps_u[:, 0:DV + 1], scalar=egb_t[:, c:c + 1], in1=state[:],
                                                          op0=ALU.mult, op1=ALU.add), reads=[ps_ub, egb, stb], writes=[stb])
        sm = sml[s]
        S.op("act", lambda E, s=s, sm=sm: E.activation(out=sm[:, 0:1], in_=ps_o[s][:, DV:DV + 1], func=AF.Abs),
             reads=[ps_ob[s]], writes=[smlb[s]])
        S.op("dve", lambda E, sm=sm: E.tensor_scalar(out=sm[:, 0:1], in0=sm[:, 0:1], scalar1=1.0, scalar2=None, op0=ALU.max),
             reads=[smlb[s]], writes=[smlb[s]])
        S.op("dve", lambda E, sm=sm: E.reciprocal(out=sm[:, 0:1], in_=sm[:, 0:1]), reads=[smlb[s]], writes=[smlb[s]])
        S.op("act", lambda E, s=s, sm=sm: E.activation(out=hh[s][:], in_=ps_o[s][:, 0:DV], func=AF.Identity, scale=sm[:, 0:1]),
             reads=[ps_ob[s], smlb[s]], writes=[hhb[s]])
        S.op("act", lambda E, s=s: E.activation(out=junk[:], in_=hh[s][:], func=AF.Square), reads=[hhb[s]], writes=[junkb])
        S.op("dve", lambda E, sm=sm: E.reduce_sum(out=sm[:, 1:2], in_=junk[:], axis=AX.X), reads=[junkb], writes=[smlb[s]])
        S.op("act", lambda E, sm=sm: E.activation(out=sm[:, 2:3], in_=sm[:, 1:2], func=AF.Sqrt, scale=1.0 / DV, bias=eps[0:L, 0:1]),
             reads=[smlb[s], epsb], writes=[smlb[s]])
        S.op("dve", lambda E, sm=sm: E.reciprocal(out=sm[:, 3:4], in_=sm[:, 2:3]), reads=[smlb[s]], writes=[smlb[s]])
        S.op("dve", lambda E, s=s, sm=sm: E.scalar_tensor_tensor(out=hh[s][:], in0=hh[s][:], scalar=sm[:, 3:4], in1=gain[:], op0=ALU.mult, op1=ALU.mult),
             reads=[hhb[s], smlb[s], prb], writes=[hhb[s]])
        S.op("dve", lambda E, s=s, c=c: E.tensor_tensor(out=yo[s][:], in0=hh[s][:], in1=ogt[:, c, :], op=ALU.mult),
             reads=[hhb[s], ogb], writes=[yob[s]])
        if fm_out:
            for hf in range(2):
                S.op("pe", lambda E, s=s, hf=hf: E.transpose(ps_t[:, hf, :], yo[s][:, hf * 128:(hf + 1) * 128], idbf[0:L, 0:L]),
                     reads=[yob[s], idbfb], writes=[ps_tb], pe_chain=(hf > 0))
            S.op("act", lambda E, s=s: E.copy(out=yt2[s][:], in_=ps_t[:]), reads=[ps_tb], writes=[yt2b[s]])
            S.dma("sp", y[ycol0:ycol0 + DV, t0 + c * L:t0 + (c + 1) * L].rearrange("(hf p) t -> p hf t", p=128), yt2[s][:],
                  reads=[yt2b[s]], writes=[yb], owner=yt2b[s])
        else:
            S.dma("sp", y[t0 + c * L:t0 + (c + 1) * L, ycol0:ycol0 + DV], yo[s][:], reads=[yob[s]], writes=[yb], owner=yob[s], is_output=True)


def rwkv_stage(C, pT, pTb, r_r, r_k, r_v, r_wl, r_al, r_gl, prm, mul, lnp, w_up, a_up, g_up, yT, yTb, T, uid="rw", final_out=True):
    S = C.S
    NP = 3
    TBA = min(512, T)
    if not hasattr(C, "_rw_scratch"):
        C._rw_scratch = ([C.dram("rws_%s" % n, [NP, 128, T], F32) for n in ("Rfm", "KKfm", "Wfm", "BONfm", "Gfm")]
                         + [C.dram("rws_%s" % n, [T, 384], F32) for n in ("NBtm", "KMtm", "Vtm")]
                         + [C.dram("rws_Yh", [6, 64, T], F32)], Buf("rws_A"), Buf("rws_Yh"))
    (Rfm, KKfm, Wfm, BONfm, Gfm, NBtm, KMtm, Vtm, Yh), scrb, yhb = C._rw_scratch
    DEC = -math.exp(-0.5)
    with C.stage():
        ident, idb = make_ident(C, "ra")
        bones = C.sb("ra_bones", [128, 128], F32)
        bob = Buf("ra_bones")
        S.op("pool", lambda E: E.memset(bones[:], 0.0), writes=[bob])
        S.op("pool", lambda E: E.memset(bones[0:64, 0:64], 1.0), reads=[bob], writes=[bob])
        S.op("pool", lambda E: E.memset(bones[64:128, 64:128], 1.0), reads=[bob], writes=[bob])
        prmt = C.sb("ra_prm", [128, NP, 11], F32)
        mult = C.sb("ra_mul", [128, 6], F32)
        wup = C.sb("ra_wup", [128, 384], F32)
        aup = C.sb("ra_aup", [128, 384], F32)
        gup = C.sb("ra_gup", [128, 4, 384], F32)
        pb = Buf("ra_params")
        S.dma("sp", prmt[:], prm, writes=[pb])
        S.dma("sp", mult[:], mul, writes=[pb])
        S.dma("sp", wup[:], w_up, writes=[pb])
        S.dma("sp", aup[:], a_up, writes=[pb])
        for ki, (k0, ksz) in enumerate(ktiles(480)):
            S.dma("sp", gup[0:ksz, ki, :], g_up[k0:k0 + ksz, :], writes=[pb])
        tiny = C.sb("ra_tiny", [128, 1], F32)
        tnb = Buf("ra_tiny")
        S.op("pool", lambda E: E.memset(tiny[:], 0.0), writes=[tnb])

        def tl(name, shape=None):
            return C.sb("ra_" + name, shape or [128, TBA], F32), Buf("ra_" + name)

        xp, xpb = tl("xp", [128, TBA + 1])
        dd, ddb = tl("dd")
        twl, twlb = tl("twl")
        als, alsb = tl("als")
        sgl, sglb = tl("sgl", [128, 4, TBA])
        rs, rsb = tl("rs")
        ks, ksb = tl("ks")
        vs, vsb = tl("vs")
        Wt, Wtb = tl("W")
        at, atb = tl("a")
        gt, gtb = tl("g")
        kk, kkb = tl("kk")
        t1, t1b = tl("t1")
        t2, t2b = tl("t2")
        km, kmb = tl("km")
        nbt, nbtb = tl("nb")
        bon, bonb = tl("bon")
        tst = [C.sb("ra_tst%d" % i, [128, 128], F32) for i in range(3)]
        tstb = [Buf("ra_tst%d" % i) for i in range(3)]
        psA = [C.ps("ra_ps%d" % i, [128, 512]) for i in range(6)]
        psAb = [Buf("ra_ps%d" % i) for i in range(6)]
        pc = [0]

        def nps():
            i = pc[0] % 6
            pc[0] += 1
            return psA[i], psAb[i]

        def shifted(row0, nrows, t0, tsz, mu_ap, dst, dstb, dsl=None):
            if t0 == 0:
                S.op("pool", lambda E: E.memset(xp[0:nrows, 0:1], 0.0), writes=[xpb])
                S.dma("sp", xp[0:nrows, 1:tsz + 1], pT[row0:row0 + nrows, 0:tsz], reads=[pTb], writes=[xpb])
            else:
                S.dma("sp", xp[0:nrows, 0:tsz + 1], pT[row0:row0 + nrows, t0 - 1:t0 + tsz], reads=[pTb], writes=[xpb])
            S.op("dve", lambda E: E.tensor_tensor(out=dd[0:nrows, 0:tsz], in0=xp[0:nrows, 0:tsz], in1=xp[0:nrows, 1:tsz + 1], op=ALU.subtract),
                 reads=[xpb], writes=[ddb])
            d_ap = dst[0:nrows, 0:tsz] if dsl is None else dsl
            S.op("dve", lambda E: E.scalar_tensor_tensor(out=d_ap, in0=dd[0:nrows, 0:tsz], scalar=mu_ap, in1=xp[0:nrows, 1:tsz + 1],
                                                         op0=ALU.mult, op1=ALU.add), reads=[ddb, xpb, pb], writes=[dstb])

        tr_i = [0]
        for t0 in range(0, T, TBA):
            tsz = min(TBA, T - t0)
            shifted(r_wl, 128, t0, tsz, mult[:, 0:1], twl, twlb)
            S.op("act", lambda E: E.activation(out=twl[:, 0:tsz], in_=twl[:, 0:tsz], func=AF.Tanh), reads=[twlb], writes=[twlb])
            shifted(r_al, 128, t0, tsz, mult[:, 1:2], als, alsb)
            for ki, (k0, ksz) in enumerate(ktiles(480)):
                shifted(r_gl + k0, ksz, t0, tsz, mult[0:ksz, 2 + ki:3 + ki], sgl, sglb, dsl=sgl[0:ksz, ki, 0:tsz])
                S.op("act", lambda E, ki=ki, ksz=ksz: E.activation(out=sgl[0:ksz, ki, 0:tsz], in_=sgl[0:ksz, ki, 0:tsz], func=AF.Sigmoid),
                     reads=[sglb], writes=[sglb])
            for pr in range(NP):
                P = lambda c: prmt[:, pr, c:c + 1]
                cs = slice(pr * 128, (pr + 1) * 128)
                shifted(r_r + pr * 128, 128, t0, tsz, P(0), rs, rsb)
                shifted(r_k + pr * 128, 128, t0, tsz, P(1), ks, ksb)
                shifted(r_v + pr * 128, 128, t0, tsz, P(2), vs, vsb)
                p1, p1b = nps()
                S.op("pe", lambda E: E.matmul(p1[:, 0:tsz], lhsT=wup[:, cs], rhs=twl[:, 0:tsz], start=True, stop=True), reads=[pb, twlb], writes=[p1b])
                S.op("act", lambda E: E.activation(out=Wt[:, 0:tsz], in_=p1[:, 0:tsz], func=AF.Sigmoid, bias=P(3), scale=1.0), reads=[p1b, pb], writes=[Wtb])
                S.op("act", lambda E: E.activation(out=Wt[:, 0:tsz], in_=Wt[:, 0:tsz], func=AF.Exp, scale=DEC), reads=[Wtb], writes=[Wtb])
                p2, p2b = nps()
                S.op("pe", lambda E: E.matmul(p2[:, 0:tsz], lhsT=aup[:, cs], rhs=als[:, 0:tsz], start=True, stop=True), reads=[pb, alsb], writes=[p2b])
                S.op("act", lambda E: E.activation(out=at[:, 0:tsz], in_=p2[:, 0:tsz], func=AF.Sigmoid, bias=P(4), scale=1.0), reads=[p2b, pb], writes=[atb])
                p3, p3b = nps()
                kts = ktiles(480)
                for ki, (k0, ksz) in enumerate(kts):
                    S.op("pe", lambda E, ki=ki, ksz=ksz: E.matmul(p3[:, 0:tsz], lhsT=gup[0:ksz, ki, cs], rhs=sgl[0:ksz, ki, 0:tsz],
                                                                 start=(ki == 0), stop=(ki == len(kts) - 1)), reads=[pb, sglb], writes=[p3b], pe_chain=(ki > 0))
                S.op("act", lambda E: E.copy(out=gt[:, 0:tsz], in_=p3[:, 0:tsz]), reads=[p3b], writes=[gtb])
                S.op("dve", lambda E: E.tensor_scalar(out=kk[:, 0:tsz], in0=ks[:, 0:tsz], scalar1=P(5), scalar2=None, op0=ALU.mult), reads=[ksb, pb], writes=[kkb])
                S.op("act", lambda E: E.activation(out=t1[:, 0:tsz], in_=kk[:, 0:tsz], func=AF.Square), reads=[kkb], writes=[t1b])
                p4, p4b = nps()
                S.op("pe", lambda E: E.matmul(p4[:, 0:tsz], lhsT=bones[:], rhs=t1[:, 0:tsz], start=True, stop=True), reads=[bob, t1b], writes=[p4b])
                S.op("act", lambda E: E.activation(out=t1[:, 0:tsz], in_=p4[:, 0:tsz], func=AF.Sqrt), reads=[p4b], writes=[t1b])
                S.op("dve", lambda E: E.tensor_scalar(out=t1[:, 0:tsz], in0=t1[:, 0:tsz], scalar1=1e-12, scalar2=None, op0=ALU.max), reads=[t1b], writes=[t1b])
                S.op("dve", lambda E: E.reciprocal(out=t1[:, 0:tsz], in_=t1[:, 0:tsz]), reads=[t1b], writes=[t1b])
                S.op("dve", lambda E: E.tensor_tensor(out=kk[:, 0:tsz], in0=kk[:, 0:tsz], in1=t1[:, 0:tsz], op=ALU.mult), reads=[kkb, t1b], writes=[kkb])
                S.op("dve", lambda E: E.tensor_scalar(out=t2[:, 0:tsz], in0=at[:, 0:tsz], scalar1=-1.0, scalar2=P(6), op0=ALU.add, op1=ALU.mult),
                     reads=[atb, pb], writes=[t2b])
                S.op("dve", lambda E: E.scalar_tensor_tensor(out=km[:, 0:tsz], in0=t2[:, 0:tsz], scalar=1.0, in1=ks[:, 0:tsz], op0=ALU.add, op1=ALU.mult),
                     reads=[t2b, ksb], writes=[kmb])
                S.op("dve", lambda E: E.scalar_tensor_tensor(out=nbt[:, 0:tsz], in0=at[:, 0:tsz], scalar=-1.0, in1=kk[:, 0:tsz], op0=ALU.mult, op1=ALU.mult),
                     reads=[atb, kkb], writes=[nbtb])
                S.op("dve", lambda E: E.scalar_tensor_tensor(out=t2[:, 0:tsz], in0=rs[:, 0:tsz], scalar=P(7), in1=km[:, 0:tsz], op0=ALU.mult, op1=ALU.mult),
                     reads=[rsb, kmb, pb], writes=[t2b])
                p5, p5b = nps()
                S.op("pe", lambda E: E.matmul(p5[:, 0:tsz], lhsT=bones[:], rhs=t2[:, 0:tsz], start=True, stop=True), reads=[bob, t2b], writes=[p5b])
                S.op("dve", lambda E: E.tensor_tensor(out=bon[:, 0:tsz], in0=p5[:, 0:tsz], in1=vs[:, 0:tsz], op=ALU.mult), reads=[p5b, vsb], writes=[bonb])
                for (dr, src, srcb) in ((Rfm, rs, rsb), (KKfm, kk, kkb), (Wfm, Wt, Wtb), (BONfm, bon, bonb), (Gfm, gt, gtb)):
                    S.dma("sp", dr[pr, :, t0:t0 + tsz], src[:, 0:tsz], reads=[srcb], writes=[scrb], owner=srcb)
                for (dr, src, srcb) in ((NBtm, nbt, nbtb), (KMtm, km, kmb), (Vtm, vs, vsb)):
                    for c0 in range(0, tsz, 128):
                        pp, ppb = nps()
                        si = tr_i[0] % 3
                        tr_i[0] += 1
                        S.op("pe", lambda E, c0=c0, src=src, pp=pp: E.transpose(pp[:, 0:128], src[:, c0:c0 + 128], ident[:]), reads=[srcb, idb], writes=[ppb])
                        if si == 0:
                            S.op("act", lambda E, pp=pp, si=si: E.copy(out=tst[si][:], in_=pp[:, 0:128]), reads=[ppb], writes=[tstb[si]])
                        else:
                            S.op("dve", lambda E, pp=pp, si=si: E.tensor_copy(out=tst[si][:], in_=pp[:, 0:128]), reads=[ppb], writes=[tstb[si]])
                        S.dma("sp", dr[t0 + c0:t0 + c0 + 128, pr * 128:(pr + 1) * 128], tst[si][:], reads=[tstb[si]], writes=[scrb], owner=tstb[si])
    with C.stage():
        TB2 = min(256, T)
        TBK = 32
        TS = 64
        hmask = C.sb("rb_hmask", [128, 2], F32)
        hmb = Buf("rb_hmask")
        S.op("pool", lambda E: E.memset(hmask[:], 0.0), writes=[hmb])
        S.op("pool", lambda E: E.memset(hmask[0:64, 0:1], 1.0), reads=[hmb], writes=[hmb])
        S.op("pool", lambda E: E.memset(hmask[64:128, 1:2], 1.0), reads=[hmb], writes=[hmb])
        mask6 = C.sb("rb_mask6", [6, 3, 64], F32)
        m6b = Buf("rb_mask6")
        S.op("pool", lambda E: E.memset(mask6[:], 1.0), writes=[m6b])
        S.op("pool", lambda E: E.affine_select(out=mask6[:], in_=mask6[:], pattern=[[-2, 3], [0, 64]], compare_op=ALU.is_ge, fill=0.0,
                                               base=0, channel_multiplier=1), reads=[m6b], writes=[m6b])
        S.op("pool", lambda E: E.affine_select(out=mask6[:], in_=mask6[:], pattern=[[2, 3], [0, 64]], compare_op=ALU.is_ge, fill=0.0,
                                               base=1, channel_multiplier=-1), reads=[m6b], writes=[m6b])
        fm = [[C.sb("rb_fm%d_%d" % (k, i), [128, NP, TB2], F32) for i in range(2)] for k in range(3)]
        fmb = [[Buf("rb_fm%d_%d" % (k, i)) for i in range(2)] for k in range(3)]
        KKZ = [C.sb("rb_kkz%d" % i, [128, TB2, 6], F32) for i in range(2)]
        RZ = [C.sb("rb_rz%d" % i, [128, TB2, 6], F32) for i in range(2)]
        KKZb = [Buf("rb_kkz%d" % i) for i in range(2)]
        RZb = [Buf("rb_rz%d" % i) for i in range(2)]
        LB = [C.sb("rb_lb%d" % i, [6, TBK, 128], F32) for i in range(2)]
        LK = [C.sb("rb_lk%d" % i, [6, TBK, 128], F32) for i in range(2)]
        VM = [C.sb("rb_vm%d" % i, [6, TBK, 192], F32) for i in range(2)]
        LBb = [Buf("rb_lb%d" % i) for i in range(2)]
        LKb = [Buf("rb_lk%d" % i) for i in range(2)]
        VMb = [Buf("rb_vm%d" % i) for i in range(2)]
        for i in range(2):
            S.op("pool", lambda E, i=i: E.memset(LB[i][:], 0.0), writes=[LBb[i]])
            S.op("pool", lambda E, i=i: E.memset(LK[i][:], 0.0), writes=[LKb[i]])
            S.op("pool", lambda E, i=i: E.memset(VM[i][:], 0.0), writes=[VMb[i]])
        St = C.sb("rb_S", [128, 192], F32)
        Sd = C.sb("rb_Sd", [128, 192], F32)
        Sb = Buf("rb_S")
        Sdb3 = [Buf("rb_Sd%d" % i) for i in range(3)]
        S.op("pool", lambda E: E.memset(St[:], 0.0), writes=[Sb])
        RH = [C.sb("rb_rh%d" % i, [6, 192], F32) for i in range(2)]
        RHb = [Buf("rb_rh%d" % i) for i in range(2)]
        ps_sk = [C.ps("rb_psk%d" % i, [6, 512]) for i in range(2)]
        ps_skb = [Buf("rb_psk%d" % i) for i in range(2)]
        ps_up = [C.ps("rb_pup%d" % i, [128, 512]) for i in range(2)]
        ps_upb = [Buf("rb_pup%d" % i) for i in range(2)]
        ps_y = [C.ps("rb_py%d" % i, [64, 512]) for i in range(2)]
        ps_yb = [Buf("rb_py%d" % i) for i in range(2)]
        Yst = [C.sb("rb_yst%d" % i, [64, 6, TS], F32) for i in range(2)]
        Ystb = [Buf("rb_yst%d" % i) for i in range(2)]
        for t in range(T):
            f2 = (t // TB2) % 2
            tf = t % TB2
            if tf == 0:
                n2 = min(TB2, T - t)
                for k, dr in enumerate((KKfm, Rfm, Wfm)):
                    S.dma("sp", fm[k][f2][:, :, 0:n2], dr[:, :, t:t + n2].rearrange("a p t -> p a t"), reads=[scrb], writes=[fmb[k][f2]])
                for pr in range(NP):
                    for h2 in range(2):
                        S.op("pool", lambda E, pr=pr, h2=h2: E.tensor_scalar(out=KKZ[f2][:, 0:n2, 2 * pr + h2], in0=fm[0][f2][:, pr, 0:n2],
                                                                           scalar1=hmask[:, h2:h2 + 1], scalar2=None, op0=ALU.mult),
                             reads=[fmb[0][f2], hmb], writes=[KKZb[f2]])
                        S.op("pool", lambda E, pr=pr, h2=h2: E.tensor_scalar(out=RZ[f2][:, 0:n2, 2 * pr + h2], in0=fm[1][f2][:, pr, 0:n2],
                                                                           scalar1=hmask[:, h2:h2 + 1], scalar2=None, op0=ALU.mult),
                             reads=[fmb[1][f2], hmb], writes=[RZb[f2]])
            fk = (t // TBK) % 2
            tk = t % TBK
            if tk == 0:
                nk = min(TBK, T - t)
                for h2 in range(2):
                    S.dma("sp", LB[fk][h2:6:2, 0:nk, h2 * 64:(h2 + 1) * 64],
                          NBtm[t:t + nk, :].rearrange("t (pr h j) -> pr h t j", pr=3, h=2)[:, h2], reads=[scrb], writes=[LBb[fk]])
                    S.dma("sp", LK[fk][h2:6:2, 0:nk, h2 * 64:(h2 + 1) * 64],
                          KMtm[t:t + nk, :].rearrange("t (pr h j) -> pr h t j", pr=3, h=2)[:, h2], reads=[scrb], writes=[LKb[fk]])
                for pr in range(NP):
                    S.dma("sp", VM[fk][2 * pr:2 * pr + 2, 0:nk, pr * 64:(pr + 1) * 64],
                          Vtm[t:t + nk, pr * 128:(pr + 1) * 128].rearrange("t (h j) -> h t j", h=2), reads=[scrb], writes=[VMb[fk]])
            s = t % 2
            ys = (t // TS) % 2
            ty = t % TS
            S.op("pe", lambda E, s=s, fk=fk, tk=tk: E.matmul(ps_up[s][:, 0:192], lhsT=LK[fk][:, tk, :], rhs=VM[fk][:, tk, :], start=True, stop=False),
                 reads=[LKb[fk], VMb[fk]], writes=[ps_upb[s]])
            S.op("pe", lambda E, s=s, f2=f2, tf=tf: E.matmul(ps_sk[s][:, 0:192], lhsT=KKZ[f2][:, tf, :], rhs=St[:], start=True, stop=True),
                 reads=[KKZb[f2], Sb], writes=[ps_skb[s]])
            S.op("dve", lambda E, s=s: E.tensor_tensor(out=RH[s][:], in0=ps_sk[s][:, 0:192], in1=mask6[:].rearrange("p a i -> p (a i)"), op=ALU.mult),
                 reads=[ps_skb[s], m6b], writes=[RHb[s]])
            S.op("pe", lambda E, s=s, fk=fk, tk=tk: E.matmul(ps_up[s][:, 0:192], lhsT=LB[fk][:, tk, :], rhs=RH[s][:], start=False, stop=True),
                 reads=[LBb[fk], RHb[s]], writes=[ps_upb[s]], pe_chain=True)
            for pr in range(NP):
                eng = ("pool", "act", "pool")[pr]
                if eng == "act":
                    S.op("act", lambda E, pr=pr, f2=f2, tf=tf: E.activation(out=Sd[:, pr * 64:(pr + 1) * 64], in_=St[:, pr * 64:(pr + 1) * 64], func=AF.Identity,
                                                                          scale=fm[2][f2][:, pr, tf:tf + 1]), reads=[Sb, fmb[2][f2]], writes=[Sdb3[pr]])
                else:
                    S.op("pool", lambda E, pr=pr, f2=f2, tf=tf: E.tensor_scalar(out=Sd[:, pr * 64:(pr + 1) * 64], in0=St[:, pr * 64:(pr + 1) * 64],
                                                                              scalar1=fm[2][f2][:, pr, tf:tf + 1], scalar2=None, op0=ALU.mult),
                         reads=[Sb, fmb[2][f2]], writes=[Sdb3[pr]])
            S.op("dve", lambda E, s=s: E.tensor_tensor(out=St[:], in0=Sd[:], in1=ps_up[s][:, 0:192], op=ALU.add), reads=Sdb3 + [ps_upb[s]], writes=[Sb])
            for pr in range(NP):
                S.op("pe", lambda E, pr=pr, ys=ys, ty=ty, f2=f2, tf=tf: E.matmul(ps_y[ys][:, ty * 6 + 2 * pr:ty * 6 + 2 * pr + 2], lhsT=St[:, pr * 64:(pr + 1) * 64],
                                                                              rhs=RZ[f2][:, tf, 2 * pr:2 * pr + 2], start=True, stop=True),
                     reads=[Sb, RZb[f2]], writes=[ps_yb[ys]], pe_chain=(not (ty == 0 and pr == 0)))
            if ty == TS - 1 or t == T - 1:
                n = ty + 1
                tb0 = t - ty
                S.op("act", lambda E, ys=ys, n=n: E.copy(out=Yst[ys][:, :, 0:n], in_=ps_y[ys][:, 0:n * 6].rearrange("p (t h) -> p h t", h=6)),
                     reads=[ps_yb[ys]], writes=[Ystb[ys]])
                S.dma("sp", Yh[:, :, tb0:tb0 + n].rearrange("h i t -> i h t"), Yst[ys][:, :, 0:n], reads=[Ystb[ys]], writes=[yhb], owner=Ystb[ys])
    with C.stage():
        o64 = C.sb("rc_ones", [64, 64], F32)
        o64b = Buf("rc_ones")
        S.op("pool", lambda E: E.memset(o64[:], 1.0 / 64.0), writes=[o64b])
        lnt = C.sb("rc_ln", [64, 6, 2], F32)
        lnb = Buf("rc_ln")
        S.dma("sp", lnt[:], lnp, writes=[lnb])
        epst = C.sb("rc_eps", [64, 1], F32)
        epstb = Buf("rc_eps")
        S.op("pool", lambda E: E.memset(epst[:], RWKV_LN_EPS), writes=[epstb])
        TC = min(512, T)
        yt = [C.sb("rc_y%d" % i, [64, TC], F32) for i in range(2)]
        bt = [C.sb("rc_b%d" % i, [64, TC], F32) for i in range(2)]
        gg = [C.sb("rc_g%d" % i, [64, TC], F32) for i in range(2)]
        ytb = [Buf("rc_y%d" % i) for i in range(2)]
        btb = [Buf("rc_b%d" % i) for i in range(2)]
        ggb = [Buf("rc_g%d" % i) for i in range(2)]
        yc = C.sb("rc_yc", [64, TC], F32)
        ycb = Buf("rc_yc")
        sq = C.sb("rc_sq", [64, TC], F32)
        sqb = Buf("rc_sq")
        rsd = C.sb("rc_rsd", [64, TC], F32)
        rsdb = Buf("rc_rsd")
        oo = [C.sb("rc_o%d" % i, [64, TC], BF16) for i in range(2)]
        oob = [Buf("rc_o%d" % i) for i in range(2)]
        pm = [C.ps("rc_pm%d" % i, [64, 512]) for i in range(2)]
        pmb = [Buf("rc_pm%d" % i) for i in range(2)]
        pv = [C.ps("rc_pv%d" % i, [64, 512]) for i in range(2)]
        pvb = [Buf("rc_pv%d" % i) for i in range(2)]
        it = 0
        for h in range(6):
            pr, h2 = h // 2, h % 2
            for t0 in range(0, T, TC):
                n = min(TC, T - t0)
                s = it % 2
                it += 1
                S.dma("sp", yt[s][:, 0:n], Yh[h, :, t0:t0 + n], reads=[yhb], writes=[ytb[s]])
                S.dma("sp", bt[s][:, 0:n], BONfm[pr, h2 * 64:(h2 + 1) * 64, t0:t0 + n], reads=[scrb], writes=[btb[s]])
                S.dma("sp", gg[s][:, 0:n], Gfm[pr, h2 * 64:(h2 + 1) * 64, t0:t0 + n], reads=[scrb], writes=[ggb[s]])
                S.op("pe", lambda E, s=s, n=n: E.matmul(pm[s][:, 0:n], lhsT=o64[:], rhs=yt[s][:, 0:n], start=True, stop=True), reads=[o64b, ytb[s]], writes=[pmb[s]])
                S.op("dve", lambda E, s=s, n=n: E.tensor_tensor(out=yc[:, 0:n], in0=yt[s][:, 0:n], in1=pm[s][:, 0:n], op=ALU.subtract), reads=[ytb[s], pmb[s]], writes=[ycb])
                S.op("act", lambda E, n=n: E.activation(out=sq[:, 0:n], in_=yc[:, 0:n], func=AF.Square), reads=[ycb], writes=[sqb])
                S.op("pe", lambda E, s=s, n=n: E.matmul(pv[s][:, 0:n], lhsT=o64[:], rhs=sq[:, 0:n], start=True, stop=True), reads=[o64b, sqb], writes=[pvb[s]])
                S.op("act", lambda E, s=s, n=n: E.activation(out=rsd[:, 0:n], in_=pv[s][:, 0:n], func=AF.Sqrt, bias=epst[:, 0:1], scale=1.0), reads=[pvb[s], epstb], writes=[rsdb])
                S.op("dve", lambda E, n=n: E.reciprocal(out=rsd[:, 0:n], in_=rsd[:, 0:n]), reads=[rsdb], writes=[rsdb])
                S.op("dve", lambda E, n=n: E.tensor_tensor(out=yc[:, 0:n], in0=yc[:, 0:n], in1=rsd[:, 0:n], op=ALU.mult), reads=[ycb, rsdb], writes=[ycb])
                S.op("act", lambda E, n=n, h=h: E.activation(out=yc[:, 0:n], in_=yc[:, 0:n], func=AF.Identity, scale=lnt[:, h, 0:1], bias=lnt[:, h, 1:2]),
                     reads=[ycb, lnb], writes=[ycb])
                S.op("dve", lambda E, s=s, n=n: E.tensor_tensor(out=yc[:, 0:n], in0=yc[:, 0:n], in1=bt[s][:, 0:n], op=ALU.add), reads=[ycb, btb[s]], writes=[ycb])
                S.op("dve", lambda E, s=s, n=n: E.tensor_tensor(out=oo[s][:, 0:n], in0=yc[:, 0:n], in1=gg[s][:, 0:n], op=ALU.mult), reads=[ycb, ggb[s]], writes=[oob[s]])
                S.dma("sp", yT[h * 64:(h + 1) * 64, t0:t0 + n], oo[s][:, 0:n], reads=[oob[s]], writes=[yTb], owner=oob[s], is_output=final_out)


NMIX = 3813
R_FQ, R_FK, R_FV, R_FF = 0, 384, 768, 1152
R_MQ, R_MK, R_MV, R_MI, R_MF, R_MO = 1155, 1283, 1411, 1667, 1668, 1669
R_RR, R_RK, R_RV, R_RWL, R_RAL, R_RGL = 1925, 2309, 2693, 3077, 3205, 3333
FF_J = D_FF // 4


def mix_cols(j):
    fox0, ml0, rw0 = 0, 4620, 7700
    r = np.arange
    idx = [fox0 + j * 384 + r(384), fox0 + 1536 + j * 384 + r(384), fox0 + 3072 + j * 384 + r(384), fox0 + 4608 + j * 3 + r(3),
           ml0 + j * 128 + r(128), ml0 + 512 + j * 128 + r(128), ml0 + 1024 + j * 256 + r(256), ml0 + 2048 + j + r(1), ml0 + 2052 + j + r(1),
           ml0 + 2056 + j * 256 + r(256),
           rw0 + j * 384 + r(384), rw0 + 1536 + j * 384 + r(384), rw0 + 3072 + j * 384 + r(384), rw0 + 4608 + r(128), rw0 + 4736 + r(128), rw0 + 4864 + r(480)]
    idx = np.concatenate(idx)
    assert idx.shape[0] == NMIX
    return idx


def build_mod(D=D_MODEL, NC=3072, L=DEPTH):
    C = Ctx()
    S = C.S
    KT = D // 128
    cT = C.dram("cT", [128, KT, 2], F32, "ExternalInput")
    aw = C.dram("aw", [L, D, NC], F32, "ExternalInput")
    ab = C.dram("ab", [128, L, NC // 128], F32, "ExternalInput")
    mo = C.dram("modT", [128, L, NC // 128, 2], F32, "ExternalOutput")
    awb, mob = Buf("aw"), Buf("mo")
    sc = C.sb("m_sc", [128, KT, 2], F32)
    scb = Buf("m_sc")
    S.dma("sp", sc[:], cT, writes=[scb])
    S.op("act", lambda E: E.activation(out=sc[:], in_=sc[:], func=AF.Silu), reads=[scb], writes=[scb])
    abt = C.sb("m_ab", [128, L, NC // 128], F32)
    abb = Buf("m_ab")
    S.dma("sp", abt[:], ab, writes=[abb])
    ot = C.sb("m_ot", [128, L, NC // 128, 2], F32)
    otb = Buf("m_ot")
    NCH = 512
    wt = [C.sb("m_w%d" % i, [128, KT, NCH], F32) for i in range(2)]
    wtb = [Buf("m_w%d" % i) for i in range(2)]
    ps = [C.ps("m_ps%d" % i, [128, 512]) for i in range(2)]
    psb = [Buf("m_ps%d" % i) for i in range(2)]
    it = 0
    pi = 0
    for l in range(L):
        for n0 in range(0, NC, NCH):
            s = it % 2
            it += 1
            for half in range(2):
                kh = KT // 2
                hb = _half_bufs.setdefault((id(C), s, half), Buf("m_wh%d_%d" % (s, half)))
                S.dma("sp" if half == 0 else "act", wt[s][:, half * kh:(half + 1) * kh, :],
                      aw[l, half * kh * 128:(half + 1) * kh * 128, n0:n0 + NCH].rearrange("(kt p) n -> p kt n", p=128),
                      reads=[awb], writes=[wtb[s]] if half == 0 else [hb], owner=wtb[s] if half == 0 else hb)
                if half == 1:
                    wtb_extra[(id(C), s)] = hb
            for m in range(NCH // 128):
                p = pi % 2
                pi += 1
                ch = (n0 // 128) + m
                hb = wtb_extra[(id(C), s)]
                for kt in range(KT):
                    S.op("pe", lambda E, kt=kt, m=m, p=p, s=s: E.matmul(ps[p][:, 0:2], lhsT=wt[s][:, kt, m * 128:(m + 1) * 128], rhs=sc[:, kt, :],
                                                                         start=(kt == 0), stop=(kt == KT - 1)),
                         reads=[wtb[s], hb, scb], writes=[psb[p]], pe_chain=(kt > 0))
                S.op("dve", lambda E, p=p, l=l, ch=ch: E.tensor_scalar(out=ot[:, l, ch, :], in0=ps[p][:, 0:2], scalar1=abt[:, l, ch:ch + 1], scalar2=None, op0=ALU.add),
                     reads=[psb[p], abb], writes=[otb])
    S.dma("sp", mo, ot[:], reads=[otb], writes=[mob], owner=otb, is_output=True)
    C.close()
    return C


_half_bufs = {}
wtb_extra = {}


def build_mix(T=SEQ, D=D_MODEL):
    C = Ctx()
    S = C.S
    KT = D // 128
    xT = C.dram("xT", [D, T], F32, "ExternalInput")
    ng = C.dram("ng", [128, KT], F32, "ExternalInput")
    sc = C.dram("sc", [128, KT], F32, "ExternalInput")
    sh = C.dram("sh", [128, KT], F32, "ExternalInput")
    w = C.dram("w", [D, NMIX], F32, "ExternalInput")
    fbias = C.dram("fbias", [1, 3], F32, "ExternalInput")
    fgain = C.dram("fgain", [128, 384], F32, "ExternalInput")
    cw = C.dram("cw", [128, 2, 4], F32, "ExternalInput")
    cb = C.dram("cb", [128, 2], F32, "ExternalInput")
    gb = C.dram("gb", [1, 2], F32, "ExternalInput")
    mgain = C.dram("mgain", [64, 256], F32, "ExternalInput")
    prm = C.dram("prm", [128, 3, 11], F32, "ExternalInput")
    mul = C.dram("mul", [128, 6], F32, "ExternalInput")
    lnp = C.dram("lnp", [64, 6, 2], F32, "ExternalInput")
    w_up = C.dram("w_up", [128, 384], F32, "ExternalInput")
    a_up = C.dram("a_up", [128, 384], F32, "ExternalInput")
    g_up = C.dram("g_up", [480, 384], F32, "ExternalInput")
    y_tm = C.dram("y_tm", [T, 640], BF16, "ExternalOutput")
    yT_rw = C.dram("yT_rw", [384, T], BF16, "ExternalOutput")
    hT = C.dram("hT_s", [D, T], BF16)
    pT = C.dram("pT_s", [NMIX, T], F32)
    xTb, wb, hTb, pTb, ytb, yrb = Buf("xT"), Buf("w"), Buf("hT"), Buf("pT"), Buf("y_tm"), Buf("yT_rw")
    eps_tile(C)
    with C.stage():
        norm_stage(C, xT, xTb, ng, sc, sh, hT, hTb, D, T, BF16, False)
    with C.stage():
        stg = [C.sb("mx_stg%d" % i, [128, 512], F32) for i in range(3)]
        stgb = [Buf("mx_stg%d" % i) for i in range(3)]
        cnt = [0]

        def epi(n0, nsz, t0, tsz, ps, psb):
            s = cnt[0] % 3
            cnt[0] += 1
            if cnt[0] % 2:
                S.op("act", lambda E: E.copy(out=stg[s][0:nsz, 0:tsz], in_=ps[0]), reads=[psb[0]], writes=[stgb[s]])
            else:
                S.op("dve", lambda E: E.tensor_copy(out=stg[s][0:nsz, 0:tsz], in_=ps[0]), reads=[psb[0]], writes=[stgb[s]])
            S.dma("sp", pT[n0:n0 + nsz, t0:t0 + tsz], stg[s][0:nsz, 0:tsz], reads=[stgb[s]], writes=[pTb], owner=stgb[s])
        gemm_fm(C, hT, hTb, [(w, wb)], D, NMIX, T, epi, TB=1024, NCH=512, tag="mx")
    with C.stage():
        fox_stage(C, pT, pTb, R_FQ, R_FK, R_FV, R_FF, 3, fbias, fgain, y_tm, ytb, 0, T)
    with C.stage():
        mlstm_stage(C, pT, pTb, R_MQ, R_MK, R_MV, R_MI, R_MF, R_MO, cw, cb, gb, mgain, y_tm, ytb, 384, T)
    rwkv_stage(C, pT, pTb, R_RR, R_RK, R_RV, R_RWL, R_RAL, R_RGL, prm, mul, lnp, w_up, a_up, g_up, yT_rw, yrb, T)
    C.close()
    return C


def build_resid_gemm(K, N, T, TB, NCH):
    C = Ctx()
    S = C.S
    inT = C.dram("inT", [K, T], BF16, "ExternalInput")
    w = C.dram("w", [N // NCH, 128, K // 128, NCH], F32, "ExternalInput")
    resT = C.dram("resT", [N, T], F32, "ExternalInput")
    gv = C.dram("gv", [128, N // 128], F32, "ExternalInput")
    outT = C.dram("outT", [N, T], F32, "ExternalOutput")
    inb, wb, rb, ob = Buf("inT"), Buf("w"), Buf("resT"), Buf("outT")
    gt = C.sb("rg_g", [128, N // 128], F32)
    gtb = Buf("rg_g")
    S.dma("sp", gt[:], gv, writes=[gtb])
    rs = [C.sb("rg_r%d" % i, [128, 512], F32) for i in range(3)]
    rsb = [Buf("rg_r%d" % i) for i in range(3)]
    st = [C.sb("rg_s%d" % i, [128, 512], F32) for i in range(3)]
    stb = [Buf("rg_s%d" % i) for i in range(3)]
    cnt = [0]

    def epi(n0, nsz, t0, tsz, ps, psb):
        s = cnt[0] % 3
        cnt[0] += 1
        S.dma("act", rs[s][0:nsz, 0:tsz], resT[n0:n0 + nsz, t0:t0 + tsz], reads=[rb], writes=[rsb[s]])
        ch = n0 // 128
        S.op("dve", lambda E: E.scalar_tensor_tensor(out=st[s][0:nsz, 0:tsz], in0=ps[0], scalar=gt[0:nsz, ch:ch + 1], in1=rs[s][0:nsz, 0:tsz],
                                                     op0=ALU.mult, op1=ALU.add), reads=[psb[0], gtb, rsb[s]], writes=[stb[s]])
        S.dma("sp", outT[n0:n0 + nsz, t0:t0 + tsz], st[s][0:nsz, 0:tsz], reads=[stb[s]], writes=[ob], owner=stb[s], is_output=True)
    gemm_fm(C, inT, inb, [(w, wb)], K, N, T, epi, TB=TB, NCH=NCH, tag="rg", w_pre=True)
    C.close()
    return C


def build_ffn_up(T=SEQ, D=D_MODEL, NF=FF_J):
    C = Ctx()
    S = C.S
    KT = D // 128
    xT = C.dram("xT", [D, T], F32, "ExternalInput")
    ng = C.dram("ng", [128, KT], F32, "ExternalInput")
    sc = C.dram("sc", [128, KT], F32, "ExternalInput")
    sh = C.dram("sh", [128, KT], F32, "ExternalInput")
    wg = C.dram("wg", [D, NF], F32, "ExternalInput")
    wu = C.dram("wu", [D, NF], F32, "ExternalInput")
    hid = C.dram("hidT", [NF, T], BF16, "ExternalOutput")
    hT = C.dram("hT_s", [D, T], BF16)
    xTb, wgb, wub, hTb, hidb = Buf("xT"), Buf("wg"), Buf("wu"), Buf("hT"), Buf("hid")
    eps_tile(C)
    with C.stage():
        norm_stage(C, xT, xTb, ng, sc, sh, hT, hTb, D, T, BF16, False)
    with C.stage():
        sg = [C.sb("fu_sg%d" % i, [128, 512], F32) for i in range(2)]
        sgb = [Buf("fu_sg%d" % i) for i in range(2)]
        ho = [C.sb("fu_ho%d" % i, [128, 512], BF16) for i in range(3)]
        hob = [Buf("fu_ho%d" % i) for i in range(3)]
        cnt = [0]

        def epi(n0, nsz, t0, tsz, ps, psb):
            s = cnt[0] % 2
            o = cnt[0] % 3
            cnt[0] += 1
            S.op("act", lambda E: E.activation(out=sg[s][0:nsz, 0:tsz], in_=ps[0], func=AF.Silu), reads=[psb[0]], writes=[sgb[s]])
            S.op("dve", lambda E: E.tensor_tensor(out=ho[o][0:nsz, 0:tsz], in0=sg[s][0:nsz, 0:tsz], in1=ps[1], op=ALU.mult),
                 reads=[sgb[s], psb[1]], writes=[hob[o]])
            S.dma("sp", hid[n0:n0 + nsz, t0:t0 + tsz], ho[o][0:nsz, 0:tsz], reads=[hob[o]], writes=[hidb], owner=hob[o], is_output=True)
        gemm_fm(C, hT, hTb, [(wg, wgb), (wu, wub)], D, NF, T, epi, TB=1024, NCH=256, tag="fu")
    C.close()
    return C


def build_final_norm(T, D=D_MODEL):
    C = Ctx()
    KT = D // 128
    xT = C.dram("xT", [D, T], F32, "ExternalInput")
    ng = C.dram("ng", [128, KT], F32, "ExternalInput")
    oT = C.dram("oT", [D, T], F32, "ExternalOutput")
    eps_tile(C)
    norm_stage(C, xT, Buf("xT"), ng, None, None, oT, Buf("oT"), D, T, F32, True)
    C.close()
    return C


def fmaj(v):
    v = np.asarray(v, np.float32)
    return np.ascontiguousarray(v.reshape(-1, 128).T)


def pack_rwkv_params(mu_r, mu_k, mu_v, mu_wl, mu_al, mu_gl, w0, a0, k_k, k_a, r_k, ln_w, ln_b):
    prm = np.zeros((128, 3, 11), np.float32)
    v = lambda a: np.asarray(a, np.float32).reshape(3, 128).T
    for i, a in enumerate((mu_r, mu_k, mu_v, w0, a0, k_k, k_a, r_k)):
        prm[:, :, i] = v(a)
    mul = np.zeros((128, 6), np.float32)
    mul[:, 0] = mu_wl
    mul[:, 1] = mu_al
    gl = np.zeros(512, np.float32)
    gl[:480] = mu_gl
    mul[:, 2:6] = gl.reshape(4, 128).T
    lnp = np.ascontiguousarray(np.stack([np.asarray(ln_w).reshape(6, 64).T, np.asarray(ln_b).reshape(6, 64).T], axis=-1).astype(np.float32))
    return prm, mul, lnp


def prelayout(w, NCH):
    K, N = w.shape
    return np.ascontiguousarray(w.reshape(K // 128, 128, N // NCH, NCH).transpose(2, 1, 0, 3))


_PROGS = {}


def _prog(name, fn):
    if name not in _PROGS:
        _PROGS[name] = fn()
    return _PROGS[name]


def _run(C, in_maps):
    res = run_bass_kernel_spmd(C.nc, in_maps, core_ids=list(range(8)))
    return res.results


def kernel(x, c, ada_w, ada_b, norm1, w_in, fox_f_bias, fox_norm, ml_conv_w, ml_conv_b, ml_i_bias, ml_f_bias, ml_norm,
           rw_mu, rw_w0, rw_w_up, rw_a0, rw_a_up, rw_g_up, rw_k_k, rw_k_a, rw_r_k, rw_ln_w, rw_ln_b, w_out, norm2,
           ffn_gate, ffn_up, ffn_down, final_norm):
    f32 = lambda a: np.asarray(a, np.float32)
    x, c, ada_w, ada_b = f32(x), f32(c), f32(ada_w), f32(ada_b)
    D, T, B, L = D_MODEL, SEQ, BATCH, DEPTH
    KT = D // 128
    Cm = _prog("mod", build_mod)
    cT = np.ascontiguousarray(c.T.reshape(KT, 128, B).transpose(1, 0, 2))
    ims = []
    for core in range(8):
        cols = slice(core * 3072, (core + 1) * 3072)
        ims.append({"cT": cT, "aw": np.ascontiguousarray(ada_w[:, :, cols]),
                    "ab": np.ascontiguousarray(ada_b[:, cols].reshape(L, 24, 128).transpose(2, 0, 1))})
    res = _run(Cm, ims)
    mod = np.zeros((L, B, 6 * D), np.float32)
    for core in range(8):
        m = res[core]["modT"]
        mod[:, :, core * 3072:(core + 1) * 3072] = m.transpose(1, 3, 2, 0).reshape(L, B, 3072)
    xT = [np.ascontiguousarray(x[b].T) for b in range(B)]
    idxs = [mix_cols(j) for j in range(4)]
    for l in range(L):
        sh1, sc1, g1, sh2, sc2, g2 = [mod[l][:, i * D:(i + 1) * D] for i in range(6)]
        Cx = _prog("mix", build_mix)
        ims = []
        for core in range(8):
            b, j = core // 4, core % 4
            mu = f32(rw_mu[l])
            rs = slice(j * 384, (j + 1) * 384)
            prm, mul, lnp = pack_rwkv_params(mu[0:1536][rs], mu[1536:3072][rs], mu[3072:4608][rs], mu[4608:4736], mu[4736:4864], mu[4864:5344],
                                             f32(rw_w0[l])[rs], f32(rw_a0[l])[rs], f32(rw_k_k[l])[rs], f32(rw_k_a[l])[rs],
                                             f32(rw_r_k[l]).reshape(-1)[rs], f32(rw_ln_w[l])[rs], f32(rw_ln_b[l])[rs])
            cwl = f32(ml_conv_w[l])
            cbl = f32(ml_conv_b[l])
            qs, ks_ = slice(j * 128, (j + 1) * 128), slice(512 + j * 128, 512 + (j + 1) * 128)
            ims.append({
                "xT": xT[b], "ng": fmaj(norm1[l]), "sc": fmaj(sc1[b]), "sh": fmaj(sh1[b]),
                "w": np.ascontiguousarray(f32(w_in[l])[:, idxs[j]]),
                "fbias": np.ascontiguousarray(f32(fox_f_bias[l])[None, j * 3:(j + 1) * 3]),
                "fgain": np.ascontiguousarray(np.tile(f32(fox_norm[l])[None, j * 384:(j + 1) * 384], (128, 1))),
                "cw": np.ascontiguousarray(np.stack([cwl[:, qs].T, cwl[:, ks_].T], axis=1)),
                "cb": np.ascontiguousarray(np.stack([cbl[qs], cbl[ks_]], axis=1)),
                "gb": np.array([[f32(ml_i_bias[l])[j], f32(ml_f_bias[l])[j]]], np.float32),
                "mgain": np.ascontiguousarray(np.tile(f32(ml_norm[l])[None, j * 256:(j + 1) * 256], (64, 1))),
                "prm": prm, "mul": mul, "lnp": lnp,
                "w_up": np.ascontiguousarray(f32(rw_w_up[l])[:, rs]), "a_up": np.ascontiguousarray(f32(rw_a_up[l])[:, rs]),
                "g_up": np.ascontiguousarray(f32(rw_g_up[l])[:, rs]),
            })
        res = _run(Cx, ims)
        yT = [np.zeros((D, T), NPBF16) for _ in range(B)]
        for core in range(8):
            b, j = core // 4, core % 4
            ytm = res[core]["y_tm"]
            yT[b][j * 384:(j + 1) * 384] = ytm[:, 0:384].T
            yT[b][1536 + j * 256:1536 + (j + 1) * 256] = ytm[:, 384:640].T
            yT[b][2560 + j * 384:2560 + (j + 1) * 384] = res[core]["yT_rw"]
        del res
        Co = _prog("oproj", lambda: build_resid_gemm(D, 1024, T, 1024, 512))
        ims = []
        for core in range(8):
            b, j = core // 4, core % 4
            cs = slice(j * 1024, (j + 1) * 1024)
            ims.append({"inT": yT[b], "w": prelayout(f32(w_out[l])[:, cs], 512), "resT": np.ascontiguousarray(xT[b][cs]),
                        "gv": fmaj(g1[b][cs])})
        res = _run(Co, ims)
        xT = [np.ascontiguousarray(np.concatenate([res[b * 4 + j]["outT"] for j in range(4)], axis=0)) for b in range(B)]
        del res, yT
        Cu = _prog("ffup", build_ffn_up)
        ims = []
        for core in range(8):
            b, j = core // 4, core % 4
            fs = slice(j * FF_J, (j + 1) * FF_J)
            ims.append({"xT": xT[b], "ng": fmaj(norm2[l]), "sc": fmaj(sc2[b]), "sh": fmaj(sh2[b]),
                        "wg": np.ascontiguousarray(f32(ffn_gate[l])[:, fs]), "wu": np.ascontiguousarray(f32(ffn_up[l])[:, fs])})
        res = _run(Cu, ims)
        hidT = [np.ascontiguousarray(np.concatenate([res[b * 4 + j]["hidT"] for j in range(4)], axis=0)) for b in range(B)]
        del res
        Cd = _prog("ffdown", lambda: build_resid_gemm(D_FF, 1024, T, 512, 128))
        ims = []
        for core in range(8):
            b, j = core // 4, core % 4
            cs = slice(j * 1024, (j + 1) * 1024)
            ims.append({"inT": hidT[b], "w": prelayout(f32(ffn_down[l])[:, cs], 128), "resT": np.ascontiguousarray(xT[b][cs]),
                        "gv": fmaj(g2[b][cs])})
        res = _run(Cd, ims)
        xT = [np.ascontiguousarray(np.concatenate([res[b * 4 + j]["outT"] for j in range(4)], axis=0)) for b in range(B)]
        del res, hidT
    Cf = _prog("fnorm", lambda: build_final_norm(1024))
    ims = []
    for core in range(8):
        b, q = core // 4, core % 4
        ims.append({"xT": np.ascontiguousarray(xT[b][:, q * 1024:(q + 1) * 1024]), "ng": fmaj(final_norm)})
    res = _run(Cf, ims)
    out = np.zeros((B, T, D), np.float32)
    for core in range(8):
        b, q = core // 4, core % 4
        out[b, q * 1024:(q + 1) * 1024, :] = res[core]["oT"].T
    return out
```
